# Optimizing a Trainium2 kernel written in Bass

```python
import jax, jax.numpy as jnp
from jax import lax
import numpy as np

D_MODEL = 1024
BATCH = 2
SEQ = 8192
DEPTH = 1

ATTN_GROUPS = ((128, 1), (512, 4), (2048, 16))
ATTN_HEADS_PER_GROUP = 8
ATTN_HEAD_DIM = 64
ATTN_BLOCK = 128
N_ATTN_HEADS = len(ATTN_GROUPS) * ATTN_HEADS_PER_GROUP
ATTN_QKV_WIDTH = 3 * N_ATTN_HEADS * ATTN_HEAD_DIM
ATTN_OUT = ATTN_HEADS_PER_GROUP * ATTN_HEAD_DIM

GLA_HEADS = 4
GLA_DK = D_MODEL // 2
GLA_DV = D_MODEL
GLA_HK = GLA_DK // GLA_HEADS
GLA_HV = GLA_DV // GLA_HEADS
GLA_GATE_RANK = 16
GLA_TAU = 16.0
GLA_CHUNK = 64

D_FF = 4 * D_MODEL

N_BRANCHES = 2
EPS = 1e-6

_IN_SIZES = (ATTN_QKV_WIDTH, GLA_DK, GLA_DK, GLA_DV, GLA_DV, GLA_GATE_RANK, N_BRANCHES * D_MODEL)
D_IN = sum(_IN_SIZES)
_IN_OFFSETS = tuple(int(v) for v in np.cumsum(_IN_SIZES)[:-1])

kernel_name = 'hybrid_dilated_attn_gla_gated_block'


def rms_norm(x, g):
    xf = x.astype(jnp.float32)
    y = xf * lax.rsqrt(jnp.mean(jnp.square(xf), axis=-1, keepdims=True) + EPS)
    return (y * g.astype(jnp.float32)).astype(x.dtype)


def _dilated_group(q, k, v, window, dilation):
    b, s, h, e = q.shape
    n_back = window // dilation
    blk = ATTN_BLOCK
    assert n_back <= blk
    seg = dilation * blk
    s_pad = -(-s // seg) * seg
    L = s_pad // dilation
    nb = L // blk

    def to_blocks(t):
        t = jnp.pad(t, ((0, 0), (0, s_pad - s), (0, 0), (0, 0)))
        t = t.reshape(b, L, dilation, h, e).transpose(0, 2, 3, 1, 4)
        return t.reshape(b, dilation, h, nb, blk, e)

    def with_prev(t):
        prev = jnp.pad(t[:, :, :, :-1], ((0, 0), (0, 0), (0, 0), (1, 0), (0, 0), (0, 0)))
        return jnp.concatenate([prev, t], axis=4)

    qb, kb, vb = to_blocks(q), to_blocks(k), to_blocks(v)
    kw, vw = with_prev(kb), with_prev(vb)
    scores = jnp.einsum('bdhnqe,bdhnke->bdhnqk', qb, kw).astype(jnp.float32) * (e ** -0.5)
    qi = jnp.arange(blk)[:, None]
    ki = jnp.arange(2 * blk)[None, :]
    dist = blk + qi - ki
    first = (jnp.arange(nb) == 0)[:, None, None]
    valid = (dist >= 0) & (dist <= n_back) & ~(first & (ki < blk))
    scores = jnp.where(valid, scores, -jnp.inf)
    mx = jnp.max(scores, axis=-1, keepdims=True)
    p = jnp.exp(scores - mx)
    den = jnp.sum(p, axis=-1, keepdims=True)
    o = jnp.einsum('bdhnqk,bdhnke->bdhnqe', p, vw.astype(jnp.float32)) / den
    lse = (mx + jnp.log(den))[..., 0]
    o = o.reshape(b, dilation, h, L, e).transpose(0, 3, 1, 2, 4).reshape(b, s_pad, h, e)[:, :s]
    lse = lse.reshape(b, dilation, h, L).transpose(0, 3, 1, 2).reshape(b, s_pad, h)[:, :s]
    return o, lse


def dilated_attention(q, k, v, gq, gk):
    b, s = q.shape[:2]
    q = rms_norm(q, gq)
    k = rms_norm(k, gk)
    outs, lses = [], []
    for gi, (window, dilation) in enumerate(ATTN_GROUPS):
        sl = slice(gi * ATTN_HEADS_PER_GROUP, (gi + 1) * ATTN_HEADS_PER_GROUP)
        o, l = _dilated_group(q[:, :, sl], k[:, :, sl], v[:, :, sl], window, dilation)
        outs.append(o)
        lses.append(l)
    wts = jax.nn.softmax(jnp.stack(lses, axis=0), axis=0)
    o = jnp.sum(wts[..., None] * jnp.stack(outs, axis=0), axis=0)
    return o.reshape(b, s, ATTN_OUT).astype(v.dtype)


def gated_linear_attention(q, k, v, log_a):
    b, s, h, dk = q.shape
    dv = v.shape[-1]
    c = GLA_CHUNK
    n = s // c

    def chunks(t):
        return t.reshape(b, n, c, h, t.shape[-1]).transpose(1, 0, 3, 2, 4).astype(jnp.float32)

    qc = chunks(q) * (dk ** -0.5)
    kc, vc, ac = chunks(k), chunks(v), chunks(log_a)
    causal = jnp.tril(jnp.ones((c, c), dtype=bool))[:, :, None]

    def step(state, inp):
        qt, kt, vt, at = inp
        bcum = jnp.cumsum(at, axis=2)
        o_inter = jnp.einsum('bhtk,bhkv->bhtv', qt * jnp.exp(bcum), state)
        diff = bcum[:, :, :, None, :] - bcum[:, :, None, :, :]
        decay = jnp.exp(jnp.where(causal, diff, -jnp.inf))
        attn = jnp.einsum('bhtk,bhsk,bhtsk->bhts', qt, kt, decay)
        o_intra = jnp.einsum('bhts,bhsv->bhtv', attn, vt)
        blast = bcum[:, :, -1:, :]
        state = (jnp.exp(blast[:, :, 0, :])[..., None] * state
                 + jnp.einsum('bhsk,bhsv->bhkv', kt * jnp.exp(blast - bcum), vt))
        return state, o_inter + o_intra

    state0 = jnp.zeros((b, h, dk, dv), jnp.float32)
    _, o = lax.scan(step, state0, (qc, kc, vc, ac))
    return o.transpose(1, 0, 3, 2, 4).reshape(b, s, h, dv).astype(v.dtype)


def setup_inputs(seed: int = 0) -> dict:
    key = jax.random.key(seed)
    ks = jax.random.split(key, 16)
    f32 = jnp.float32

    def dense(k, fan_in, fan_out):
        return jax.random.normal(k, (DEPTH, fan_in, fan_out), f32) * fan_in ** -0.5

    def gain(k, n):
        return 1.0 + 0.02 * jax.random.normal(k, (DEPTH, n), f32)

    return {
        'x': jax.random.normal(ks[0], (BATCH, SEQ, D_MODEL), f32),
        'norm1_g': gain(ks[1], D_MODEL),
        'w_in': dense(ks[2], D_MODEL, D_IN),
        'attn_q_norm_g': gain(ks[3], ATTN_HEAD_DIM),
        'attn_k_norm_g': gain(ks[4], ATTN_HEAD_DIM),
        'gla_gate_up': dense(ks[5], GLA_GATE_RANK, GLA_DK),
        'gla_gate_bias': 0.1 * jax.random.normal(ks[6], (DEPTH, GLA_DK), f32),
        'gla_out_norm_g': gain(ks[7], GLA_HV),
        'branch_gate_bias': 0.1 * jax.random.normal(ks[8], (DEPTH, N_BRANCHES * D_MODEL), f32),
        'w_attn_branch': dense(ks[9], ATTN_OUT, D_MODEL),
        'w_gla_branch': dense(ks[10], GLA_DV, D_MODEL),
        'w_out': dense(ks[11], D_MODEL, D_MODEL),
        'norm2_g': gain(ks[12], D_MODEL),
        'w_ff_up': dense(ks[13], D_MODEL, D_FF),
        'w_ff_down': dense(ks[14], D_FF, D_MODEL),
    }


def reference(x, norm1_g, w_in, attn_q_norm_g, attn_k_norm_g, gla_gate_up, gla_gate_bias,
              gla_out_norm_g, branch_gate_bias, w_attn_branch, w_gla_branch, w_out,
              norm2_g, w_ff_up, w_ff_down):
    b, s, _ = x.shape
    for l in range(DEPTH):
        h = rms_norm(x, norm1_g[l])
        proj = jnp.einsum('bsd,de->bse', h, w_in[l])
        p_attn, p_q, p_k, p_v, p_r, p_a, p_gate = jnp.split(proj, _IN_OFFSETS, axis=-1)

        qkv = p_attn.reshape(b, s, 3, N_ATTN_HEADS, ATTN_HEAD_DIM)
        o_attn = dilated_attention(qkv[:, :, 0], qkv[:, :, 1], qkv[:, :, 2],
                                   attn_q_norm_g[l], attn_k_norm_g[l])

        gate_logits = (p_a @ gla_gate_up[l] + gla_gate_bias[l]).astype(jnp.float32)
        log_a = jax.nn.log_sigmoid(gate_logits) / GLA_TAU
        o_gla = gated_linear_attention(
            p_q.reshape(b, s, GLA_HEADS, GLA_HK),
            p_k.reshape(b, s, GLA_HEADS, GLA_HK),
            p_v.reshape(b, s, GLA_HEADS, GLA_HV),
            log_a.reshape(b, s, GLA_HEADS, GLA_HK))
        o_gla = rms_norm(o_gla, gla_out_norm_g[l]).reshape(b, s, GLA_DV) * jax.nn.silu(p_r)

        gates = jax.nn.sigmoid(p_gate + branch_gate_bias[l]).reshape(b, s, N_BRANCHES, D_MODEL)
        mixed = (gates[:, :, 0] * (o_attn @ w_attn_branch[l])
                 + gates[:, :, 1] * (o_gla @ w_gla_branch[l]))
        x = x + mixed @ w_out[l]

        h2 = rms_norm(x, norm2_g[l])
        x = x + jnp.square(jax.nn.relu(h2 @ w_ff_up[l])) @ w_ff_down[l]
    return x
```

```python
import os
import bisect
import numpy as np
import concourse.bass as bass
import concourse.mybir as mybir
from concourse.bass_utils import run_bass_kernel_spmd
from contextlib import ExitStack

F32 = mybir.dt.float32
BF16 = mybir.dt.bfloat16
AF = mybir.ActivationFunctionType
ALU = mybir.AluOpType

D = 1024
SEQ = 8192
NTOK = 2048
NPRE = 6144
NT_PRE = NPRE // 128
NT_OWN = NTOK // 128
DIN = 9744
EPS = 1e-6
O_AQ, O_AK, O_AV = 0, 1536, 3072
O_GQ, O_GK, O_GV, O_GR, O_GA, O_GATE = 4608, 5120, 5632, 6656, 7680, 7696
DIL = (1, 4, 16)

CF_UINC, CF_LST, CF_CIND, CF_G1, CF_G2, CF_GB, CF_GQ, CF_GK, CF_GOUT, CF_EPS, CF_ONE = \
    0, 128, 256, 258, 266, 274, 290, 291, 292, 548, 549
NCF = 552
CB_ID, CB_M4, CB_M4H0, CB_M4HH, CB_GM, CB_BONES, CB_ONES = 0, 128, 640, 1152, 1664, 1792, 1920
NCB = 1984

DEBUG = bool(os.environ.get("KDEBUG"))


class Prog:
    ENG = ('pe', 'act', 'dve', 'pool', 'sp')

    def __init__(self, nc):
        self.nc = nc
        self.ops = {e: [] for e in self.ENG}
        self.lastw = {}
        self.readers = {}
        self.seen = {e: {} for e in self.ENG}
        self.dma_cnt = {}

    def add(self, eng, fn, reads=(), writes=(), dma_key=None):
        deps = set()
        for k in reads:
            t = self.lastw.get(k)
            if t is not None:
                deps.add(t)
        for k in writes:
            t = self.lastw.get(k)
            if t is not None:
                deps.add(t)
            for t in self.readers.get(k, ()):
                deps.add(t)
        if dma_key is None:
            tok = (eng, len(self.ops[eng]))
        else:
            self.dma_cnt[dma_key] = self.dma_cnt.get(dma_key, 0) + 1
            tok = (('dma', dma_key), self.dma_cnt[dma_key])
        waits = {}
        for (s, v) in deps:
            if s == eng and eng == 'pe':
                continue
            if self.seen[eng].get(s, -1) >= v:
                continue
            waits[s] = max(waits.get(s, -1), v)
        for s, v in waits.items():
            self.seen[eng][s] = v
        self.ops[eng].append([waits, fn, tok, dma_key])
        for k in writes:
            self.lastw[k] = tok
            self.readers[k] = []
        for k in reads:
            self.readers.setdefault(k, []).append(tok)
        return tok

    def barrier(self):
        last = {}
        for e in self.ENG:
            n = [i for i, o in enumerate(self.ops[e]) if o[3] is None and o[1] is not None]
            if n:
                last[e] = n[-1]
        dm = dict(self.dma_cnt)
        for e in self.ENG:
            waits = {}
            for s, v in last.items():
                if s != e and self.seen[e].get(s, -1) < v:
                    waits[s] = v
            for dk, c in dm.items():
                s = ('dma', dk)
                if self.seen[e].get(s, -1) < c:
                    waits[s] = c
            for s, v in waits.items():
                self.seen[e][s] = v
            self.ops[e].append([waits, None, None, None])
        self.lastw = {}
        self.readers = {}

    def emit(self, stack, final_eng='sp'):
        nc = self.nc
        self.barrier()
        needed = {e: set() for e in self.ENG}
        for e in self.ENG:
            for waits, fn, tok, dk in self.ops[e]:
                for s, v in waits.items():
                    if not isinstance(s, tuple):
                        needed[s].add(v)
        semval = {}
        for e in self.ENG:
            for c, i in enumerate(sorted(needed[e])):
                semval[(e, i)] = c + 1
        sems = {}
        for e in self.ENG:
            sems[e] = stack.enter_context(nc.semaphore("sem_" + e))
        for dk in self.dma_cnt:
            sems[('dma', dk)] = stack.enter_context(nc.semaphore("dsem_%d" % len(sems)))
        self.n_sems = len(sems)
        self.max_semval = max(list(semval.values()) + [0])
        block = stack.enter_context(nc.Block())
        engmap = {'pe': block.tensor, 'act': block.scalar, 'dve': block.vector,
                  'pool': block.gpsimd, 'sp': block.sync}

        def run(ename, eobj):
            for idx, (waits, fn, tok, dk) in enumerate(self.ops[ename]):
                for s, v in waits.items():
                    if isinstance(s, tuple):
                        eobj.wait_ge(sems[s], 16 * v)
                    else:
                        eobj.wait_ge(sems[s], semval[(s, v)])
                if fn is None:
                    continue
                ins = fn(eobj)
                if dk is not None:
                    ins.then_inc(sems[tok[0]], 16)
                elif (ename, idx) in semval:
                    ins.then_inc(sems[ename], 1)

        for ename in self.ENG:
            def mk(ename):
                def f(eobj):
                    run(ename, eobj)
                return f
            engmap[ename](mk(ename))


class Region:
    def __init__(self, t, n):
        self.t = t
        self.n = n
        self.off = 0

    def reset(self):
        self.off = 0

    def bf(self, n):
        n2 = (n + 15) // 16 * 16
        assert self.off + n2 <= self.n, (self.off, n2, self.n)
        ap = self.t[:, self.off:self.off + n]
        self.off += n2
        return ap

    def f32(self, n):
        return self.bf(2 * n).bitcast(F32)


def build_program():
    nc = bass.Bass("TRN2", target_bir_lowering=False)

    def dram(name, shape, kind="ExternalInput", dt=F32):
        return nc.dram_tensor(name, list(shape), dt, kind=kind).ap()

    xin = dram("xin", [NPRE + NTOK, D])
    w_in = dram("w_in", [D, DIN])
    w_a = dram("w_a", [512, D])
    w_b = dram("w_b", [D, D])
    w_o = dram("w_o", [D, D])
    w_up = dram("w_up", [D, 4 * D])
    w_dn = dram("w_dn", [4 * D, D])
    cf_d = dram("cf", [128, NCF])
    cb_d = dram("cb", [128, NCB])
    gup_d = dram("gup", [17, 512])
    out_d = dram("out", [NTOK, D], kind="ExternalOutput")
    dbg = {}

    with ExitStack() as st:
        def sb(name, shape, dt):
            return st.enter_context(nc.sbuf_tensor(name, list(shape), dt))

        hTo = sb("hTo", [128, 8 * NTOK], BF16)
        hTh = sb("hTh", [128, 8 * NTOK], BF16)
        ogT = sb("ogT", [128, 8 * NTOK], BF16)
        oaT = sb("oaT", [128, 4 * NTOK], BF16)
        arena = sb("arena", [128, 32768], BF16)
        S = sb("S", [128, 1024], F32)
        Sbf = sb("Sbf", [128, 1024], BF16)
        cf = sb("cfs", [128, NCF], F32)
        cb = sb("cbs", [128, NCB], BF16)
        gup = sb("gups", [17, 512], F32)
        paT = sb("paT", [17, 256], F32)
        sm = sb("smalls", [128, 64], F32)
        banks = [st.enter_context(nc.psum_tensor("bank%d" % i, [128, 512], F32)) for i in range(8)]

        hTo3 = hTo[:, :].rearrange("p (c t) -> p c t", c=8)
        hTh3 = hTh[:, :].rearrange("p (c t) -> p c t", c=8)
        ogT3 = ogT[:, :].rearrange("p (c t) -> p c t", c=8)
        oaT3 = oaT[:, :].rearrange("p (c t) -> p c t", c=4)
        RA = Region(arena, 32768)
        RG = Region(ogT, 8 * NTOK)
        RH = Region(hTh, 8 * NTOK)
        RO = Region(hTo, 8 * NTOK)
        RT = Region(oaT, 4 * NTOK)

        P = Prog(nc)
        B = ['b%d' % i for i in range(8)]

        def MM(out, lhsT, rhs, start=True, stop=True, reads=(), writes=(), skip=False):
            if skip:
                P.add('pe', lambda e: e.matmul(out, lhsT, rhs, start=start, stop=stop, skip_group_check=True),
                      reads, writes)
            else:
                P.add('pe', lambda e: e.matmul(out, lhsT, rhs, start=start, stop=stop), reads, writes)

        def TR(out, in_, reads=(), writes=()):
            ident = cb[:, CB_ID:CB_ID + 128]
            P.add('pe', lambda e: e.transpose(out, in_, ident), reads, writes)

        def ACT(out, in_, func, reads=(), writes=(), scale=None, bias=None, accum_out=None):
            kw = {}
            if scale is not None:
                kw['scale'] = scale
            if bias is not None:
                kw['bias'] = bias
            if accum_out is not None:
                kw['accum_out'] = accum_out
            P.add('act', lambda e: e.activation(out=out, in_=in_, func=func, **kw), reads, writes)

        def TT(eng, out, in0, in1, op, reads=(), writes=()):
            P.add(eng, lambda e: e.tensor_tensor(out=out, in0=in0, in1=in1, op=op), reads, writes)

        def TS(eng, out, in0, s1, op0, reads=(), writes=(), s2=None, op1=None):
            if op1 is None:
                P.add(eng, lambda e: e.tensor_scalar(out=out, in0=in0, scalar1=s1, scalar2=None, op0=op0),
                      reads, writes)
            else:
                P.add(eng, lambda e: e.tensor_scalar(out=out, in0=in0, scalar1=s1, scalar2=s2, op0=op0, op1=op1),
                      reads, writes)

        def STT(out, in0, scalar, in1, op0, op1, reads=(), writes=()):
            P.add('dve', lambda e: e.scalar_tensor_tensor(out=out, in0=in0, scalar=scalar, in1=in1,
                                                          op0=op0, op1=op1), reads, writes)

        def CP(eng, out, in_, reads=(), writes=()):
            if eng == 'act':
                ACT(out, in_, AF.Copy, reads, writes)
            else:
                P.add(eng, lambda e: e.tensor_copy(out=out, in_=in_), reads, writes)

        def MSET(eng, ap, val, writes=()):
            P.add(eng, lambda e: e.memset(ap, val), (), writes)

        def DMA(eng, out, in_, key, reads=(), writes=()):
            P.add(eng, lambda e: e.dma_start(out=out, in_=in_), reads, writes, dma_key=key)

        def wcols(w, c0, n):
            return w[:, c0:c0 + n].rearrange("(c p) n -> p c n", p=128)

        def v3(ap, c):
            return ap.rearrange("p (c n) -> p c n", c=c)

        def dump(name, ap, shape, key_reads=()):
            if not DEBUG:
                return
            d = dram("dbg_" + name, shape, kind="ExternalOutput", dt=ap.dtype)
            dbg[name] = d
            DMA('sp', d, ap, 'dbg_' + name, reads=key_reads)

        eps_ap = cf[:, CF_EPS:CF_EPS + 1]
        one_ap = cf[:, CF_ONE:CF_ONE + 1]

        def rstd_from_sum(out_ap, in_ap, inv_n, key_in, key_out, tmpkey):
            ACT(out_ap, in_ap, AF.Ln, reads=[key_in], writes=[key_out], scale=inv_n, bias=eps_ap)
            ACT(out_ap, out_ap, AF.Exp, reads=[key_out], writes=[key_out], scale=-0.5)

        DMA('sp', cf[:, :], cf_d, 'cf', writes=['cf'])
        DMA('pool', cb[:, :], cb_d, 'cb', writes=['cb'])
        DMA('sp', gup[:, :], gup_d, 'gup', writes=['gup'])
        MSET('dve', paT[:, :], 1.0, writes=['paT0', 'paT1'])
        MSET('dve', S[:, :], 0.0, writes=['S0', 'S1', 'S2', 'S3'])
        MSET('pool', Sbf[:, :], 0.0, writes=['Sbf0', 'Sbf1', 'Sbf2', 'Sbf3'])
        P.barrier()

        RA.reset()
        Wk_tok = RA.bf(8 * 512)
        Wv = RA.bf(8 * 1024)
        Wpa = RA.bf(8 * 16)
        Wk3, Wv3, Wpa3 = v3(Wk_tok, 8), v3(Wv, 8), v3(Wpa, 8)
        xt = [RA.f32(1024) for _ in range(2)]
        xs = [RA.bf(1024) for _ in range(2)]
        junk = RA.bf(1024)
        hTt = [RA.bf(1024) for _ in range(2)]
        la = [RA.f32(512) for _ in range(2)]
        e1 = RA.f32(512)
        ekd = RA.f32(512)
        kd = [RA.bf(512) for _ in range(2)]
        vbf = [RA.bf(1024) for _ in range(2)]

        DMA('pool', Wk3, wcols(w_in, O_GK, 512), 'Wk', writes=['Wk'])
        DMA('pool', Wv3, wcols(w_in, O_GV, 1024), 'Wv', writes=['Wv'])
        DMA('pool', Wpa3, wcols(w_in, O_GA, 16), 'Wpa', writes=['Wpa'])

        g1b = bass.AP(cf, CF_G1, [[NCF, 128], [1, 8], [0, 128]])
        g2b = bass.AP(cf, CF_G2, [[NCF, 128], [1, 8], [0, 128]])
        lst_ap = cf[:, CF_LST:CF_LST + 128]
        uinc_ap = cf[:, CF_UINC:CF_UINC + 128]
        cind_ap = cf[:, CF_CIND:CF_CIND + 2]

        def norm_transpose(src_rows, s, dst3, gb, xkey=None):
            ss = sm[:, s:s + 1]
            rs = sm[:, 2 + s:3 + s]
            if xkey is None:
                DMA('sp', xt[s], src_rows, 'xt%d' % s, writes=['xt%d' % s])
                xap, xk = xt[s], 'xt%d' % s
            else:
                xap, xk = src_rows, xkey
            ACT(junk, xap, AF.Square, reads=[xk], writes=['junk', 'ss%d' % s], accum_out=ss)
            rstd_from_sum(rs, ss, 1.0 / D, 'ss%d' % s, 'rs%d' % s, 'rst%d' % s)
            TS('dve', xs[s], xap, rs, ALU.mult, reads=[xk, 'rs%d' % s], writes=['xs%d' % s])
            pb = banks[0][:, :].bitcast(BF16)
            for c in range(8):
                TR(pb[:, c * 128:(c + 1) * 128], xs[s][:, c * 128:(c + 1) * 128], reads=['xs%d' % s], writes=[B[0]])
            TT('dve', dst3, v3(pb, 8), gb, ALU.mult, reads=[B[0]], writes=['hT%d' % s])

        def gla_common(hT3, s, own):
            hk = 'hT%d' % s
            for c in range(8):
                MM(banks[1][:, :], hT3[:, c, :], Wk3[:, c, :], start=(c == 0), stop=(c == 7),
                   reads=[hk, 'Wk'], writes=[B[1]])
            for half in range(2):
                for c in range(8):
                    MM(banks[2 + half][:, :], hT3[:, c, :], Wv3[:, c, half * 512:(half + 1) * 512],
                       start=(c == 0), stop=(c == 7), reads=[hk, 'Wv'], writes=[B[2 + half]])
            for c in range(8):
                MM(banks[4][0:16, 0:128], Wpa3[:, c, :], hT3[:, c, :], start=(c == 0), stop=(c == 7),
                   reads=[hk, 'Wpa'], writes=[B[4]])
            CP('act', paT[0:16, s * 128:(s + 1) * 128], banks[4][0:16, 0:128], reads=[B[4]], writes=['paT%d' % s])
            MM(banks[5][:, :], paT[0:17, s * 128:(s + 1) * 128], gup[0:17, :], reads=['paT%d' % s, 'gup'],
               writes=[B[5]])
            ACT(e1, banks[5][:, :], AF.Exp, reads=[B[5]], writes=['e1'], scale=-1.0)
            ACT(la[s], e1, AF.Ln, reads=['e1'], writes=['la%d' % s], bias=one_ap)
            MM(banks[5][:, :], lst_ap, la[s], reads=['la%d' % s], writes=[B[5]])
            for h in range(4):
                MM(banks[4][:, 128 + 2 * h:130 + 2 * h], la[s][:, h * 128:(h + 1) * 128], cind_ap,
                   reads=['la%d' % s], writes=[B[4]])
            ACT(ekd, banks[5][:, :], AF.Exp, reads=[B[5]], writes=['ekd'], scale=-1.0 / 16)
            ebl = sm[:, 8 + 8 * s:16 + 8 * s]
            ACT(ebl, banks[4][:, 128:136], AF.Exp, reads=[B[4]], writes=['ebl%d' % s], scale=-1.0 / 16)
            TT('dve', kd[s], banks[1][:, :], ekd, ALU.mult, reads=[B[1], 'ekd'], writes=['kd%d' % s])
            CP('act', vbf[s][:, 0:512], banks[2][:, :], reads=[B[2]], writes=['vbfa%d' % s])
            CP('act', vbf[s][:, 512:1024], banks[3][:, :], reads=[B[3]], writes=['vbfb%d' % s])
            return ebl

        def state_update(s, c, h, ubank, ebl, own):
            uo = banks[ubank][:, (h % 2) * 256:(h % 2) * 256 + 256]
            vk = 'vbfa%d' % s if h < 2 else 'vbfb%d' % s
            MM(uo, kd[s][c * 64:(c + 1) * 64, h * 128:(h + 1) * 128],
               vbf[s][c * 64:(c + 1) * 64, h * 256:(h + 1) * 256],
               reads=['kd%d' % s, vk], writes=[B[ubank]])
            Sh = S[:, h * 256:(h + 1) * 256]
            STT(Sh, Sh, ebl[:, 2 * h + c:2 * h + c + 1], uo, ALU.mult, ALU.add,
                reads=['S%d' % h, 'ebl%d' % s, B[ubank]], writes=['S%d' % h])
            if own:
                CP('act', Sbf[:, h * 256:(h + 1) * 256], Sh, reads=['S%d' % h], writes=['Sbf%d' % h])

        for ti in range(NT_PRE):
            s = ti % 2
            if ti >= NT_PRE - NT_OWN:
                j = ti - (NT_PRE - NT_OWN)
                dst3 = hTh3[:, :, j * 128:(j + 1) * 128]
            else:
                dst3 = v3(hTt[s], 8)
            norm_transpose(xin[ti * 128:(ti + 1) * 128, :], s, dst3, g1b)
            ebl = gla_common(dst3, s, False)
            for c in range(2):
                for h in range(4):
                    state_update(s, c, h, 6 if h < 2 else 7, ebl, False)
        for j in range(NT_OWN):
            s = j % 2
            ti = NT_PRE + j
            norm_transpose(xin[ti * 128:(ti + 1) * 128, :], s, hTo3[:, :, j * 128:(j + 1) * 128], g1b)
        P.barrier()
        dump("hTo", hTo[:, :], [128, 8 * NTOK])
        dump("hTh", hTh[:, :], [128, 8 * NTOK])
        dump("Spre", S[:, :], [128, 1024])
        for h in range(4):
            CP('act', Sbf[:, h * 256:(h + 1) * 256], S[:, h * 256:(h + 1) * 256])
        P.barrier()

        RA.reset()
        RG.reset()
        Wq_a = [RA.bf(1024) for _ in range(3)]
        Wk_a = [RA.bf(1024) for _ in range(3)]
        QT = [RA.bf(NTOK) for _ in range(3)]
        HALO = [128, 512, 2048]
        KT = [RA.bf(HALO[g] + NTOK) for g in range(3)]
        NBLK = [17, 20, 32]
        Vb = [RA.bf(NBLK[g] * 128) for g in range(3)]
        Wv_a = [RG.bf(1024) for _ in range(3)]
        PT3h = [RG.bf(16 * 256) for _ in range(2)]
        ptb = [RG.bf(512) for _ in range(4)]
        sqb = [RG.bf(512) for _ in range(2)]
        lnb = [RG.f32(512) for _ in range(2)]
        lnd = [RA.f32(512) for _ in range(2)]
        gq_ap = cf[:, CF_GQ:CF_GQ + 1]
        gk_ap = cf[:, CF_GK:CF_GK + 1]
        bones = cb[:, CB_BONES:CB_BONES + 128]
        ones64 = cb[:, CB_ONES:CB_ONES + 64]
        m4 = {0: cb[:, CB_M4:CB_M4 + 512], 1: cb[:, CB_M4H0:CB_M4H0 + 512], 2: cb[:, CB_M4HH:CB_M4HH + 512]}
        cnt = {'qk': 0, 'pt': 0, 'rnd': 0, 'sc': 0, 'vb': 0}

        def qk_tile(wap, hsrc3, t0, n, dst, gain, wkey):
            i = cnt['qk']
            cnt['qk'] += 1
            pbk = (0, 5)[i % 2]
            pk = B[pbk]
            stb = 3 + (i % 2)
            w3 = v3(wap, 8)
            for c in range(8):
                MM(banks[pbk][:, 0:n], w3[:, c, :], hsrc3[:, c, t0:t0 + n], start=(c == 0), stop=(c == 7),
                   reads=[wkey], writes=[pk])
            sq = sqb[i % 2]
            ACT(sq[:, 0:n], banks[pbk][:, 0:n], AF.Square, reads=[pk], writes=['sq%d' % (i % 2)])
            MM(banks[stb][:, 0:n], bones, sq[:, 0:n], reads=['sq%d' % (i % 2)], writes=[B[stb]])
            ln = lnb[i % 2]
            ACT(ln[:, 0:n], banks[stb][:, 0:n], AF.Ln, reads=[B[stb]], writes=['ln%d' % (i % 2)], bias=eps_ap)
            ACT(ln[:, 0:n], ln[:, 0:n], AF.Exp, reads=['ln%d' % (i % 2)], writes=['ln%d' % (i % 2)], scale=-0.5)
            STT(dst, banks[pbk][:, 0:n], gain, ln[:, 0:n], ALU.mult, ALU.mult,
                reads=[pk, 'ln%d' % (i % 2)], writes=['qkdst'])

        for p in range(4):
            for g in range(3):
                hc = (g * 8 + 2 * p) * 64
                DMA('pool', v3(Wq_a[g], 8), wcols(w_in, O_AQ + hc, 128), 'Wq_a%d' % g, writes=['Wq_a%d' % g])
                DMA('pool', v3(Wk_a[g], 8), wcols(w_in, O_AK + hc, 128), 'Wk_a%d' % g, writes=['Wk_a%d' % g])
                DMA('pool', v3(Wv_a[g], 8), wcols(w_in, O_AV + hc, 128), 'Wv_a%d' % g, writes=['Wv_a%d' % g])
            for g in range(3):
                d = DIL[g]
                for u in range(4):
                    qk_tile(Wq_a[g], hTo3, u * 512, 512, QT[g][:, u * 512:(u + 1) * 512], gq_ap, 'Wq_a%d' % g)
                hl = HALO[g]
                nsp = max(1, hl // 512)
                w = hl // nsp
                for u in range(nsp):
                    qk_tile(Wk_a[g], hTh3, NTOK - hl + u * w, w, KT[g][:, u * w:(u + 1) * w], gk_ap, 'Wk_a%d' % g)
                for u in range(4):
                    qk_tile(Wk_a[g], hTo3, u * 512, 512, KT[g][:, hl + u * 512:hl + (u + 1) * 512], gk_ap,
                            'Wk_a%d' % g)
                seg = 128 * d
                wv3 = v3(Wv_a[g], 8)
                for b0 in range(0, NBLK[g], 4):
                    nb = min(4, NBLK[g] - b0)
                    vi = cnt['vb']
                    cnt['vb'] += 1
                    vbank = 3 + (vi % 2)
                    for bi in range(nb):
                        blk = b0 + bi
                        n_, r_ = blk // d - 1, blk % d
                        t0 = n_ * seg + r_
                        if t0 < 0:
                            src3, tt0 = hTh3, NTOK + t0
                        else:
                            src3, tt0 = hTo3, t0
                        for c in range(8):
                            MM(banks[vbank][:, bi * 128:(bi + 1) * 128],
                               src3[:, c, tt0:tt0 + 127 * d + 1:d], wv3[:, c, :],
                               start=(c == 0), stop=(c == 7), reads=['Wv_a%d' % g], writes=[B[vbank]])
                    CP('act' if vi % 2 == 0 else 'dve', Vb[g][:, b0 * 128:(b0 + nb) * 128],
                       banks[vbank][:, 0:nb * 128], reads=[B[vbank]], writes=['Vb'])
            for R in range(4):
                rn = cnt['rnd']
                cnt['rnd'] += 1
                nbk, dbk = (1, 2)
                numb, denb = banks[nbk], banks[dbk]
                first = {0: True, 1: True}
                lastmm = {}

                def pv(hd, g, kb, mov, cols, mkey, first=first):
                    po = hd * 64
                    stf = first[hd]
                    first[hd] = False
                    MM(numb[po:po + 64, cols], Vb[g][:, kb * 128 + hd * 64:kb * 128 + hd * 64 + 64], mov,
                       start=stf, stop=False, reads=['Vb', mkey], writes=[B[nbk] + str(hd)], skip=True)
                    MM(denb[po:po + 64, cols], ones64, mov,
                       start=stf, stop=False, reads=[mkey], writes=[B[dbk] + str(hd)], skip=True)

                def score_tile(hd, g, blks, mkind, dst_pt, dst_key):
                    d = DIL[g]
                    seg = 128 * d
                    hl = HALO[g]
                    po = hd * 64
                    si = cnt['sc']
                    cnt['sc'] += 1
                    sbank = 6 + (si % 2)
                    for bi, (n_, r_) in enumerate(blks):
                        t0 = n_ * seg + r_
                        qap = QT[g][po:po + 64, t0:t0 + 127 * d + 1:d]
                        kprev = KT[g][po:po + 64, hl + t0 - seg:hl + t0 - seg + 127 * d + 1:d]
                        kcur = KT[g][po:po + 64, hl + t0:hl + t0 + 127 * d + 1:d]
                        MM(banks[sbank][:, bi * 256:bi * 256 + 128], kprev, qap, reads=['qkdst'],
                           writes=[B[sbank]])
                        MM(banks[sbank][:, bi * 256 + 128:bi * 256 + 256], kcur, qap, reads=['qkdst'],
                           writes=[B[sbank]])
                    w = 256 * len(blks)
                    ACT(dst_pt[:, 0:w], banks[sbank][:, 0:w], AF.Exp, reads=[B[sbank]], writes=[dst_key],
                        scale=0.125)
                    TT('pool', dst_pt[:, 0:w], dst_pt[:, 0:w], m4[mkind][:, 0:w], ALU.mult,
                       reads=[dst_key], writes=[dst_key])

                for hd in range(2):
                    if R == 0:
                        for rp in range(8):
                            score_tile(hd, 2, [(0, 2 * rp), (0, 2 * rp + 1)], 2,
                                       PT3h[hd][:, rp * 512:(rp + 1) * 512], 'PT3_%d_%d' % (hd, rp))
                    for half in range(2):
                        n0 = 4 * R + 2 * half
                        i = cnt['pt']
                        cnt['pt'] += 1
                        pt = ptb[i % 4]
                        mk = 1 if n0 == 0 else 0
                        score_tile(hd, 0, [(n0, 0), (n0 + 1, 0)], mk, pt, 'pt%d' % (i % 4))
                        for bi in range(2):
                            n_ = n0 + bi
                            cols = slice((n_ - 4 * R) * 128, (n_ - 4 * R) * 128 + 128)
                            pv(hd, 0, n_, pt[:, bi * 256:bi * 256 + 128], cols, 'pt%d' % (i % 4))
                            pv(hd, 0, n_ + 1, pt[:, bi * 256 + 128:bi * 256 + 256], cols, 'pt%d' % (i % 4))
                    for half in range(2):
                        i = cnt['pt']
                        cnt['pt'] += 1
                        pt = ptb[i % 4]
                        mk = 2 if R == 0 else 0
                        score_tile(hd, 1, [(R, 2 * half), (R, 2 * half + 1)], mk, pt, 'pt%d' % (i % 4))
                        for bi in range(2):
                            r_ = 2 * half + bi
                            cols = slice(r_, 512, 4)
                            blk = (R + 1) * 4 + r_
                            pv(hd, 1, blk - 4, pt[:, bi * 256:bi * 256 + 128], cols, 'pt%d' % (i % 4))
                            pv(hd, 1, blk, pt[:, bi * 256 + 128:bi * 256 + 256], cols, 'pt%d' % (i % 4))
                    for r_ in range(16):
                        cols = slice(r_, 512, 16)
                        base = r_ * 256
                        pv(hd, 2, r_, PT3h[hd][:, base + 32 * R:base + 32 * R + 32], cols, 'PT3_%d_%d' % (hd, r_ // 2))
                        pv(hd, 2, 16 + r_, PT3h[hd][:, base + 128 + 32 * R:base + 128 + 32 * R + 32], cols,
                           'PT3_%d_%d' % (hd, r_ // 2))
                nk = [B[nbk] + '0', B[nbk] + '1']
                dk_ = [B[dbk] + '0', B[dbk] + '1']
                l = lnd[rn % 2]
                ACT(l, denb[:, :], AF.Ln, reads=dk_, writes=['lnd%d' % (rn % 2)])
                ACT(l, l, AF.Exp, reads=['lnd%d' % (rn % 2)], writes=['lnd%d' % (rn % 2)], scale=-1.0)
                TT('dve', oaT3[:, p, R * 512:(R + 1) * 512], numb[:, :], l, ALU.mult,
                   reads=nk + ['lnd%d' % (rn % 2)], writes=['oaT'])
            P.barrier()
        dump("oaT", oaT[:, :], [128, 4 * NTOK])
        P.barrier()

        RA.reset()
        RH.reset()
        Wk_tok = RA.bf(8 * 512)
        Wv = RA.bf(8 * 1024)
        Wpa = RA.bf(8 * 16)
        Wk3, Wv3, Wpa3 = v3(Wk_tok, 8), v3(Wv, 8), v3(Wpa, 8)
        la = [RA.f32(512) for _ in range(2)]
        e1 = RA.f32(512)
        ekd = RA.f32(512)
        kd = [RA.bf(512) for _ in range(2)]
        vbf = [RA.bf(1024) for _ in range(2)]
        junk = RA.bf(1024)
        Wq_g = RH.bf(8 * 512)
        Wkf_g = RH.bf(8 * 512)
        Wr_g = RH.bf(8 * 1024)
        Wq3, Wkf3, Wr3 = v3(Wq_g, 8), v3(Wkf_g, 8), v3(Wr_g, 8)
        sr = RA.f32(1024)
        eq = RA.f32(512)
        ek = RA.f32(512)
        Zq = RA.bf(4 * 256)
        ke = RA.bf(512)
        AT = RA.bf(512)
        t1 = RA.f32(1024)
        og = RA.bf(1024)
        gm_b = bass.AP(cb, CB_GM, [[NCB, 128], [0, 4], [1, 128]])
        gout_ap = cf[:, CF_GOUT:CF_GOUT + 256]
        DMA('pool', Wk3, wcols(w_in, O_GK, 512), 'Wk', writes=['Wk'])
        DMA('pool', Wv3, wcols(w_in, O_GV, 1024), 'Wv', writes=['Wv'])
        DMA('pool', Wpa3, wcols(w_in, O_GA, 16), 'Wpa', writes=['Wpa'])
        DMA('pool', Wq3, wcols(w_in, O_GQ, 512), 'Wq_g', writes=['Wq_g'])
        DMA('pool', Wkf3, wcols(w_in, O_GK, 512), 'Wkf_g', writes=['Wkf_g'])
        DMA('pool', Wr3, wcols(w_in, O_GR, 1024), 'Wr_g', writes=['Wr_g'])
        MSET('dve', Zq, 0.0, writes=['Zq0', 'Zq1', 'Zq2', 'Zq3'])

        for j in range(NT_OWN):
            s = j % 2
            hT3 = hTo3[:, :, j * 128:(j + 1) * 128]
            ebl = gla_common(hT3, s, True)
            for half in range(2):
                for c in range(8):
                    MM(banks[6 + half][:, :], hT3[:, c, :], Wr3[:, c, half * 512:(half + 1) * 512],
                       start=(c == 0), stop=(c == 7), reads=['Wr_g'], writes=[B[6 + half]])
                ACT(sr[:, half * 512:(half + 1) * 512], banks[6 + half][:, :], AF.Silu,
                    reads=[B[6 + half]], writes=['sr%d' % half])
            for h in range(4):
                for c in range(8):
                    MM(banks[0][:, h * 128:(h + 1) * 128], Wq3[:, c, h * 128:(h + 1) * 128], hT3[:, c, :],
                       start=(c == 0), stop=(c == 7), reads=['Wq_g'], writes=[B[0]])
            for h in range(4):
                for c in range(8):
                    MM(banks[1][:, h * 128:(h + 1) * 128], Wkf3[:, c, h * 128:(h + 1) * 128], hT3[:, c, :],
                       start=(c == 0), stop=(c == 7), reads=['Wkf_g'], writes=[B[1]])
            for h in range(4):
                MM(banks[2][:, h * 128:(h + 1) * 128], la[s][:, h * 128:(h + 1) * 128], uinc_ap,
                   reads=['la%d' % s], writes=[B[2]])
            ACT(eq, banks[2][:, :], AF.Exp, reads=[B[2]], writes=['eq'], scale=-1.0 / 16)
            ACT(ek, banks[2][:, :], AF.Exp, reads=[B[2]], writes=['ek'], scale=1.0 / 16)
            for h in range(4):
                zo = v3(Zq[:, h * 256:(h + 1) * 256], 2)[:, :, 64:128]
                STT(zo, v3(banks[0][:, h * 128:(h + 1) * 128], 2), float(128 ** -0.5),
                    v3(eq[:, h * 128:(h + 1) * 128], 2), ALU.mult, ALU.mult,
                    reads=[B[0], 'eq'], writes=['Zq%d' % h])
            TT('dve', ke, banks[1][:, :], ek, ALU.mult, reads=[B[1], 'ek'], writes=['ke'])
            for h in range(4):
                qfull = v3(Zq[:, h * 256:(h + 1) * 256], 2)[:, :, 64:128]
                MM(banks[3][:, h * 128:(h + 1) * 128], ke[:, h * 128:(h + 1) * 128], qfull,
                   reads=['ke', 'Zq%d' % h], writes=[B[3]])
            TT('dve', v3(AT, 4), v3(banks[3][:, :], 4), gm_b, ALU.mult, reads=[B[3]], writes=['AT'])
            ob = {0: 4, 1: 4, 2: 5, 3: 5}
            for h in range(4):
                okey = B[ob[h]]
                MM(banks[ob[h]][:, (h % 2) * 256:(h % 2) * 256 + 256], AT[:, h * 128:(h + 1) * 128],
                   vbf[s][:, h * 256:(h + 1) * 256], start=(h % 2 == 0), stop=False,
                   reads=['AT', 'vbfa%d' % s if h < 2 else 'vbfb%d' % s], writes=[okey],
                   skip=True)
            for c in range(2):
                for h in range(4):
                    okey = B[ob[h]]
                    zl = Zq[:, h * 256 + 64 + 64 * c:h * 256 + 192 + 64 * c]
                    MM(banks[ob[h]][:, (h % 2) * 256:(h % 2) * 256 + 256], zl, Sbf[:, h * 256:(h + 1) * 256],
                       start=False, stop=(c == 1), reads=['Zq%d' % h, 'Sbf%d' % h, okey], writes=[okey], skip=True)
                    state_update(s, c, h, 6 if h < 2 else 7, ebl, True)
            ssh = sm[:, 32:36]
            rsh = sm[:, 36:40]
            for h in range(4):
                okey = B[ob[h]]
                ACT(junk[:, 0:256], banks[ob[h]][:, (h % 2) * 256:(h % 2) * 256 + 256], AF.Square,
                    reads=[okey], writes=['junk', 'ssh%d' % h], accum_out=ssh[:, h:h + 1])
            ACT(rsh, ssh, AF.Ln, reads=['ssh0', 'ssh1', 'ssh2', 'ssh3'], writes=['rsh'], scale=1.0 / 256, bias=eps_ap)
            ACT(rsh, rsh, AF.Exp, reads=['rsh'], writes=['rsh'], scale=-0.5)
            for h in range(4):
                okey = B[ob[h]]
                STT(t1[:, h * 256:(h + 1) * 256], banks[ob[h]][:, (h % 2) * 256:(h % 2) * 256 + 256],
                    rsh[:, h:h + 1], gout_ap, ALU.mult, ALU.mult, reads=[okey, 'rsh'],
                    writes=['t1_%d' % h])
            TT('pool', og, t1, sr, ALU.mult, reads=['t1_0', 't1_1', 't1_2', 't1_3', 'sr0', 'sr1'], writes=['og'])
            pb = banks[0][:, :].bitcast(BF16)
            for c in range(8):
                TR(pb[:, c * 128:(c + 1) * 128], og[:, c * 128:(c + 1) * 128], reads=['og'], writes=[B[0]])
            CP('act', ogT3[:, :, j * 128:(j + 1) * 128], v3(pb, 8), reads=[B[0]], writes=['ogT'])
        P.barrier()
        dump("ogT", ogT[:, :], [128, 8 * NTOK])
        P.barrier()

        RA.reset()
        mixT3 = hTh3
        WA = [RA.bf(4 * 128) for _ in range(2)]
        WB = [RA.bf(8 * 128) for _ in range(2)]
        WGA = [RA.bf(8 * 128) for _ in range(2)]
        WGB = [RA.bf(8 * 128) for _ in range(2)]
        sgA = [RA.f32(512) for _ in range(2)]
        sgB = [RA.f32(512) for _ in range(2)]
        tA = [RA.f32(512) for _ in range(2)]
        tB = [RA.f32(512) for _ in range(2)]
        it = 0
        for jc in range(8):
            ws = jc % 2
            wa3, wb3, wga3, wgb3 = v3(WA[ws], 4), v3(WB[ws], 8), v3(WGA[ws], 8), v3(WGB[ws], 8)
            DMA('pool', wa3, wcols(w_a, jc * 128, 128), 'WA%d' % ws, writes=['WA%d' % ws])
            DMA('pool', wb3, wcols(w_b, jc * 128, 128), 'WB%d' % ws, writes=['WB%d' % ws])
            DMA('pool', wga3, wcols(w_in, O_GATE + jc * 128, 128), 'WGA%d' % ws, writes=['WGA%d' % ws])
            DMA('pool', wgb3, wcols(w_in, O_GATE + D + jc * 128, 128), 'WGB%d' % ws, writes=['WGB%d' % ws])
            for R in range(4):
                k = it % 2
                it += 1
                ts_ = slice(R * 512, (R + 1) * 512)
                bA, bB, bGA, bGB = (0 + 4 * k, 1 + 4 * k, 2 + 4 * k, 3 + 4 * k)
                for c in range(4):
                    MM(banks[bA][:, :], wa3[:, c, :], oaT3[:, c, ts_], start=(c == 0), stop=(c == 3),
                       reads=['WA%d' % ws], writes=[B[bA]])
                for c in range(8):
                    MM(banks[bB][:, :], wb3[:, c, :], ogT3[:, c, ts_], start=(c == 0), stop=(c == 7),
                       reads=['WB%d' % ws], writes=[B[bB]])
                for c in range(8):
                    MM(banks[bGA][:, :], wga3[:, c, :], hTo3[:, c, ts_], start=(c == 0), stop=(c == 7),
                       reads=['WGA%d' % ws], writes=[B[bGA]])
                for c in range(8):
                    MM(banks[bGB][:, :], wgb3[:, c, :], hTo3[:, c, ts_], start=(c == 0), stop=(c == 7),
                       reads=['WGB%d' % ws], writes=[B[bGB]])
                ACT(sgA[k], banks[bGA][:, :], AF.Sigmoid, reads=[B[bGA]], writes=['sgA%d' % k],
                    bias=cf[:, CF_GB + jc:CF_GB + jc + 1])
                ACT(sgB[k], banks[bGB][:, :], AF.Sigmoid, reads=[B[bGB]], writes=['sgB%d' % k],
                    bias=cf[:, CF_GB + 8 + jc:CF_GB + 8 + jc + 1])
                TT('dve', tA[k], banks[bA][:, :], sgA[k], ALU.mult, reads=[B[bA], 'sgA%d' % k], writes=['tA%d' % k])
                TT('dve', tB[k], banks[bB][:, :], sgB[k], ALU.mult, reads=[B[bB], 'sgB%d' % k], writes=['tB%d' % k])
                TT('pool', mixT3[:, jc, ts_], tA[k], tB[k], ALU.add, reads=['tA%d' % k, 'tB%d' % k], writes=['mixT'])
        P.barrier()
        dump("mixT", hTh[:, :], [128, 8 * NTOK])
        P.barrier()

        RA.reset()
        RG.reset()
        RT.reset()
        x1 = RA.f32(NT_OWN * 1024)
        Wo = RT.bf(8 * 1024)
        Wo3 = v3(Wo, 8)
        xt = [RG.f32(1024) for _ in range(2)]
        xs = [RG.bf(1024) for _ in range(2)]
        junk = RG.bf(1024)
        DMA('pool', Wo3, wcols(w_o, 0, 1024), 'Wo', writes=['Wo'])
        for j in range(NT_OWN):
            s = j % 2
            DMA('sp', xt[s], xin[NPRE + j * 128:NPRE + (j + 1) * 128, :], 'xt%d' % s, writes=['xt%d' % s])
            for half in range(2):
                bk = 1 + half + 2 * s
                for c in range(8):
                    MM(banks[bk][:, :], mixT3[:, c, j * 128:(j + 1) * 128], Wo3[:, c, half * 512:(half + 1) * 512],
                       start=(c == 0), stop=(c == 7), reads=['Wo'], writes=[B[bk]])
                TT('dve', x1[:, j * 1024 + half * 512:j * 1024 + (half + 1) * 512], banks[bk][:, :],
                   xt[s][:, half * 512:(half + 1) * 512], ALU.add, reads=[B[bk], 'xt%d' % s], writes=['x1t%d' % s])
            norm_transpose(x1[:, j * 1024:(j + 1) * 1024], s, hTo3[:, :, j * 128:(j + 1) * 128], g2b,
                           xkey='x1t%d' % s)
        P.barrier()
        dump("x1", x1, [128, NT_OWN * 1024])
        P.barrier()

        RG.reset()
        RH.reset()
        RT.reset()
        h2T3 = hTo3
        aT = [RG.bf(4 * NTOK) for _ in range(2)]
        Wup = [RH.bf(8 * 512) for _ in range(2)]
        Wdn = [RH.bf(4 * 1024) for _ in range(2)]
        rst = [RT.f32(512) for _ in range(2)]
        ui = 0
        di = 0
        for f in range(8):
            ws = f % 2
            wu3, wd3, a3 = v3(Wup[ws], 8), v3(Wdn[ws], 4), v3(aT[ws], 4)
            DMA('pool', wu3, wcols(w_up, f * 512, 512), 'Wup%d' % ws, writes=['Wup%d' % ws])
            DMA('pool', wd3, w_dn[f * 512:(f + 1) * 512, :].rearrange("(c p) n -> p c n", p=128), 'Wdn%d' % ws,
                writes=['Wdn%d' % ws])
            for q in range(4):
                for u in range(4):
                    bk = ui % 4
                    k = ui % 2
                    ui += 1
                    for c in range(8):
                        MM(banks[bk][:, :], wu3[:, c, q * 128:(q + 1) * 128], h2T3[:, c, u * 512:(u + 1) * 512],
                           start=(c == 0), stop=(c == 7), reads=['Wup%d' % ws], writes=[B[bk]])
                    ACT(rst[k], banks[bk][:, :], AF.Relu, reads=[B[bk]], writes=['rst%d' % k])
                    TT('pool', a3[:, q, u * 512:(u + 1) * 512], rst[k], rst[k], ALU.mult, reads=['rst%d' % k],
                       writes=['aT%d' % ws])
            for j in range(NT_OWN):
                for half in range(2):
                    bk = 4 + di % 4
                    di += 1
                    for q in range(4):
                        MM(banks[bk][:, :], a3[:, q, j * 128:(j + 1) * 128], wd3[:, q, half * 512:(half + 1) * 512],
                           start=(q == 0), stop=(q == 3), reads=['aT%d' % ws, 'Wdn%d' % ws], writes=[B[bk]])
                    xsl = x1[:, j * 1024 + half * 512:j * 1024 + (half + 1) * 512]
                    TT('dve', xsl, xsl, banks[bk][:, :], ALU.add, reads=[B[bk], 'x1_%d_%d' % (j, half)],
                       writes=['x1_%d_%d' % (j, half)])
                if f == 7:
                    DMA('sp', out_d[j * 128:(j + 1) * 128, :], x1[:, j * 1024:(j + 1) * 1024], 'out%d' % (j % 4),
                        reads=['x1_%d_0' % j, 'x1_%d_1' % j])
        P.emit(st)
        info = {"n_sems": P.n_sems, "max_semval": P.max_semval,
                "n_ops": {e: len(P.ops[e]) for e in P.ENG}}
    return nc, dbg, info


def _consts(inputs, flag):
    i = np.arange(128)
    same = (i[:, None] // 64) == (i[None, :] // 64)
    uinc = (same & (i[:, None] <= i[None, :])).astype(np.float32)
    lst = (same & (i[:, None] > i[None, :])).astype(np.float32)
    cf = np.zeros((128, NCF), np.float32)
    cf[:, CF_UINC:CF_UINC + 128] = uinc
    cf[:, CF_LST:CF_LST + 128] = lst
    cf[:64, CF_CIND] = 1.0
    cf[64:, CF_CIND + 1] = 1.0
    cf[:, CF_G1:CF_G1 + 8] = inputs['norm1_g'].reshape(8, 128).T
    cf[:, CF_G2:CF_G2 + 8] = inputs['norm2_g'].reshape(8, 128).T
    cf[:, CF_GB:CF_GB + 16] = inputs['branch_gate_bias'].reshape(16, 128).T
    cf[:, CF_GQ] = np.tile(inputs['attn_q_norm_g'].reshape(64), 2)
    cf[:, CF_GK] = np.tile(inputs['attn_k_norm_g'].reshape(64), 2)
    cf[:, CF_GOUT:CF_GOUT + 256] = np.broadcast_to(inputs['gla_out_norm_g'].reshape(1, 256), (128, 256))
    cf[:, CF_EPS] = EPS
    cf[:, CF_ONE] = 1.0
    cb = np.zeros((128, NCB), np.float32)
    cb[:, CB_ID:CB_ID + 128] = np.eye(128, dtype=np.float32)
    prev = (i[:, None] >= i[None, :]).astype(np.float32)
    cur = (i[:, None] <= i[None, :]).astype(np.float32)
    cb[:, CB_M4:CB_M4 + 512] = np.concatenate([prev, cur, prev, cur], 1)
    cb[:, CB_M4H0:CB_M4H0 + 512] = np.concatenate([prev * flag, cur, prev, cur], 1)
    cb[:, CB_M4HH:CB_M4HH + 512] = np.concatenate([prev * flag, cur, prev * flag, cur], 1)
    cb[:, CB_GM:CB_GM + 128] = uinc
    cb[:64, CB_BONES:CB_BONES + 64] = 1.0 / 64
    cb[64:, CB_BONES + 64:CB_BONES + 128] = 1.0 / 64
    cb[:, CB_ONES:CB_ONES + 64] = 1.0
    return cf, cb


_CACHE = {}


def kernel(**inputs):
    inputs = {k: np.asarray(v, dtype=np.float32) for k, v in inputs.items()}
    x = inputs['x']
    if 'nc' not in _CACHE:
        _CACHE['nc'] = build_program()
    nc, dbg, info = _CACHE['nc']
    w_in = np.ascontiguousarray(inputs['w_in'].reshape(D, DIN))
    shared = {
        "w_in": w_in,
        "w_a": np.ascontiguousarray(inputs['w_attn_branch'].reshape(512, D)),
        "w_b": np.ascontiguousarray(inputs['w_gla_branch'].reshape(D, D)),
        "w_o": np.ascontiguousarray(inputs['w_out'].reshape(D, D)),
        "w_up": np.ascontiguousarray(inputs['w_ff_up'].reshape(D, 4 * D)),
        "w_dn": np.ascontiguousarray(inputs['w_ff_down'].reshape(4 * D, D)),
        "gup": np.ascontiguousarray(np.concatenate([inputs['gla_gate_up'].reshape(16, 512),
                                                    inputs['gla_gate_bias'].reshape(1, 512)], 0)),
    }
    in_maps = []
    for c in range(8):
        b, ch = c // 4, c % 4
        xi = np.zeros((NPRE + NTOK, D), np.float32)
        npre = ch * NTOK
        if npre:
            xi[NPRE - npre:NPRE] = x[b, 0:npre]
        xi[NPRE:] = x[b, ch * NTOK:(ch + 1) * NTOK]
        cf, cb = _consts(inputs, 0.0 if ch == 0 else 1.0)
        m = dict(shared)
        m.update({"xin": xi, "cf": cf, "cb": cb})
        in_maps.append(m)
    res = run_bass_kernel_spmd(nc, in_maps, core_ids=list(range(8)))
    out = np.zeros((2, SEQ, D), np.float32)
    for c in range(8):
        b, ch = c // 4, c % 4
        out[b, ch * NTOK:(ch + 1) * NTOK] = res.results[c]["out"]
    if DEBUG:
        _CACHE['last'] = (res, dbg)
    return out
```

```python
import os
import bisect
import numpy as np
import concourse.bass as bass
import concourse.mybir as mybir
from concourse.bass_utils import run_bass_kernel_spmd
from contextlib import ExitStack

F32 = mybir.dt.float32
BF16 = mybir.dt.bfloat16
AF = mybir.ActivationFunctionType
ALU = mybir.AluOpType

D = 1024
SEQ = 8192
NTOK = 2048
NPRE = 6144
NT_PRE = NPRE // 128
NT_OWN = NTOK // 128
DIN = 9744
EPS = 1e-6
O_AQ, O_AK, O_AV = 0, 1536, 3072
O_GQ, O_GK, O_GV, O_GR, O_GA, O_GATE = 4608, 5120, 5632, 6656, 7680, 7696
DIL = (1, 4, 16)

CF_UINC, CF_LST, CF_CIND, CF_G1, CF_G2, CF_GB, CF_GQ, CF_GK, CF_GOUT, CF_EPS, CF_ONE = \
    0, 128, 256, 258, 266, 274, 290, 291, 292, 548, 549
NCF = 552
CB_ID, CB_M4, CB_M4H0, CB_M4HH, CB_GM, CB_BONES, CB_ONES = 0, 128, 640, 1152, 1664, 1792, 1920
NCB = 1984

DEBUG = bool(os.environ.get("KDEBUG"))


class Prog:
    ENG = ('pe', 'act', 'dve', 'pool', 'sp')

    def __init__(self, nc):
        self.nc = nc
        self.ops = {e: [] for e in self.ENG}
        self.lastw = {}
        self.readers = {}
        self.seen = {e: {} for e in self.ENG}
        self.dma_cnt = {}

    def add(self, eng, fn, reads=(), writes=(), dma_key=None):
        bank_r = [k for k in reads if len(k) >= 2 and k[0] == 'b' and k[1].isdigit()]
        if bank_r:
            reads = [k for k in reads if k not in bank_r]
            writes = list(writes) + bank_r
        deps = set()
        for k in reads:
            t = self.lastw.get(k)
            if t is not None:
                deps.add(t)
        for k in writes:
            t = self.lastw.get(k)
            if t is not None:
                deps.add(t)
            for t in self.readers.get(k, ()):
                deps.add(t)
        if dma_key is None:
            tok = (eng, len(self.ops[eng]))
        else:
            self.dma_cnt[dma_key] = self.dma_cnt.get(dma_key, 0) + 1
            tok = (('dma', dma_key), self.dma_cnt[dma_key])
        waits = {}
        for (s, v) in deps:
            if s == eng and eng == 'pe':
                continue
            if self.seen[eng].get(s, -1) >= v:
                continue
            waits[s] = max(waits.get(s, -1), v)
        for s, v in waits.items():
            self.seen[eng][s] = v
        self.ops[eng].append([waits, fn, tok, dma_key])
        for k in writes:
            self.lastw[k] = tok
            self.readers[k] = []
        for k in reads:
            self.readers.setdefault(k, []).append(tok)
        return tok

    def barrier(self):
        last = {}
        for e in self.ENG:
            n = [i for i, o in enumerate(self.ops[e]) if o[3] is None and o[1] is not None]
            if n:
                last[e] = n[-1]
        dm = dict(self.dma_cnt)
        for e in self.ENG:
            waits = {}
            for s, v in last.items():
                if s != e and self.seen[e].get(s, -1) < v:
                    waits[s] = v
            for dk, c in dm.items():
                s = ('dma', dk)
                if self.seen[e].get(s, -1) < c:
                    waits[s] = c
            for s, v in waits.items():
                self.seen[e][s] = v
            self.ops[e].append([waits, None, None, None])
        self.lastw = {}
        self.readers = {}

    def emit(self, stack, final_eng='sp'):
        nc = self.nc
        self.barrier()
        needed = {e: set() for e in self.ENG}
        for e in self.ENG:
            for waits, fn, tok, dk in self.ops[e]:
                for s, v in waits.items():
                    if not isinstance(s, tuple):
                        needed[s].add(v)
        semval = {}
        for e in self.ENG:
            for c, i in enumerate(sorted(needed[e])):
                semval[(e, i)] = c + 1
        sems = {}
        for e in self.ENG:
            sems[e] = stack.enter_context(nc.semaphore("sem_" + e))
        for dk in self.dma_cnt:
            sems[('dma', dk)] = stack.enter_context(nc.semaphore("dsem_%d" % len(sems)))
        self.n_sems = len(sems)
        self.max_semval = max(list(semval.values()) + [0])
        block = stack.enter_context(nc.Block())
        engmap = {'pe': block.tensor, 'act': block.scalar, 'dve': block.vector,
                  'pool': block.gpsimd, 'sp': block.sync}

        def run(ename, eobj):
            for idx, (waits, fn, tok, dk) in enumerate(self.ops[ename]):
                for s, v in waits.items():
                    if isinstance(s, tuple):
                        eobj.wait_ge(sems[s], 16 * v)
                    else:
                        eobj.wait_ge(sems[s], semval[(s, v)])
                if fn is None:
                    continue
                ins = fn(eobj)
                if dk is not None:
                    ins.then_inc(sems[tok[0]], 16)
                elif (ename, idx) in semval:
                    ins.then_inc(sems[ename], 1)

        for ename in self.ENG:
            def mk(ename):
                def f(eobj):
                    run(ename, eobj)
                return f
            engmap[ename](mk(ename))


class Region:
    def __init__(self, t, n):
        self.t = t
        self.n = n
        self.off = 0

    def reset(self):
        self.off = 0

    def bf(self, n):
        n2 = (n + 15) // 16 * 16
        assert self.off + n2 <= self.n, (self.off, n2, self.n)
        ap = self.t[:, self.off:self.off + n]
        self.off += n2
        return ap

    def f32(self, n):
        return self.bf(2 * n).bitcast(F32)


def build_program():
    nc = bass.Bass("TRN2", target_bir_lowering=False)

    def dram(name, shape, kind="ExternalInput", dt=F32):
        return nc.dram_tensor(name, list(shape), dt, kind=kind).ap()

    xin = dram("xin", [NPRE + NTOK, D])
    w_in = dram("w_in", [D, DIN])
    w_a = dram("w_a", [512, D])
    w_b = dram("w_b", [D, D])
    w_o = dram("w_o", [D, D])
    w_up = dram("w_up", [D, 4 * D])
    w_dn = dram("w_dn", [4 * D, D])
    cf_d = dram("cf", [128, NCF])
    cb_d = dram("cb", [128, NCB])
    gup_d = dram("gup", [17, 512])
    out_d = dram("out", [NTOK, D], kind="ExternalOutput")
    dbg = {}

    with ExitStack() as st:
        def sb(name, shape, dt):
            return st.enter_context(nc.sbuf_tensor(name, list(shape), dt))

        hTo = sb("hTo", [128, 8 * NTOK], BF16)
        hTh = sb("hTh", [128, 8 * NTOK], BF16)
        ogT = sb("ogT", [128, 8 * NTOK], BF16)
        oaT = sb("oaT", [128, 4 * NTOK], BF16)
        arena = sb("arena", [128, 32768], BF16)
        S = sb("S", [128, 1024], F32)
        Sbf = sb("Sbf", [128, 1024], BF16)
        cf = sb("cfs", [128, NCF], F32)
        cb = sb("cbs", [128, NCB], BF16)
        gup = sb("gups", [17, 512], F32)
        paT = sb("paT", [17, 256], F32)
        sm = sb("smalls", [128, 64], F32)
        banks = [st.enter_context(nc.psum_tensor("bank%d" % i, [128, 512], F32)) for i in range(8)]

        hTo3 = hTo[:, :].rearrange("p (c t) -> p c t", c=8)
        hTh3 = hTh[:, :].rearrange("p (c t) -> p c t", c=8)
        ogT3 = ogT[:, :].rearrange("p (c t) -> p c t", c=8)
        oaT3 = oaT[:, :].rearrange("p (c t) -> p c t", c=4)
        RA = Region(arena, 32768)
        RG = Region(ogT, 8 * NTOK)
        RH = Region(hTh, 8 * NTOK)
        RO = Region(hTo, 8 * NTOK)
        RT = Region(oaT, 4 * NTOK)

        P = Prog(nc)
        B = ['b%d' % i for i in range(8)]

        def MM(out, lhsT, rhs, start=True, stop=True, reads=(), writes=(), skip=False):
            if skip:
                P.add('pe', lambda e: e.matmul(out, lhsT, rhs, start=start, stop=stop, skip_group_check=True),
                      reads, writes)
            else:
                P.add('pe', lambda e: e.matmul(out, lhsT, rhs, start=start, stop=stop), reads, writes)

        def TR(out, in_, reads=(), writes=()):
            ident = cb[:, CB_ID:CB_ID + 128]
            P.add('pe', lambda e: e.transpose(out, in_, ident), reads, writes)

        def ACT(out, in_, func, reads=(), writes=(), scale=None, bias=None, accum_out=None):
            kw = {}
            if scale is not None:
                kw['scale'] = scale
            if bias is not None:
                kw['bias'] = bias
            if accum_out is not None:
                kw['accum_out'] = accum_out
            P.add('act', lambda e: e.activation(out=out, in_=in_, func=func, **kw), reads, writes)

        def TT(eng, out, in0, in1, op, reads=(), writes=()):
            P.add(eng, lambda e: e.tensor_tensor(out=out, in0=in0, in1=in1, op=op), reads, writes)

        def TS(eng, out, in0, s1, op0, reads=(), writes=(), s2=None, op1=None):
            if op1 is None:
                P.add(eng, lambda e: e.tensor_scalar(out=out, in0=in0, scalar1=s1, scalar2=None, op0=op0),
                      reads, writes)
            else:
                P.add(eng, lambda e: e.tensor_scalar(out=out, in0=in0, scalar1=s1, scalar2=s2, op0=op0, op1=op1),
                      reads, writes)

        def STT(out, in0, scalar, in1, op0, op1, reads=(), writes=()):
            P.add('dve', lambda e: e.scalar_tensor_tensor(out=out, in0=in0, scalar=scalar, in1=in1,
                                                          op0=op0, op1=op1), reads, writes)

        def CP(eng, out, in_, reads=(), writes=()):
            if eng == 'act':
                ACT(out, in_, AF.Copy, reads, writes)
            else:
                P.add(eng, lambda e: e.tensor_copy(out=out, in_=in_), reads, writes)

        def MSET(eng, ap, val, writes=()):
            P.add(eng, lambda e: e.memset(ap, val), (), writes)

        def DMA(eng, out, in_, key, reads=(), writes=()):
            P.add(eng, lambda e: e.dma_start(out=out, in_=in_), reads, writes, dma_key=key)

        def wcols(w, c0, n):
            return w[:, c0:c0 + n].rearrange("(c p) n -> p c n", p=128)

        def v3(ap, c):
            return ap.rearrange("p (c n) -> p c n", c=c)

        def dump(name, ap, shape, key_reads=()):
            if not DEBUG:
                return
            d = dram("dbg_" + name, shape, kind="ExternalOutput", dt=ap.dtype)
            dbg[name] = d
            DMA('sp', d, ap, 'dbg_' + name, reads=key_reads)

        eps_ap = cf[:, CF_EPS:CF_EPS + 1]
        one_ap = cf[:, CF_ONE:CF_ONE + 1]

        def rstd_from_sum(out_ap, in_ap, inv_n, key_in, key_out, tmpkey):
            ACT(out_ap, in_ap, AF.Ln, reads=[key_in], writes=[key_out], scale=inv_n, bias=eps_ap)
            ACT(out_ap, out_ap, AF.Exp, reads=[key_out], writes=[key_out], scale=-0.5)

        DMA('sp', cf[:, :], cf_d, 'cf', writes=['cf'])
        DMA('pool', cb[:, :], cb_d, 'cb', writes=['cb'])
        DMA('sp', gup[:, :], gup_d, 'gup', writes=['gup'])
        MSET('dve', paT[:, :], 1.0, writes=['paT0', 'paT1'])
        MSET('dve', S[:, :], 0.0, writes=['S0', 'S1', 'S2', 'S3'])
        MSET('pool', Sbf[:, :], 0.0, writes=['Sbf0', 'Sbf1', 'Sbf2', 'Sbf3'])
        P.barrier()

        RA.reset()
        Wk_tok = RA.bf(8 * 512)
        Wv = RA.bf(8 * 1024)
        Wpa = RA.bf(8 * 16)
        Wk3, Wv3, Wpa3 = v3(Wk_tok, 8), v3(Wv, 8), v3(Wpa, 8)
        xt = [RA.f32(1024) for _ in range(2)]
        xs = [RA.bf(1024) for _ in range(2)]
        junk = RA.bf(1024)
        hTt = [RA.bf(1024) for _ in range(2)]
        la = [RA.f32(512) for _ in range(2)]
        e1 = RA.f32(512)
        ekd = RA.f32(512)
        kd = [RA.bf(512) for _ in range(2)]
        vbf = [RA.bf(1024) for _ in range(2)]

        DMA('pool', Wk3, wcols(w_in, O_GK, 512), 'Wk', writes=['Wk'])
        DMA('pool', Wv3, wcols(w_in, O_GV, 1024), 'Wv', writes=['Wv'])
        DMA('pool', Wpa3, wcols(w_in, O_GA, 16), 'Wpa', writes=['Wpa'])

        g1b = bass.AP(cf, CF_G1, [[NCF, 128], [1, 8], [0, 128]])
        g2b = bass.AP(cf, CF_G2, [[NCF, 128], [1, 8], [0, 128]])
        lst_ap = cf[:, CF_LST:CF_LST + 128]
        uinc_ap = cf[:, CF_UINC:CF_UINC + 128]
        cind_ap = cf[:, CF_CIND:CF_CIND + 2]

        def norm_transpose(src_rows, s, dst3, gb, xkey=None):
            ss = sm[:, s:s + 1]
            rs = sm[:, 2 + s:3 + s]
            if xkey is None:
                DMA('sp', xt[s], src_rows, 'xt%d' % s, writes=['xt%d' % s])
                xap, xk = xt[s], 'xt%d' % s
            else:
                xap, xk = src_rows, xkey
            ACT(junk, xap, AF.Square, reads=[xk], writes=['junk', 'ss%d' % s], accum_out=ss)
            rstd_from_sum(rs, ss, 1.0 / D, 'ss%d' % s, 'rs%d' % s, 'rst%d' % s)
            TS('dve', xs[s], xap, rs, ALU.mult, reads=[xk, 'rs%d' % s], writes=['xs%d' % s])
            pb = banks[0][:, :].bitcast(BF16)
            for c in range(8):
                TR(pb[:, c * 128:(c + 1) * 128], xs[s][:, c * 128:(c + 1) * 128], reads=['xs%d' % s], writes=[B[0]])
            TT('dve', dst3, v3(pb, 8), gb, ALU.mult, reads=[B[0]], writes=['hT%d' % s])

        def gla_stage2(hT3, s, own):
            hk = 'hT%d' % s
            for c in range(8):
                MM(banks[4][0:16, 0:128], Wpa3[:, c, :], hT3[:, c, :], start=(c == 0), stop=(c == 7),
                   reads=[hk, 'Wpa'], writes=[B[4]])
            CP('act', paT[0:16, s * 128:(s + 1) * 128], banks[4][0:16, 0:128], reads=[B[4]], writes=['paT%d' % s])
            for c in range(8):
                MM(banks[1][:, :], hT3[:, c, :], Wk3[:, c, :], start=(c == 0), stop=(c == 7),
                   reads=[hk, 'Wk'], writes=[B[1]])
            if own:
                for c in range(8):
                    MM(banks[6][:, :], hT3[:, c, :], Wq3[:, c, :], start=(c == 0), stop=(c == 7),
                       reads=[hk, 'Wq_g'], writes=[B[6]])
            MM(banks[5][:, :], paT[0:17, s * 128:(s + 1) * 128], gup[0:17, :], reads=['paT%d' % s, 'gup'],
               writes=[B[5]])
            ACT(e1, banks[5][:, :], AF.Exp, reads=[B[5]], writes=['e1'], scale=-1.0)
            ACT(la[s], e1, AF.Ln, reads=['e1'], writes=['la%d' % s], bias=one_ap)
            for half in range(2):
                for c in range(8):
                    MM(banks[2 + half][:, :], hT3[:, c, :], Wv3[:, c, half * 512:(half + 1) * 512],
                       start=(c == 0), stop=(c == 7), reads=[hk, 'Wv'], writes=[B[2 + half]])
            CP('act', vbf[s][:, 0:512], banks[2][:, :], reads=[B[2]], writes=['vbfa%d' % s])
            CP('act', vbf[s][:, 512:1024], banks[3][:, :], reads=[B[3]], writes=['vbfb%d' % s])
            MM(banks[5][:, :], lst_ap, la[s], reads=['la%d' % s], writes=[B[5]])
            for h in range(4):
                MM(banks[4][:, 128 + 2 * h:130 + 2 * h], la[s][:, h * 128:(h + 1) * 128], cind_ap,
                   reads=['la%d' % s], writes=[B[4]])
            ACT(ekd, banks[5][:, :], AF.Exp, reads=[B[5]], writes=['ekd'], scale=-1.0 / 16)
            ebl = sm[:, 8 + 8 * s:16 + 8 * s]
            ACT(ebl, banks[4][:, 128:136], AF.Exp, reads=[B[4]], writes=['ebl%d' % s], scale=-1.0 / 16)
            TT('dve', kd[s], banks[1][:, :], ekd, ALU.mult, reads=[B[1], 'ekd'], writes=['kd%d' % s])
            return ebl

        def gla_stage3(s, own):
            ebl = sm[:, 8 + 8 * s:16 + 8 * s]
            for h in range(4):
                ub = 6 if h < 2 else 7
                MM(banks[ub][:, (h % 2) * 256:(h % 2) * 256 + 256], kd[s][:, h * 128:(h + 1) * 128],
                   vbf[s][:, h * 256:(h + 1) * 256],
                   reads=['kd%d' % s, 'vbfa%d' % s if h < 2 else 'vbfb%d' % s], writes=[B[ub]])
            for h in range(4):
                ub = 6 if h < 2 else 7
                Sh = S[:, h * 256:(h + 1) * 256]
                STT(Sh, Sh, ebl[:, 2 * h:2 * h + 1], banks[ub][:, (h % 2) * 256:(h % 2) * 256 + 256], ALU.mult, ALU.add,
                    reads=['S%d' % h, 'ebl%d' % s, B[ub]], writes=['S%d' % h])
                if own:
                    CP('act', Sbf[:, h * 256:(h + 1) * 256], Sh, reads=['S%d' % h], writes=['Sbf%d' % h])

        def tile_dst(ti):
            if ti >= NT_PRE:
                j = ti - NT_PRE
                return hTo3[:, :, j * 128:(j + 1) * 128]
            if ti >= NT_PRE - NT_OWN:
                j = ti - (NT_PRE - NT_OWN)
                return hTh3[:, :, j * 128:(j + 1) * 128]
            return v3(hTt[ti % 2], 8)

        NT_ALL = NT_PRE + NT_OWN
        norm_transpose(xin[0:128, :], 0, tile_dst(0), g1b)
        for ti in range(NT_PRE):
            s = ti % 2
            if ti + 1 < NT_ALL:
                norm_transpose(xin[(ti + 1) * 128:(ti + 2) * 128, :], (ti + 1) % 2, tile_dst(ti + 1), g1b)
            gla_stage2(tile_dst(ti), s, False)
            if ti >= 1:
                gla_stage3((ti - 1) % 2, False)
        gla_stage3((NT_PRE - 1) % 2, False)
        for ti in range(NT_PRE + 1, NT_ALL):
            norm_transpose(xin[ti * 128:(ti + 1) * 128, :], ti % 2, tile_dst(ti), g1b)
        P.barrier()
        dump("hTo", hTo[:, :], [128, 8 * NTOK])
        dump("hTh", hTh[:, :], [128, 8 * NTOK])
        dump("Spre", S[:, :], [128, 1024])
        for h in range(4):
            CP('act', Sbf[:, h * 256:(h + 1) * 256], S[:, h * 256:(h + 1) * 256])
        P.barrier()

        RA.reset()
        RG.reset()
        Wq_a = [RA.bf(1024) for _ in range(3)]
        Wk_a = [RA.bf(1024) for _ in range(3)]
        QT = [RA.bf(NTOK) for _ in range(3)]
        HALO = [128, 512, 2048]
        KT = [RA.bf(HALO[g] + NTOK) for g in range(3)]
        NBLK = [17, 20, 32]
        Vb = [RA.bf(NBLK[g] * 128) for g in range(3)]
        Wv_a = [RG.bf(1024) for _ in range(3)]
        PT3h = [RG.bf(16 * 256) for _ in range(2)]
        ptb = [RG.bf(512) for _ in range(4)]
        sqb = [RG.bf(512) for _ in range(2)]
        lnb = [RG.f32(512) for _ in range(2)]
        lnd = [RA.f32(512) for _ in range(2)]
        gq_ap = cf[:, CF_GQ:CF_GQ + 1]
        gk_ap = cf[:, CF_GK:CF_GK + 1]
        bones = cb[:, CB_BONES:CB_BONES + 128]
        ones64 = cb[:, CB_ONES:CB_ONES + 64]
        m4 = {0: cb[:, CB_M4:CB_M4 + 512], 1: cb[:, CB_M4H0:CB_M4H0 + 512], 2: cb[:, CB_M4HH:CB_M4HH + 512]}
        cnt = {'qk': 0, 'pt': 0, 'rnd': 0, 'sc': 0, 'vb': 0}

        def qk_tile(wap, hsrc3, t0, n, dst, gain, wkey):
            i = cnt['qk']
            cnt['qk'] += 1
            pbk = (0, 5)[i % 2]
            pk = B[pbk]
            stb = 3 + (i % 2)
            w3 = v3(wap, 8)
            for c in range(8):
                MM(banks[pbk][:, 0:n], w3[:, c, :], hsrc3[:, c, t0:t0 + n], start=(c == 0), stop=(c == 7),
                   reads=[wkey], writes=[pk])
            sq = sqb[i % 2]
            ACT(sq[:, 0:n], banks[pbk][:, 0:n], AF.Square, reads=[pk], writes=['sq%d' % (i % 2)])
            MM(banks[stb][:, 0:n], bones, sq[:, 0:n], reads=['sq%d' % (i % 2)], writes=[B[stb]])
            ln = lnb[i % 2]
            ACT(ln[:, 0:n], banks[stb][:, 0:n], AF.Ln, reads=[B[stb]], writes=['ln%d' % (i % 2)], bias=eps_ap)
            ACT(ln[:, 0:n], ln[:, 0:n], AF.Exp, reads=['ln%d' % (i % 2)], writes=['ln%d' % (i % 2)], scale=-0.5)
            STT(dst, banks[pbk][:, 0:n], gain, ln[:, 0:n], ALU.mult, ALU.mult,
                reads=[pk, 'ln%d' % (i % 2)], writes=['qkdst'])

        for p in range(4):
            for g in range(3):
                hc = (g * 8 + 2 * p) * 64
                DMA('pool', v3(Wq_a[g], 8), wcols(w_in, O_AQ + hc, 128), 'Wq_a%d' % g, writes=['Wq_a%d' % g])
                DMA('pool', v3(Wk_a[g], 8), wcols(w_in, O_AK + hc, 128), 'Wk_a%d' % g, writes=['Wk_a%d' % g])
                DMA('pool', v3(Wv_a[g], 8), wcols(w_in, O_AV + hc, 128), 'Wv_a%d' % g, writes=['Wv_a%d' % g])
            for g in range(3):
                d = DIL[g]
                for u in range(4):
                    qk_tile(Wq_a[g], hTo3, u * 512, 512, QT[g][:, u * 512:(u + 1) * 512], gq_ap, 'Wq_a%d' % g)
                hl = HALO[g]
                nsp = max(1, hl // 512)
                w = hl // nsp
                for u in range(nsp):
                    qk_tile(Wk_a[g], hTh3, NTOK - hl + u * w, w, KT[g][:, u * w:(u + 1) * w], gk_ap, 'Wk_a%d' % g)
                for u in range(4):
                    qk_tile(Wk_a[g], hTo3, u * 512, 512, KT[g][:, hl + u * 512:hl + (u + 1) * 512], gk_ap,
                            'Wk_a%d' % g)
                seg = 128 * d
                wv3 = v3(Wv_a[g], 8)
                for b0 in range(0, NBLK[g], 4):
                    nb = min(4, NBLK[g] - b0)
                    vi = cnt['vb']
                    cnt['vb'] += 1
                    vbank = 3 + (vi % 2)
                    for bi in range(nb):
                        blk = b0 + bi
                        n_, r_ = blk // d - 1, blk % d
                        t0 = n_ * seg + r_
                        if t0 < 0:
                            src3, tt0 = hTh3, NTOK + t0
                        else:
                            src3, tt0 = hTo3, t0
                        for c in range(8):
                            MM(banks[vbank][:, bi * 128:(bi + 1) * 128],
                               src3[:, c, tt0:tt0 + 127 * d + 1:d], wv3[:, c, :],
                               start=(c == 0), stop=(c == 7), reads=['Wv_a%d' % g], writes=[B[vbank]])
                    CP('act' if vi % 2 == 0 else 'dve', Vb[g][:, b0 * 128:(b0 + nb) * 128],
                       banks[vbank][:, 0:nb * 128], reads=[B[vbank]], writes=['Vb'])
            for R in range(4):
                rn = cnt['rnd']
                cnt['rnd'] += 1
                nbk, dbk = (1, 2)
                numb, denb = banks[nbk], banks[dbk]
                first = {0: True, 1: True}
                lastmm = {}

                def pv(hd, g, kb, mov, cols, mkey, first=first):
                    po = hd * 64
                    stf = first[hd]
                    first[hd] = False
                    MM(numb[po:po + 64, cols], Vb[g][:, kb * 128 + hd * 64:kb * 128 + hd * 64 + 64], mov,
                       start=stf, stop=False, reads=['Vb', mkey], writes=[B[nbk] + str(hd)], skip=True)
                    MM(denb[po:po + 64, cols], ones64, mov,
                       start=stf, stop=False, reads=[mkey], writes=[B[dbk] + str(hd)], skip=True)

                def score_tile(hd, g, blks, mkind, dst_pt, dst_key):
                    d = DIL[g]
                    seg = 128 * d
                    hl = HALO[g]
                    po = hd * 64
                    si = cnt['sc']
                    cnt['sc'] += 1
                    sbank = 6 + (si % 2)
                    for bi, (n_, r_) in enumerate(blks):
                        t0 = n_ * seg + r_
                        qap = QT[g][po:po + 64, t0:t0 + 127 * d + 1:d]
                        kprev = KT[g][po:po + 64, hl + t0 - seg:hl + t0 - seg + 127 * d + 1:d]
                        kcur = KT[g][po:po + 64, hl + t0:hl + t0 + 127 * d + 1:d]
                        MM(banks[sbank][:, bi * 256:bi * 256 + 128], kprev, qap, reads=['qkdst'],
                           writes=[B[sbank]])
                        MM(banks[sbank][:, bi * 256 + 128:bi * 256 + 256], kcur, qap, reads=['qkdst'],
                           writes=[B[sbank]])
                    w = 256 * len(blks)
                    ACT(dst_pt[:, 0:w], banks[sbank][:, 0:w], AF.Exp, reads=[B[sbank]], writes=[dst_key],
                        scale=0.125)
                    TT('pool', dst_pt[:, 0:w], dst_pt[:, 0:w], m4[mkind][:, 0:w], ALU.mult,
                       reads=[dst_key], writes=[dst_key])

                for hd in range(2):
                    if R == 0:
                        for rp in range(8):
                            score_tile(hd, 2, [(0, 2 * rp), (0, 2 * rp + 1)], 2,
                                       PT3h[hd][:, rp * 512:(rp + 1) * 512], 'PT3_%d_%d' % (hd, rp))
                    for half in range(2):
                        n0 = 4 * R + 2 * half
                        i = cnt['pt']
                        cnt['pt'] += 1
                        pt = ptb[i % 4]
                        mk = 1 if n0 == 0 else 0
                        score_tile(hd, 0, [(n0, 0), (n0 + 1, 0)], mk, pt, 'pt%d' % (i % 4))
                        for bi in range(2):
                            n_ = n0 + bi
                            cols = slice((n_ - 4 * R) * 128, (n_ - 4 * R) * 128 + 128)
                            pv(hd, 0, n_, pt[:, bi * 256:bi * 256 + 128], cols, 'pt%d' % (i % 4))
                            pv(hd, 0, n_ + 1, pt[:, bi * 256 + 128:bi * 256 + 256], cols, 'pt%d' % (i % 4))
                    for half in range(2):
                        i = cnt['pt']
                        cnt['pt'] += 1
                        pt = ptb[i % 4]
                        mk = 2 if R == 0 else 0
                        score_tile(hd, 1, [(R, 2 * half), (R, 2 * half + 1)], mk, pt, 'pt%d' % (i % 4))
                        for bi in range(2):
                            r_ = 2 * half + bi
                            cols = slice(r_, 512, 4)
                            blk = (R + 1) * 4 + r_
                            pv(hd, 1, blk - 4, pt[:, bi * 256:bi * 256 + 128], cols, 'pt%d' % (i % 4))
                            pv(hd, 1, blk, pt[:, bi * 256 + 128:bi * 256 + 256], cols, 'pt%d' % (i % 4))
                    for r_ in range(16):
                        cols = slice(r_, 512, 16)
                        base = r_ * 256
                        pv(hd, 2, r_, PT3h[hd][:, base + 32 * R:base + 32 * R + 32], cols, 'PT3_%d_%d' % (hd, r_ // 2))
                        pv(hd, 2, 16 + r_, PT3h[hd][:, base + 128 + 32 * R:base + 128 + 32 * R + 32], cols,
                           'PT3_%d_%d' % (hd, r_ // 2))
                nk = [B[nbk] + '0', B[nbk] + '1']
                dk_ = [B[dbk] + '0', B[dbk] + '1']
                l = lnd[rn % 2]
                ACT(l, denb[:, :], AF.Ln, reads=dk_, writes=['lnd%d' % (rn % 2)])
                ACT(l, l, AF.Exp, reads=['lnd%d' % (rn % 2)], writes=['lnd%d' % (rn % 2)], scale=-1.0)
                TT('dve', oaT3[:, p, R * 512:(R + 1) * 512], numb[:, :], l, ALU.mult,
                   reads=nk + ['lnd%d' % (rn % 2)], writes=['oaT'])
            P.barrier()
        dump("oaT", oaT[:, :], [128, 4 * NTOK])
        P.barrier()

        RA.reset()
        RH.reset()
        Wk_tok = RA.bf(8 * 512)
        Wv = RA.bf(8 * 1024)
        Wpa = RA.bf(8 * 16)
        Wk3, Wv3, Wpa3 = v3(Wk_tok, 8), v3(Wv, 8), v3(Wpa, 8)
        la = [RA.f32(512) for _ in range(2)]
        e1 = RA.f32(512)
        ekd = RA.f32(512)
        kd = [RA.bf(512) for _ in range(2)]
        vbf = [RA.bf(1024) for _ in range(2)]
        junk = RA.bf(1024)
        Wq_g = RH.bf(8 * 512)
        Wr_g = RH.bf(8 * 1024)
        Wq3, Wr3 = v3(Wq_g, 8), v3(Wr_g, 8)
        sr = RA.f32(1024)
        eq = RA.f32(512)
        ek = RA.f32(512)
        qe_tok = RA.bf(512)
        ke_tok = RA.bf(512)
        qeT = RA.bf(512)
        keT = RA.bf(512)
        AT = RA.bf(512)
        t1 = RA.f32(1024)
        og = RA.bf(1024)
        gm_b = bass.AP(cb, CB_GM, [[NCB, 128], [0, 4], [1, 128]])
        gout_ap = cf[:, CF_GOUT:CF_GOUT + 256]
        DMA('pool', Wk3, wcols(w_in, O_GK, 512), 'Wk', writes=['Wk'])
        DMA('pool', Wv3, wcols(w_in, O_GV, 1024), 'Wv', writes=['Wv'])
        DMA('pool', Wpa3, wcols(w_in, O_GA, 16), 'Wpa', writes=['Wpa'])
        DMA('pool', Wq3, wcols(w_in, O_GQ, 512), 'Wq_g', writes=['Wq_g'])
        DMA('pool', Wr3, wcols(w_in, O_GR, 1024), 'Wr_g', writes=['Wr_g'])

        for j in range(NT_OWN):
            s = j % 2
            hT3 = hTo3[:, :, j * 128:(j + 1) * 128]
            ebl = gla_stage2(hT3, s, True)
            MM(banks[5][:, :], uinc_ap, la[s], reads=['la%d' % s], writes=[B[5]])
            ACT(eq, banks[5][:, :], AF.Exp, reads=[B[5]], writes=['eq'], scale=-1.0 / 16)
            ACT(ek, banks[5][:, :], AF.Exp, reads=[B[5]], writes=['ek'], scale=1.0 / 16)
            STT(qe_tok, banks[6][:, :], float(128 ** -0.5), eq, ALU.mult, ALU.mult, reads=[B[6], 'eq'],
                writes=['qe_tok'])
            TT('dve', ke_tok, banks[1][:, :], ek, ALU.mult, reads=[B[1], 'ek'], writes=['ke_tok'])
            for half in range(2):
                for c in range(8):
                    MM(banks[2 + half][:, :], hT3[:, c, :], Wr3[:, c, half * 512:(half + 1) * 512],
                       start=(c == 0), stop=(c == 7), reads=['Wr_g'], writes=[B[2 + half]])
                ACT(sr[:, half * 512:(half + 1) * 512], banks[2 + half][:, :], AF.Silu,
                    reads=[B[2 + half]], writes=['sr%d' % half])
            pb = banks[0][:, :].bitcast(BF16)
            for h in range(4):
                TR(pb[:, h * 128:(h + 1) * 128], qe_tok[:, h * 128:(h + 1) * 128], reads=['qe_tok'], writes=[B[0]])
            for h in range(4):
                TR(pb[:, 512 + h * 128:512 + (h + 1) * 128], ke_tok[:, h * 128:(h + 1) * 128], reads=['ke_tok'],
                   writes=[B[0]])
            CP('act', qeT, pb[:, 0:512], reads=[B[0]], writes=['qeT'])
            CP('dve', keT, pb[:, 512:1024], reads=[B[0]], writes=['keT'])
            for h in range(4):
                MM(banks[7][:, h * 128:(h + 1) * 128], keT[:, h * 128:(h + 1) * 128], qeT[:, h * 128:(h + 1) * 128],
                   reads=['keT', 'qeT'], writes=[B[7]])
            TT('dve', v3(AT, 4), v3(banks[7][:, :], 4), gm_b, ALU.mult, reads=[B[7]], writes=['AT'])
            ob = {0: 2, 1: 2, 2: 3, 3: 3}
            for h in range(4):
                MM(banks[ob[h]][:, (h % 2) * 256:(h % 2) * 256 + 256], AT[:, h * 128:(h + 1) * 128],
                   vbf[s][:, h * 256:(h + 1) * 256], start=(h % 2 == 0), stop=False,
                   reads=['AT', 'vbfa%d' % s if h < 2 else 'vbfb%d' % s], writes=[B[ob[h]]], skip=True)
            for h in range(4):
                MM(banks[ob[h]][:, (h % 2) * 256:(h % 2) * 256 + 256], qeT[:, h * 128:(h + 1) * 128],
                   Sbf[:, h * 256:(h + 1) * 256], start=False, stop=True,
                   reads=['qeT', 'Sbf%d' % h], writes=[B[ob[h]]], skip=True)
            gla_stage3(s, True)
            ssh = sm[:, 32:36]
            rsh = sm[:, 36:40]
            for h in range(4):
                ACT(junk[:, h * 256:(h + 1) * 256], banks[ob[h]][:, (h % 2) * 256:(h % 2) * 256 + 256], AF.Square,
                    reads=[B[ob[h]]], writes=['junk%d' % h, 'ssh%d' % h], accum_out=ssh[:, h:h + 1])
            ACT(rsh, ssh, AF.Ln, reads=['ssh0', 'ssh1', 'ssh2', 'ssh3'], writes=['rsh'], scale=1.0 / 256, bias=eps_ap)
            ACT(rsh, rsh, AF.Exp, reads=['rsh'], writes=['rsh'], scale=-0.5)
            for h in range(4):
                STT(t1[:, h * 256:(h + 1) * 256], banks[ob[h]][:, (h % 2) * 256:(h % 2) * 256 + 256],
                    rsh[:, h:h + 1], gout_ap, ALU.mult, ALU.mult, reads=[B[ob[h]], 'rsh'],
                    writes=['t1_%d' % h])
            TT('pool', og, t1, sr, ALU.mult, reads=['t1_0', 't1_1', 't1_2', 't1_3', 'sr0', 'sr1'], writes=['og'])
            for c in range(8):
                TR(pb[:, c * 128:(c + 1) * 128], og[:, c * 128:(c + 1) * 128], reads=['og'], writes=[B[0]])
            CP('act', ogT3[:, :, j * 128:(j + 1) * 128], v3(pb, 8), reads=[B[0]], writes=['ogT'])
        P.barrier()
        dump("ogT", ogT[:, :], [128, 8 * NTOK])
        P.barrier()

        RA.reset()
        mixT3 = hTh3
        WA = [RA.bf(4 * 128) for _ in range(2)]
        WB = [RA.bf(8 * 128) for _ in range(2)]
        WGA = [RA.bf(8 * 128) for _ in range(2)]
        WGB = [RA.bf(8 * 128) for _ in range(2)]
        sgA = [RA.f32(512) for _ in range(2)]
        sgB = [RA.f32(512) for _ in range(2)]
        tA = [RA.f32(512) for _ in range(2)]
        tB = [RA.f32(512) for _ in range(2)]
        it = 0
        for jc in range(8):
            ws = jc % 2
            wa3, wb3, wga3, wgb3 = v3(WA[ws], 4), v3(WB[ws], 8), v3(WGA[ws], 8), v3(WGB[ws], 8)
            DMA('pool', wa3, wcols(w_a, jc * 128, 128), 'WA%d' % ws, writes=['WA%d' % ws])
            DMA('pool', wb3, wcols(w_b, jc * 128, 128), 'WB%d' % ws, writes=['WB%d' % ws])
            DMA('pool', wga3, wcols(w_in, O_GATE + jc * 128, 128), 'WGA%d' % ws, writes=['WGA%d' % ws])
            DMA('pool', wgb3, wcols(w_in, O_GATE + D + jc * 128, 128), 'WGB%d' % ws, writes=['WGB%d' % ws])
            for R in range(4):
                k = it % 2
                it += 1
                ts_ = slice(R * 512, (R + 1) * 512)
                bA, bB, bGA, bGB = (0 + 4 * k, 1 + 4 * k, 2 + 4 * k, 3 + 4 * k)
                for c in range(4):
                    MM(banks[bA][:, :], wa3[:, c, :], oaT3[:, c, ts_], start=(c == 0), stop=(c == 3),
                       reads=['WA%d' % ws], writes=[B[bA]])
                for c in range(8):
                    MM(banks[bB][:, :], wb3[:, c, :], ogT3[:, c, ts_], start=(c == 0), stop=(c == 7),
                       reads=['WB%d' % ws], writes=[B[bB]])
                for c in range(8):
                    MM(banks[bGA][:, :], wga3[:, c, :], hTo3[:, c, ts_], start=(c == 0), stop=(c == 7),
                       reads=['WGA%d' % ws], writes=[B[bGA]])
                for c in range(8):
                    MM(banks[bGB][:, :], wgb3[:, c, :], hTo3[:, c, ts_], start=(c == 0), stop=(c == 7),
                       reads=['WGB%d' % ws], writes=[B[bGB]])
                ACT(sgA[k], banks[bGA][:, :], AF.Sigmoid, reads=[B[bGA]], writes=['sgA%d' % k],
                    bias=cf[:, CF_GB + jc:CF_GB + jc + 1])
                ACT(sgB[k], banks[bGB][:, :], AF.Sigmoid, reads=[B[bGB]], writes=['sgB%d' % k],
                    bias=cf[:, CF_GB + 8 + jc:CF_GB + 8 + jc + 1])
                TT('dve', tA[k], banks[bA][:, :], sgA[k], ALU.mult, reads=[B[bA], 'sgA%d' % k], writes=['tA%d' % k])
                TT('dve', tB[k], banks[bB][:, :], sgB[k], ALU.mult, reads=[B[bB], 'sgB%d' % k], writes=['tB%d' % k])
                TT('pool', mixT3[:, jc, ts_], tA[k], tB[k], ALU.add, reads=['tA%d' % k, 'tB%d' % k], writes=['mixT'])
        P.barrier()
        dump("mixT", hTh[:, :], [128, 8 * NTOK])
        P.barrier()

        RA.reset()
        RG.reset()
        RT.reset()
        x1 = RA.f32(NT_OWN * 1024)
        Wo = RT.bf(8 * 1024)
        Wo3 = v3(Wo, 8)
        xt = [RG.f32(1024) for _ in range(2)]
        xs = [RG.bf(1024) for _ in range(2)]
        junk = RG.bf(1024)
        DMA('pool', Wo3, wcols(w_o, 0, 1024), 'Wo', writes=['Wo'])
        for j in range(NT_OWN):
            s = j % 2
            DMA('sp', xt[s], xin[NPRE + j * 128:NPRE + (j + 1) * 128, :], 'xt%d' % s, writes=['xt%d' % s])
            for half in range(2):
                bk = 1 + half + 2 * s
                for c in range(8):
                    MM(banks[bk][:, :], mixT3[:, c, j * 128:(j + 1) * 128], Wo3[:, c, half * 512:(half + 1) * 512],
                       start=(c == 0), stop=(c == 7), reads=['Wo'], writes=[B[bk]])
                TT('dve', x1[:, j * 1024 + half * 512:j * 1024 + (half + 1) * 512], banks[bk][:, :],
                   xt[s][:, half * 512:(half + 1) * 512], ALU.add, reads=[B[bk], 'xt%d' % s], writes=['x1t%d' % s])
            norm_transpose(x1[:, j * 1024:(j + 1) * 1024], s, hTo3[:, :, j * 128:(j + 1) * 128], g2b,
                           xkey='x1t%d' % s)
        P.barrier()
        dump("x1", x1, [128, NT_OWN * 1024])
        P.barrier()

        RG.reset()
        RH.reset()
        RT.reset()
        h2T3 = hTo3
        aT = [RG.bf(4 * NTOK) for _ in range(2)]
        Wup = [RH.bf(8 * 512) for _ in range(2)]
        Wdn = [RH.bf(4 * 1024) for _ in range(2)]
        rst = [RT.f32(512) for _ in range(2)]
        ui = 0
        di = 0
        for f in range(8):
            ws = f % 2
            wu3, wd3, a3 = v3(Wup[ws], 8), v3(Wdn[ws], 4), v3(aT[ws], 4)
            DMA('pool', wu3, wcols(w_up, f * 512, 512), 'Wup%d' % ws, writes=['Wup%d' % ws])
            DMA('pool', wd3, w_dn[f * 512:(f + 1) * 512, :].rearrange("(c p) n -> p c n", p=128), 'Wdn%d' % ws,
                writes=['Wdn%d' % ws])
            for q in range(4):
                for u in range(4):
                    bk = ui % 4
                    k = ui % 2
                    ui += 1
                    for c in range(8):
                        MM(banks[bk][:, :], wu3[:, c, q * 128:(q + 1) * 128], h2T3[:, c, u * 512:(u + 1) * 512],
                           start=(c == 0), stop=(c == 7), reads=['Wup%d' % ws], writes=[B[bk]])
                    ACT(rst[k], banks[bk][:, :], AF.Relu, reads=[B[bk]], writes=['rst%d' % k])
                    TT('pool', a3[:, q, u * 512:(u + 1) * 512], rst[k], rst[k], ALU.mult, reads=['rst%d' % k],
                       writes=['aT%d' % ws])
            for j in range(NT_OWN):
                for half in range(2):
                    bk = 4 + di % 4
                    di += 1
                    for q in range(4):
                        MM(banks[bk][:, :], a3[:, q, j * 128:(j + 1) * 128], wd3[:, q, half * 512:(half + 1) * 512],
                           start=(q == 0), stop=(q == 3), reads=['aT%d' % ws, 'Wdn%d' % ws], writes=[B[bk]])
                    xsl = x1[:, j * 1024 + half * 512:j * 1024 + (half + 1) * 512]
                    TT('dve', xsl, xsl, banks[bk][:, :], ALU.add, reads=[B[bk], 'x1_%d_%d' % (j, half)],
                       writes=['x1_%d_%d' % (j, half)])
                if f == 7:
                    DMA('sp', out_d[j * 128:(j + 1) * 128, :], x1[:, j * 1024:(j + 1) * 1024], 'out%d' % (j % 4),
                        reads=['x1_%d_0' % j, 'x1_%d_1' % j])
        P.emit(st)
        info = {"n_sems": P.n_sems, "max_semval": P.max_semval,
                "n_ops": {e: len(P.ops[e]) for e in P.ENG}}
    return nc, dbg, info


def _consts(inputs, flag):
    i = np.arange(128)
    same = np.ones((128, 128), bool)
    uinc = (same & (i[:, None] <= i[None, :])).astype(np.float32)
    lst = (same & (i[:, None] > i[None, :])).astype(np.float32)
    cf = np.zeros((128, NCF), np.float32)
    cf[:, CF_UINC:CF_UINC + 128] = uinc
    cf[:, CF_LST:CF_LST + 128] = lst
    cf[:, CF_CIND:CF_CIND + 2] = 1.0
    cf[:, CF_G1:CF_G1 + 8] = inputs['norm1_g'].reshape(8, 128).T
    cf[:, CF_G2:CF_G2 + 8] = inputs['norm2_g'].reshape(8, 128).T
    cf[:, CF_GB:CF_GB + 16] = inputs['branch_gate_bias'].reshape(16, 128).T
    cf[:, CF_GQ] = np.tile(inputs['attn_q_norm_g'].reshape(64), 2)
    cf[:, CF_GK] = np.tile(inputs['attn_k_norm_g'].reshape(64), 2)
    cf[:, CF_GOUT:CF_GOUT + 256] = np.broadcast_to(inputs['gla_out_norm_g'].reshape(1, 256), (128, 256))
    cf[:, CF_EPS] = EPS
    cf[:, CF_ONE] = 1.0
    cb = np.zeros((128, NCB), np.float32)
    cb[:, CB_ID:CB_ID + 128] = np.eye(128, dtype=np.float32)
    prev = (i[:, None] >= i[None, :]).astype(np.float32)
    cur = (i[:, None] <= i[None, :]).astype(np.float32)
    cb[:, CB_M4:CB_M4 + 512] = np.concatenate([prev, cur, prev, cur], 1)
    cb[:, CB_M4H0:CB_M4H0 + 512] = np.concatenate([prev * flag, cur, prev, cur], 1)
    cb[:, CB_M4HH:CB_M4HH + 512] = np.concatenate([prev * flag, cur, prev * flag, cur], 1)
    cb[:, CB_GM:CB_GM + 128] = uinc
    cb[:64, CB_BONES:CB_BONES + 64] = 1.0 / 64
    cb[64:, CB_BONES + 64:CB_BONES + 128] = 1.0 / 64
    cb[:, CB_ONES:CB_ONES + 64] = 1.0
    return cf, cb


_CACHE = {}


def kernel(**inputs):
    inputs = {k: np.asarray(v, dtype=np.float32) for k, v in inputs.items()}
    x = inputs['x']
    if 'nc' not in _CACHE:
        _CACHE['nc'] = build_program()
    nc, dbg, info = _CACHE['nc']
    w_in = np.ascontiguousarray(inputs['w_in'].reshape(D, DIN))
    shared = {
        "w_in": w_in,
        "w_a": np.ascontiguousarray(inputs['w_attn_branch'].reshape(512, D)),
        "w_b": np.ascontiguousarray(inputs['w_gla_branch'].reshape(D, D)),
        "w_o": np.ascontiguousarray(inputs['w_out'].reshape(D, D)),
        "w_up": np.ascontiguousarray(inputs['w_ff_up'].reshape(D, 4 * D)),
        "w_dn": np.ascontiguousarray(inputs['w_ff_down'].reshape(4 * D, D)),
        "gup": np.ascontiguousarray(np.concatenate([inputs['gla_gate_up'].reshape(16, 512),
                                                    inputs['gla_gate_bias'].reshape(1, 512)], 0)),
    }
    in_maps = []
    for c in range(8):
        b, ch = c // 4, c % 4
        xi = np.zeros((NPRE + NTOK, D), np.float32)
        npre = ch * NTOK
        if npre:
            xi[NPRE - npre:NPRE] = x[b, 0:npre]
        xi[NPRE:] = x[b, ch * NTOK:(ch + 1) * NTOK]
        cf, cb = _consts(inputs, 0.0 if ch == 0 else 1.0)
        m = dict(shared)
        m.update({"xin": xi, "cf": cf, "cb": cb})
        in_maps.append(m)
    res = run_bass_kernel_spmd(nc, in_maps, core_ids=list(range(8)))
    out = np.zeros((2, SEQ, D), np.float32)
    for c in range(8):
        b, ch = c // 4, c % 4
        out[b, ch * NTOK:(ch + 1) * NTOK] = res.results[c]["out"]
    if DEBUG:
        _CACHE['last'] = (res, dbg)
    return out
```

```python
import os
import bisect
import numpy as np
import concourse.bass as bass
import concourse.mybir as mybir
from concourse.bass_utils import run_bass_kernel_spmd
from contextlib import ExitStack

F32 = mybir.dt.float32
BF16 = mybir.dt.bfloat16
AF = mybir.ActivationFunctionType
ALU = mybir.AluOpType

D = 1024
SEQ = 8192
NTOK = 2048
NPRE = 6144
NT_PRE = NPRE // 128
NT_OWN = NTOK // 128
DIN = 9744
EPS = 1e-6
O_AQ, O_AK, O_AV = 0, 1536, 3072
O_GQ, O_GK, O_GV, O_GR, O_GA, O_GATE = 4608, 5120, 5632, 6656, 7680, 7696
DIL = (1, 4, 16)

CF_UINC, CF_LST, CF_CIND, CF_G1, CF_G2, CF_GB, CF_GQ, CF_GK, CF_GOUT, CF_EPS, CF_ONE = \
    0, 128, 256, 258, 266, 274, 290, 291, 292, 548, 549
NCF = 552
CB_ID, CB_M4, CB_M4H0, CB_M4HH, CB_GM, CB_BONES, CB_ONES = 0, 128, 640, 1152, 1664, 1792, 1920
NCB = 1984

DEBUG = bool(os.environ.get("KDEBUG"))


class Prog:
    ENG = ('pe', 'act', 'dve', 'pool', 'sp')

    def __init__(self, nc):
        self.nc = nc
        self.ops = {e: [] for e in self.ENG}
        self.lastw = {}
        self.readers = {}
        self.seen = {e: {} for e in self.ENG}
        self.dma_cnt = {}

    def add(self, eng, fn, reads=(), writes=(), dma_key=None):
        bank_r = [k for k in reads if len(k) >= 2 and k[0] == 'b' and k[1].isdigit()]
        if bank_r:
            reads = [k for k in reads if k not in bank_r]
            writes = list(writes) + bank_r
        deps = set()
        for k in reads:
            t = self.lastw.get(k)
            if t is not None:
                deps.add(t)
        for k in writes:
            t = self.lastw.get(k)
            if t is not None:
                deps.add(t)
            for t in self.readers.get(k, ()):
                deps.add(t)
        if dma_key is None:
            tok = (eng, len(self.ops[eng]))
        else:
            self.dma_cnt[dma_key] = self.dma_cnt.get(dma_key, 0) + 1
            tok = (('dma', dma_key), self.dma_cnt[dma_key])
        waits = {}
        for (s, v) in deps:
            if s == eng and eng == 'pe':
                continue
            if self.seen[eng].get(s, -1) >= v:
                continue
            waits[s] = max(waits.get(s, -1), v)
        for s, v in waits.items():
            self.seen[eng][s] = v
        self.ops[eng].append([waits, fn, tok, dma_key])
        for k in writes:
            self.lastw[k] = tok
            self.readers[k] = []
        for k in reads:
            self.readers.setdefault(k, []).append(tok)
        return tok

    def barrier(self):
        last = {}
        for e in self.ENG:
            n = [i for i, o in enumerate(self.ops[e]) if o[3] is None and o[1] is not None]
            if n:
                last[e] = n[-1]
        dm = dict(self.dma_cnt)
        for e in self.ENG:
            waits = {}
            for s, v in last.items():
                if s != e and self.seen[e].get(s, -1) < v:
                    waits[s] = v
            for dk, c in dm.items():
                s = ('dma', dk)
                if self.seen[e].get(s, -1) < c:
                    waits[s] = c
            for s, v in waits.items():
                self.seen[e][s] = v
            self.ops[e].append([waits, None, None, None])
        self.lastw = {}
        self.readers = {}

    def emit(self, stack, final_eng='sp'):
        nc = self.nc
        self.barrier()
        needed = {e: set() for e in self.ENG}
        for e in self.ENG:
            for waits, fn, tok, dk in self.ops[e]:
                for s, v in waits.items():
                    if not isinstance(s, tuple):
                        needed[s].add(v)
        semval = {}
        for e in self.ENG:
            for c, i in enumerate(sorted(needed[e])):
                semval[(e, i)] = c + 1
        sems = {}
        for e in self.ENG:
            sems[e] = stack.enter_context(nc.semaphore("sem_" + e))
        for dk in self.dma_cnt:
            sems[('dma', dk)] = stack.enter_context(nc.semaphore("dsem_%d" % len(sems)))
        self.n_sems = len(sems)
        self.max_semval = max(list(semval.values()) + [0])
        block = stack.enter_context(nc.Block())
        engmap = {'pe': block.tensor, 'act': block.scalar, 'dve': block.vector,
                  'pool': block.gpsimd, 'sp': block.sync}

        def run(ename, eobj):
            for idx, (waits, fn, tok, dk) in enumerate(self.ops[ename]):
                for s, v in waits.items():
                    if isinstance(s, tuple):
                        eobj.wait_ge(sems[s], 16 * v)
                    else:
                        eobj.wait_ge(sems[s], semval[(s, v)])
                if fn is None:
                    continue
                ins = fn(eobj)
                if dk is not None:
                    ins.then_inc(sems[tok[0]], 16)
                elif (ename, idx) in semval:
                    ins.then_inc(sems[ename], 1)

        for ename in self.ENG:
            def mk(ename):
                def f(eobj):
                    run(ename, eobj)
                return f
            engmap[ename](mk(ename))


class Region:
    def __init__(self, t, n):
        self.t = t
        self.n = n
        self.off = 0

    def reset(self):
        self.off = 0

    def bf(self, n):
        n2 = (n + 15) // 16 * 16
        assert self.off + n2 <= self.n, (self.off, n2, self.n)
        ap = self.t[:, self.off:self.off + n]
        self.off += n2
        return ap

    def f32(self, n):
        return self.bf(2 * n).bitcast(F32)


def build_program():
    nc = bass.Bass("TRN2", target_bir_lowering=False)

    def dram(name, shape, kind="ExternalInput", dt=F32):
        return nc.dram_tensor(name, list(shape), dt, kind=kind).ap()

    xin = dram("xin", [NPRE + NTOK, D])
    w_in = dram("w_in", [D, DIN])
    w_a = dram("w_a", [512, D])
    w_b = dram("w_b", [D, D])
    w_o = dram("w_o", [D, D])
    w_up = dram("w_up", [D, 4 * D])
    w_dn = dram("w_dn", [4 * D, D])
    cf_d = dram("cf", [128, NCF])
    cb_d = dram("cb", [128, NCB])
    gup_d = dram("gup", [17, 512])
    out_d = dram("out", [NTOK, D], kind="ExternalOutput")
    dbg = {}

    with ExitStack() as st:
        def sb(name, shape, dt):
            return st.enter_context(nc.sbuf_tensor(name, list(shape), dt))

        hTo = sb("hTo", [128, 8 * NTOK], BF16)
        hTh = sb("hTh", [128, 8 * NTOK], BF16)
        ogT = sb("ogT", [128, 8 * NTOK], BF16)
        oaT = sb("oaT", [128, 4 * NTOK], BF16)
        arena = sb("arena", [128, 32768], BF16)
        S = sb("S", [128, 1024], F32)
        Sbf = sb("Sbf", [128, 1024], BF16)
        cf = sb("cfs", [128, NCF], F32)
        cb = sb("cbs", [128, NCB], BF16)
        gup = sb("gups", [17, 512], F32)
        paT = sb("paT", [17, 256], F32)
        sm = sb("smalls", [128, 64], F32)
        banks = [st.enter_context(nc.psum_tensor("bank%d" % i, [128, 512], F32)) for i in range(8)]

        hTo3 = hTo[:, :].rearrange("p (c t) -> p c t", c=8)
        hTh3 = hTh[:, :].rearrange("p (c t) -> p c t", c=8)
        ogT3 = ogT[:, :].rearrange("p (c t) -> p c t", c=8)
        oaT3 = oaT[:, :].rearrange("p (c t) -> p c t", c=4)
        RA = Region(arena, 32768)
        RG = Region(ogT, 8 * NTOK)
        RH = Region(hTh, 8 * NTOK)
        RO = Region(hTo, 8 * NTOK)
        RT = Region(oaT, 4 * NTOK)

        P = Prog(nc)
        B = ['b%d' % i for i in range(8)]

        def MM(out, lhsT, rhs, start=True, stop=True, reads=(), writes=(), skip=False):
            if skip:
                P.add('pe', lambda e: e.matmul(out, lhsT, rhs, start=start, stop=stop, skip_group_check=True),
                      reads, writes)
            else:
                P.add('pe', lambda e: e.matmul(out, lhsT, rhs, start=start, stop=stop), reads, writes)

        def TR(out, in_, reads=(), writes=()):
            ident = cb[:, CB_ID:CB_ID + 128]
            P.add('pe', lambda e: e.transpose(out, in_, ident), reads, writes)

        def ACT(out, in_, func, reads=(), writes=(), scale=None, bias=None, accum_out=None):
            kw = {}
            if scale is not None:
                kw['scale'] = scale
            if bias is not None:
                kw['bias'] = bias
            if accum_out is not None:
                kw['accum_out'] = accum_out
            P.add('act', lambda e: e.activation(out=out, in_=in_, func=func, **kw), reads, writes)

        def TT(eng, out, in0, in1, op, reads=(), writes=()):
            P.add(eng, lambda e: e.tensor_tensor(out=out, in0=in0, in1=in1, op=op), reads, writes)

        def TS(eng, out, in0, s1, op0, reads=(), writes=(), s2=None, op1=None):
            if op1 is None:
                P.add(eng, lambda e: e.tensor_scalar(out=out, in0=in0, scalar1=s1, scalar2=None, op0=op0),
                      reads, writes)
            else:
                P.add(eng, lambda e: e.tensor_scalar(out=out, in0=in0, scalar1=s1, scalar2=s2, op0=op0, op1=op1),
                      reads, writes)

        def STT(out, in0, scalar, in1, op0, op1, reads=(), writes=()):
            P.add('dve', lambda e: e.scalar_tensor_tensor(out=out, in0=in0, scalar=scalar, in1=in1,
                                                          op0=op0, op1=op1), reads, writes)

        def CP(eng, out, in_, reads=(), writes=()):
            if eng == 'act':
                ACT(out, in_, AF.Copy, reads, writes)
            else:
                P.add(eng, lambda e: e.tensor_copy(out=out, in_=in_), reads, writes)

        def MSET(eng, ap, val, writes=()):
            P.add(eng, lambda e: e.memset(ap, val), (), writes)

        def DMA(eng, out, in_, key, reads=(), writes=()):
            P.add(eng, lambda e: e.dma_start(out=out, in_=in_), reads, writes, dma_key=key)

        def wcols(w, c0, n):
            return w[:, c0:c0 + n].rearrange("(c p) n -> p c n", p=128)

        def v3(ap, c):
            return ap.rearrange("p (c n) -> p c n", c=c)

        def dump(name, ap, shape, key_reads=()):
            if not DEBUG:
                return
            d = dram("dbg_" + name, shape, kind="ExternalOutput", dt=ap.dtype)
            dbg[name] = d
            DMA('sp', d, ap, 'dbg_' + name, reads=key_reads)

        eps_ap = cf[:, CF_EPS:CF_EPS + 1]
        one_ap = cf[:, CF_ONE:CF_ONE + 1]

        def rstd_from_sum(out_ap, in_ap, inv_n, key_in, key_out, tmpkey):
            ACT(out_ap, in_ap, AF.Ln, reads=[key_in], writes=[key_out], scale=inv_n, bias=eps_ap)
            ACT(out_ap, out_ap, AF.Exp, reads=[key_out], writes=[key_out], scale=-0.5)

        DMA('sp', cf[:, :], cf_d, 'cf', writes=['cf'])
        DMA('pool', cb[:, :], cb_d, 'cb', writes=['cb'])
        DMA('sp', gup[:, :], gup_d, 'gup', writes=['gup'])
        MSET('dve', paT[:, :], 1.0, writes=['paT0', 'paT1'])
        MSET('dve', S[:, :], 0.0, writes=['S0', 'S1', 'S2', 'S3'])
        MSET('pool', Sbf[:, :], 0.0, writes=['Sbf0', 'Sbf1', 'Sbf2', 'Sbf3'])
        P.barrier()

        RA.reset()
        Wk_tok = RA.bf(8 * 512)
        Wv = RA.bf(8 * 1024)
        Wpa = RA.bf(8 * 16)
        Wk3, Wv3, Wpa3 = v3(Wk_tok, 8), v3(Wv, 8), v3(Wpa, 8)
        xt = [RA.f32(1024) for _ in range(2)]
        xs = [RA.bf(1024) for _ in range(2)]
        junk = RA.bf(1024)
        hTt = [RA.bf(1024) for _ in range(2)]
        la = [RA.f32(512) for _ in range(2)]
        e1 = RA.f32(512)
        ekd = RA.f32(512)
        kd = [RA.bf(512) for _ in range(2)]
        vbf = [RA.bf(1024) for _ in range(2)]

        DMA('pool', Wk3, wcols(w_in, O_GK, 512), 'Wk', writes=['Wk'])
        DMA('pool', Wv3, wcols(w_in, O_GV, 1024), 'Wv', writes=['Wv'])
        DMA('pool', Wpa3, wcols(w_in, O_GA, 16), 'Wpa', writes=['Wpa'])

        g1b = bass.AP(cf, CF_G1, [[NCF, 128], [1, 8], [0, 128]])
        g2b = bass.AP(cf, CF_G2, [[NCF, 128], [1, 8], [0, 128]])
        lst_ap = cf[:, CF_LST:CF_LST + 128]
        uinc_ap = cf[:, CF_UINC:CF_UINC + 128]
        cind_ap = cf[:, CF_CIND:CF_CIND + 2]

        def norm_transpose(src_rows, s, dst3, gb, xkey=None):
            ss = sm[:, s:s + 1]
            rs = sm[:, 2 + s:3 + s]
            if xkey is None:
                DMA('sp', xt[s], src_rows, 'xt%d' % s, writes=['xt%d' % s])
                xap, xk = xt[s], 'xt%d' % s
            else:
                xap, xk = src_rows, xkey
            ACT(junk, xap, AF.Square, reads=[xk], writes=['junk', 'ss%d' % s], accum_out=ss)
            rstd_from_sum(rs, ss, 1.0 / D, 'ss%d' % s, 'rs%d' % s, 'rst%d' % s)
            TS('dve', xs[s], xap, rs, ALU.mult, reads=[xk, 'rs%d' % s], writes=['xs%d' % s])
            pb = banks[0][:, :].bitcast(BF16)
            for c in range(8):
                TR(pb[:, c * 128:(c + 1) * 128], xs[s][:, c * 128:(c + 1) * 128], reads=['xs%d' % s], writes=[B[0]])
            TT('dve', dst3, v3(pb, 8), gb, ALU.mult, reads=[B[0]], writes=['hT%d' % s])

        def gla_stage2(hT3, s, own):
            hk = 'hT%d' % s
            for c in range(8):
                MM(banks[4][0:16, 0:128], Wpa3[:, c, :], hT3[:, c, :], start=(c == 0), stop=(c == 7),
                   reads=[hk, 'Wpa'], writes=[B[4]])
            CP('act', paT[0:16, s * 128:(s + 1) * 128], banks[4][0:16, 0:128], reads=[B[4]], writes=['paT%d' % s])
            for c in range(8):
                MM(banks[1][:, :], hT3[:, c, :], Wk3[:, c, :], start=(c == 0), stop=(c == 7),
                   reads=[hk, 'Wk'], writes=[B[1]])
            if own:
                for c in range(8):
                    MM(banks[6][:, :], hT3[:, c, :], Wq3[:, c, :], start=(c == 0), stop=(c == 7),
                       reads=[hk, 'Wq_g'], writes=[B[6]])
            MM(banks[5][:, :], paT[0:17, s * 128:(s + 1) * 128], gup[0:17, :], reads=['paT%d' % s, 'gup'],
               writes=[B[5]])
            ACT(e1, banks[5][:, :], AF.Exp, reads=[B[5]], writes=['e1'], scale=-1.0)
            ACT(la[s], e1, AF.Ln, reads=['e1'], writes=['la%d' % s], bias=one_ap)
            for half in range(2):
                for c in range(8):
                    MM(banks[2 + half][:, :], hT3[:, c, :], Wv3[:, c, half * 512:(half + 1) * 512],
                       start=(c == 0), stop=(c == 7), reads=[hk, 'Wv'], writes=[B[2 + half]])
            CP('act', vbf[s][:, 0:512], banks[2][:, :], reads=[B[2]], writes=['vbfa%d' % s])
            CP('act', vbf[s][:, 512:1024], banks[3][:, :], reads=[B[3]], writes=['vbfb%d' % s])
            MM(banks[5][:, :], lst_ap, la[s], reads=['la%d' % s], writes=[B[5]])
            for h in range(4):
                MM(banks[4][:, 128 + 2 * h:130 + 2 * h], la[s][:, h * 128:(h + 1) * 128], cind_ap,
                   reads=['la%d' % s], writes=[B[4]])
            ACT(ekd, banks[5][:, :], AF.Exp, reads=[B[5]], writes=['ekd'], scale=-1.0 / 16)
            ebl = sm[:, 8 + 8 * s:16 + 8 * s]
            ACT(ebl, banks[4][:, 128:136], AF.Exp, reads=[B[4]], writes=['ebl%d' % s], scale=-1.0 / 16)
            TT('dve', kd[s], banks[1][:, :], ekd, ALU.mult, reads=[B[1], 'ekd'], writes=['kd%d' % s])
            return ebl

        def gla_stage3(s, own):
            ebl = sm[:, 8 + 8 * s:16 + 8 * s]
            for h in range(4):
                ub = 6 if h < 2 else 7
                MM(banks[ub][:, (h % 2) * 256:(h % 2) * 256 + 256], kd[s][:, h * 128:(h + 1) * 128],
                   vbf[s][:, h * 256:(h + 1) * 256],
                   reads=['kd%d' % s, 'vbfa%d' % s if h < 2 else 'vbfb%d' % s], writes=[B[ub]])
            for h in range(4):
                ub = 6 if h < 2 else 7
                Sh = S[:, h * 256:(h + 1) * 256]
                STT(Sh, Sh, ebl[:, 2 * h:2 * h + 1], banks[ub][:, (h % 2) * 256:(h % 2) * 256 + 256], ALU.mult, ALU.add,
                    reads=['S%d' % h, 'ebl%d' % s, B[ub]], writes=['S%d' % h])
                if own:
                    CP('act', Sbf[:, h * 256:(h + 1) * 256], Sh, reads=['S%d' % h], writes=['Sbf%d' % h])

        def tile_dst(ti):
            if ti >= NT_PRE:
                j = ti - NT_PRE
                return hTo3[:, :, j * 128:(j + 1) * 128]
            if ti >= NT_PRE - NT_OWN:
                j = ti - (NT_PRE - NT_OWN)
                return hTh3[:, :, j * 128:(j + 1) * 128]
            return v3(hTt[ti % 2], 8)

        NT_ALL = NT_PRE + NT_OWN
        norm_transpose(xin[0:128, :], 0, tile_dst(0), g1b)
        for ti in range(NT_PRE):
            s = ti % 2
            if ti + 1 < NT_ALL:
                norm_transpose(xin[(ti + 1) * 128:(ti + 2) * 128, :], (ti + 1) % 2, tile_dst(ti + 1), g1b)
            gla_stage2(tile_dst(ti), s, False)
            if ti >= 1:
                gla_stage3((ti - 1) % 2, False)
        gla_stage3((NT_PRE - 1) % 2, False)
        for ti in range(NT_PRE + 1, NT_ALL):
            norm_transpose(xin[ti * 128:(ti + 1) * 128, :], ti % 2, tile_dst(ti), g1b)
        P.barrier()
        dump("hTo", hTo[:, :], [128, 8 * NTOK])
        dump("hTh", hTh[:, :], [128, 8 * NTOK])
        dump("Spre", S[:, :], [128, 1024])
        for h in range(4):
            CP('act', Sbf[:, h * 256:(h + 1) * 256], S[:, h * 256:(h + 1) * 256])
        P.barrier()

        RA.reset()
        RG.reset()
        Wq_a = [RA.bf(1024) for _ in range(3)]
        Wk_a = [RA.bf(1024) for _ in range(3)]
        QT = [RA.bf(NTOK) for _ in range(3)]
        HALO = [128, 512, 2048]
        KT = [RA.bf(HALO[g] + NTOK) for g in range(3)]
        NBLK = [17, 20, 32]
        Vb = [RA.bf(NBLK[g] * 128) for g in range(3)]
        Wv_a = [RG.bf(1024) for _ in range(3)]
        PT3h = [RG.bf(16 * 256) for _ in range(2)]
        ptb = [RG.bf(512) for _ in range(4)]
        sqb = [RG.bf(512) for _ in range(2)]
        lnb = [RG.f32(512) for _ in range(2)]
        lnd = [RA.f32(512) for _ in range(2)]
        gq_ap = cf[:, CF_GQ:CF_GQ + 1]
        gk_ap = cf[:, CF_GK:CF_GK + 1]
        bones = cb[:, CB_BONES:CB_BONES + 128]
        ones64 = cb[:, CB_ONES:CB_ONES + 64]
        m4 = {0: cb[:, CB_M4:CB_M4 + 512], 1: cb[:, CB_M4H0:CB_M4H0 + 512], 2: cb[:, CB_M4HH:CB_M4HH + 512]}
        cnt = {'qk': 0, 'pt': 0, 'rnd': 0, 'sc': 0, 'vb': 0}

        def qk_tile(wap, hsrc3, t0, n, dst, gain, wkey):
            i = cnt['qk']
            cnt['qk'] += 1
            pbk = (0, 5)[i % 2]
            pk = B[pbk]
            stb = 3 + (i % 2)
            w3 = v3(wap, 8)
            for c in range(8):
                MM(banks[pbk][:, 0:n], w3[:, c, :], hsrc3[:, c, t0:t0 + n], start=(c == 0), stop=(c == 7),
                   reads=[wkey], writes=[pk])
            sq = sqb[i % 2]
            ACT(sq[:, 0:n], banks[pbk][:, 0:n], AF.Square, reads=[pk], writes=['sq%d' % (i % 2)])
            MM(banks[stb][:, 0:n], bones, sq[:, 0:n], reads=['sq%d' % (i % 2)], writes=[B[stb]])
            ln = lnb[i % 2]
            ACT(ln[:, 0:n], banks[stb][:, 0:n], AF.Ln, reads=[B[stb]], writes=['ln%d' % (i % 2)], bias=eps_ap)
            ACT(ln[:, 0:n], ln[:, 0:n], AF.Exp, reads=['ln%d' % (i % 2)], writes=['ln%d' % (i % 2)], scale=-0.5)
            STT(dst, banks[pbk][:, 0:n], gain, ln[:, 0:n], ALU.mult, ALU.mult,
                reads=[pk, 'ln%d' % (i % 2)], writes=['qkdst'])

        for p in range(4):
            for g in range(3):
                hc = (g * 8 + 2 * p) * 64
                DMA('pool', v3(Wq_a[g], 8), wcols(w_in, O_AQ + hc, 128), 'Wq_a%d' % g, writes=['Wq_a%d' % g])
                DMA('pool', v3(Wk_a[g], 8), wcols(w_in, O_AK + hc, 128), 'Wk_a%d' % g, writes=['Wk_a%d' % g])
                DMA('pool', v3(Wv_a[g], 8), wcols(w_in, O_AV + hc, 128), 'Wv_a%d' % g, writes=['Wv_a%d' % g])
            for g in range(3):
                d = DIL[g]
                for u in range(4):
                    qk_tile(Wq_a[g], hTo3, u * 512, 512, QT[g][:, u * 512:(u + 1) * 512], gq_ap, 'Wq_a%d' % g)
                hl = HALO[g]
                nsp = max(1, hl // 512)
                w = hl // nsp
                for u in range(nsp):
                    qk_tile(Wk_a[g], hTh3, NTOK - hl + u * w, w, KT[g][:, u * w:(u + 1) * w], gk_ap, 'Wk_a%d' % g)
                for u in range(4):
                    qk_tile(Wk_a[g], hTo3, u * 512, 512, KT[g][:, hl + u * 512:hl + (u + 1) * 512], gk_ap,
                            'Wk_a%d' % g)
                seg = 128 * d
                wv3 = v3(Wv_a[g], 8)
                for b0 in range(0, NBLK[g], 4):
                    nb = min(4, NBLK[g] - b0)
                    vi = cnt['vb']
                    cnt['vb'] += 1
                    vbank = 3 + (vi % 2)
                    for bi in range(nb):
                        blk = b0 + bi
                        n_, r_ = blk // d - 1, blk % d
                        t0 = n_ * seg + r_
                        if t0 < 0:
                            src3, tt0 = hTh3, NTOK + t0
                        else:
                            src3, tt0 = hTo3, t0
                        for c in range(8):
                            MM(banks[vbank][:, bi * 128:(bi + 1) * 128],
                               src3[:, c, tt0:tt0 + 127 * d + 1:d], wv3[:, c, :],
                               start=(c == 0), stop=(c == 7), reads=['Wv_a%d' % g], writes=[B[vbank]])
                    CP('act' if vi % 2 == 0 else 'dve', Vb[g][:, b0 * 128:(b0 + nb) * 128],
                       banks[vbank][:, 0:nb * 128], reads=[B[vbank]], writes=['Vb'])
            nbk, dbk = (1, 2)
            numb, denb = banks[nbk], banks[dbk]
            SB_ROT = (6, 7, 0, 5)
            items = []

            def mk_score(hd, g, blks, mkind, rec, fixed=None):
                def f():
                    d = DIL[g]
                    seg = 128 * d
                    hl = HALO[g]
                    po = hd * 64
                    si = cnt['sc']
                    cnt['sc'] += 1
                    sbank = SB_ROT[si % 4]
                    if fixed is None:
                        i = cnt['pt']
                        cnt['pt'] += 1
                        dst_pt, dst_key = ptb[i % 4], 'pt%d' % (i % 4)
                    else:
                        dst_pt, dst_key = fixed
                    rec['pt'], rec['key'] = dst_pt, dst_key
                    for bi, (n_, r_) in enumerate(blks):
                        t0 = n_ * seg + r_
                        qap = QT[g][po:po + 64, t0:t0 + 127 * d + 1:d]
                        kprev = KT[g][po:po + 64, hl + t0 - seg:hl + t0 - seg + 127 * d + 1:d]
                        kcur = KT[g][po:po + 64, hl + t0:hl + t0 + 127 * d + 1:d]
                        MM(banks[sbank][:, bi * 256:bi * 256 + 128], kprev, qap, reads=['qkdst'], writes=[B[sbank]])
                        MM(banks[sbank][:, bi * 256 + 128:bi * 256 + 256], kcur, qap, reads=['qkdst'],
                           writes=[B[sbank]])
                    ACT(dst_pt[:, 0:512], banks[sbank][:, 0:512], AF.Exp, reads=[B[sbank]], writes=[dst_key],
                        scale=0.125)
                    TT('dve', dst_pt[:, 0:512], dst_pt[:, 0:512], m4[mkind][:, 0:512], ALU.mult,
                       reads=[dst_key], writes=[dst_key])
                return f

            def mk_pv(hd, g, R, rec, half, first):
                def f():
                    po = hd * 64
                    pt, key = rec['pt'], rec['key']
                    pt3 = v3(pt, 2)

                    def num(kb, mov, cols):
                        stf = first['n%d' % hd]
                        first['n%d' % hd] = False
                        MM(numb[po:po + 64, cols], Vb[g][:, kb * 128 + hd * 64:kb * 128 + hd * 64 + 64], mov,
                           start=stf, stop=False, reads=['Vb', key], writes=[B[nbk] + str(hd)], skip=True)

                    def den(out_ap, mov):
                        stf = first['d%d' % hd]
                        first['d%d' % hd] = False
                        MM(out_ap, ones64, mov, start=stf, stop=False, reads=[key], writes=[B[dbk] + str(hd)],
                           skip=True)
                    if g == 0:
                        n0 = 4 * R + 2 * half
                        for bi in range(2):
                            n_ = n0 + bi
                            cols = slice((n_ - 4 * R) * 128, (n_ - 4 * R) * 128 + 128)
                            num(n_, pt[:, bi * 256:bi * 256 + 128], cols)
                            num(n_ + 1, pt[:, bi * 256 + 128:bi * 256 + 256], cols)
                        for bi in range(2):
                            n_ = n0 + bi
                            cols = slice((n_ - 4 * R) * 128, (n_ - 4 * R) * 128 + 128)
                            den(denb[po:po + 64, cols], pt[:, bi * 256:bi * 256 + 128])
                            den(denb[po:po + 64, cols], pt[:, bi * 256 + 128:bi * 256 + 256])
                    else:
                        for bi in range(2):
                            r_ = 2 * half + bi
                            cols = slice(r_, 512, 4)
                            blk = (R + 1) * 4 + r_
                            num(blk - 4, pt[:, bi * 256:bi * 256 + 128], cols)
                            num(blk, pt[:, bi * 256 + 128:bi * 256 + 256], cols)
                        for bi in range(2):
                            r_ = 2 * half + bi
                            den(denb[po:po + 64, r_:512:4], pt[:, bi * 256:bi * 256 + 128])
                            den(denb[po:po + 64, r_:512:4], pt[:, bi * 256 + 128:bi * 256 + 256])
                return f

            def mk_pv3(hd, R, first):
                def f():
                    po = hd * 64
                    keys = ['PT3_%d_%d' % (hd, rp) for rp in range(8)]
                    for r_ in range(16):
                        cols = slice(r_, 512, 16)
                        base = r_ * 256
                        for part, kb in ((0, r_), (1, 16 + r_)):
                            stf = first['n%d' % hd]
                            first['n%d' % hd] = False
                            MM(numb[po:po + 64, cols], Vb[2][:, kb * 128 + hd * 64:kb * 128 + hd * 64 + 64],
                               PT3h[hd][:, base + part * 128 + 32 * R:base + part * 128 + 32 * R + 32],
                               start=stf, stop=False, reads=['Vb', 'PT3_%d_%d' % (hd, r_ // 2)],
                               writes=[B[nbk] + str(hd)], skip=True)
                    p16 = v3(PT3h[hd], 16)
                    for r_ in range(16):
                        cols = slice(r_, 512, 16)
                        base = r_ * 256
                        for part in range(2):
                            stf = first['d%d' % hd]
                            first['d%d' % hd] = False
                            MM(denb[po:po + 64, cols], ones64,
                               PT3h[hd][:, base + part * 128 + 32 * R:base + part * 128 + 32 * R + 32],
                               start=stf, stop=False, reads=['PT3_%d_%d' % (hd, r_ // 2)],
                               writes=[B[dbk] + str(hd)], skip=True)
                return f

            def mk_norm(p, R):
                def f():
                    rn = cnt['rnd']
                    cnt['rnd'] += 1
                    nk = [B[nbk] + '0', B[nbk] + '1']
                    dk_ = [B[dbk] + '0', B[dbk] + '1']
                    l = lnd[rn % 2]
                    ACT(l, denb[:, :], AF.Ln, reads=dk_, writes=['lnd%d' % (rn % 2)])
                    ACT(l, l, AF.Exp, reads=['lnd%d' % (rn % 2)], writes=['lnd%d' % (rn % 2)], scale=-1.0)
                    TT('dve', oaT3[:, p, R * 512:(R + 1) * 512], numb[:, :], l, ALU.mult,
                       reads=nk + ['lnd%d' % (rn % 2)], writes=['oaT'])
                return f

            for R in range(4):
                first = {'n0': True, 'n1': True, 'd0': True, 'd1': True}
                for hd in range(2):
                    if R == 0:
                        for rp in range(8):
                            items.append((mk_score(hd, 2, [(0, 2 * rp), (0, 2 * rp + 1)], 2, {},
                                                   fixed=(PT3h[hd][:, rp * 512:(rp + 1) * 512],
                                                          'PT3_%d_%d' % (hd, rp))), None))
                    for half in range(2):
                        n0 = 4 * R + 2 * half
                        rec = {}
                        items.append((mk_score(hd, 0, [(n0, 0), (n0 + 1, 0)], 1 if n0 == 0 else 0, rec),
                                      mk_pv(hd, 0, R, rec, half, first)))
                    for half in range(2):
                        rec = {}
                        items.append((mk_score(hd, 1, [(R, 2 * half), (R, 2 * half + 1)], 2 if R == 0 else 0, rec),
                                      mk_pv(hd, 1, R, rec, half, first)))
                    items.append((None, mk_pv3(hd, R, first)))
                items.append((None, mk_norm(p, R)))
            LOOK = 2
            nxt = 0
            for i, (sf, pf) in enumerate(items):
                while nxt <= min(i + LOOK, len(items) - 1):
                    if items[nxt][0] is not None:
                        items[nxt][0]()
                    nxt += 1
                if pf is not None:
                    pf()
            P.barrier()
        dump("oaT", oaT[:, :], [128, 4 * NTOK])
        P.barrier()

        RA.reset()
        RH.reset()
        Wk_tok = RA.bf(8 * 512)
        Wv = RA.bf(8 * 1024)
        Wpa = RA.bf(8 * 16)
        Wk3, Wv3, Wpa3 = v3(Wk_tok, 8), v3(Wv, 8), v3(Wpa, 8)
        la = [RA.f32(512) for _ in range(2)]
        e1 = RA.f32(512)
        ekd = RA.f32(512)
        kd = [RA.bf(512) for _ in range(2)]
        vbf = [RA.bf(1024) for _ in range(2)]
        junk = RA.bf(1024)
        Wq_g = RH.bf(8 * 512)
        Wr_g = RH.bf(8 * 1024)
        Wq3, Wr3 = v3(Wq_g, 8), v3(Wr_g, 8)
        sr = RA.f32(1024)
        eq = RA.f32(512)
        ek = RA.f32(512)
        qe_tok = RA.bf(512)
        ke_tok = RA.bf(512)
        qeT = RA.bf(512)
        keT = RA.bf(512)
        AT = RA.bf(512)
        t1 = RA.f32(1024)
        og = RA.bf(1024)
        gm_b = bass.AP(cb, CB_GM, [[NCB, 128], [0, 4], [1, 128]])
        gout_ap = cf[:, CF_GOUT:CF_GOUT + 256]
        DMA('pool', Wk3, wcols(w_in, O_GK, 512), 'Wk', writes=['Wk'])
        DMA('pool', Wv3, wcols(w_in, O_GV, 1024), 'Wv', writes=['Wv'])
        DMA('pool', Wpa3, wcols(w_in, O_GA, 16), 'Wpa', writes=['Wpa'])
        DMA('pool', Wq3, wcols(w_in, O_GQ, 512), 'Wq_g', writes=['Wq_g'])
        DMA('pool', Wr3, wcols(w_in, O_GR, 1024), 'Wr_g', writes=['Wr_g'])

        for j in range(NT_OWN):
            s = j % 2
            hT3 = hTo3[:, :, j * 128:(j + 1) * 128]
            ebl = gla_stage2(hT3, s, True)
            MM(banks[5][:, :], uinc_ap, la[s], reads=['la%d' % s], writes=[B[5]])
            ACT(eq, banks[5][:, :], AF.Exp, reads=[B[5]], writes=['eq'], scale=-1.0 / 16)
            ACT(ek, banks[5][:, :], AF.Exp, reads=[B[5]], writes=['ek'], scale=1.0 / 16)
            STT(qe_tok, banks[6][:, :], float(128 ** -0.5), eq, ALU.mult, ALU.mult, reads=[B[6], 'eq'],
                writes=['qe_tok'])
            TT('dve', ke_tok, banks[1][:, :], ek, ALU.mult, reads=[B[1], 'ek'], writes=['ke_tok'])
            for half in range(2):
                for c in range(8):
                    MM(banks[2 + half][:, :], hT3[:, c, :], Wr3[:, c, half * 512:(half + 1) * 512],
                       start=(c == 0), stop=(c == 7), reads=['Wr_g'], writes=[B[2 + half]])
                ACT(sr[:, half * 512:(half + 1) * 512], banks[2 + half][:, :], AF.Silu,
                    reads=[B[2 + half]], writes=['sr%d' % half])
            pb = banks[0][:, :].bitcast(BF16)
            for h in range(4):
                TR(pb[:, h * 128:(h + 1) * 128], qe_tok[:, h * 128:(h + 1) * 128], reads=['qe_tok'], writes=[B[0]])
            for h in range(4):
                TR(pb[:, 512 + h * 128:512 + (h + 1) * 128], ke_tok[:, h * 128:(h + 1) * 128], reads=['ke_tok'],
                   writes=[B[0]])
            CP('act', qeT, pb[:, 0:512], reads=[B[0]], writes=['qeT'])
            CP('dve', keT, pb[:, 512:1024], reads=[B[0]], writes=['keT'])
            for h in range(4):
                MM(banks[7][:, h * 128:(h + 1) * 128], keT[:, h * 128:(h + 1) * 128], qeT[:, h * 128:(h + 1) * 128],
                   reads=['keT', 'qeT'], writes=[B[7]])
            TT('dve', v3(AT, 4), v3(banks[7][:, :], 4), gm_b, ALU.mult, reads=[B[7]], writes=['AT'])
            ob = {0: 2, 1: 2, 2: 3, 3: 3}
            for h in range(4):
                MM(banks[ob[h]][:, (h % 2) * 256:(h % 2) * 256 + 256], AT[:, h * 128:(h + 1) * 128],
                   vbf[s][:, h * 256:(h + 1) * 256], start=(h % 2 == 0), stop=False,
                   reads=['AT', 'vbfa%d' % s if h < 2 else 'vbfb%d' % s], writes=[B[ob[h]]], skip=True)
            for h in range(4):
                MM(banks[ob[h]][:, (h % 2) * 256:(h % 2) * 256 + 256], qeT[:, h * 128:(h + 1) * 128],
                   Sbf[:, h * 256:(h + 1) * 256], start=False, stop=True,
                   reads=['qeT', 'Sbf%d' % h], writes=[B[ob[h]]], skip=True)
            gla_stage3(s, True)
            ssh = sm[:, 32:36]
            rsh = sm[:, 36:40]
            for h in range(4):
                ACT(junk[:, h * 256:(h + 1) * 256], banks[ob[h]][:, (h % 2) * 256:(h % 2) * 256 + 256], AF.Square,
                    reads=[B[ob[h]]], writes=['junk%d' % h, 'ssh%d' % h], accum_out=ssh[:, h:h + 1])
            ACT(rsh, ssh, AF.Ln, reads=['ssh0', 'ssh1', 'ssh2', 'ssh3'], writes=['rsh'], scale=1.0 / 256, bias=eps_ap)
            ACT(rsh, rsh, AF.Exp, reads=['rsh'], writes=['rsh'], scale=-0.5)
            for h in range(4):
                STT(t1[:, h * 256:(h + 1) * 256], banks[ob[h]][:, (h % 2) * 256:(h % 2) * 256 + 256],
                    rsh[:, h:h + 1], gout_ap, ALU.mult, ALU.mult, reads=[B[ob[h]], 'rsh'],
                    writes=['t1_%d' % h])
            TT('pool', og, t1, sr, ALU.mult, reads=['t1_0', 't1_1', 't1_2', 't1_3', 'sr0', 'sr1'], writes=['og'])
            for c in range(8):
                TR(pb[:, c * 128:(c + 1) * 128], og[:, c * 128:(c + 1) * 128], reads=['og'], writes=[B[0]])
            CP('act', ogT3[:, :, j * 128:(j + 1) * 128], v3(pb, 8), reads=[B[0]], writes=['ogT'])
        P.barrier()
        dump("ogT", ogT[:, :], [128, 8 * NTOK])
        P.barrier()

        RA.reset()
        mixT3 = hTh3
        WA = [RA.bf(4 * 128) for _ in range(2)]
        WB = [RA.bf(8 * 128) for _ in range(2)]
        WGA = [RA.bf(8 * 128) for _ in range(2)]
        WGB = [RA.bf(8 * 128) for _ in range(2)]
        sgA = [RA.f32(512) for _ in range(2)]
        sgB = [RA.f32(512) for _ in range(2)]
        tA = [RA.f32(512) for _ in range(2)]
        tB = [RA.f32(512) for _ in range(2)]
        it = 0
        for jc in range(8):
            ws = jc % 2
            wa3, wb3, wga3, wgb3 = v3(WA[ws], 4), v3(WB[ws], 8), v3(WGA[ws], 8), v3(WGB[ws], 8)
            DMA('pool', wa3, wcols(w_a, jc * 128, 128), 'WA%d' % ws, writes=['WA%d' % ws])
            DMA('pool', wb3, wcols(w_b, jc * 128, 128), 'WB%d' % ws, writes=['WB%d' % ws])
            DMA('pool', wga3, wcols(w_in, O_GATE + jc * 128, 128), 'WGA%d' % ws, writes=['WGA%d' % ws])
            DMA('pool', wgb3, wcols(w_in, O_GATE + D + jc * 128, 128), 'WGB%d' % ws, writes=['WGB%d' % ws])
            for R in range(4):
                k = it % 2
                it += 1
                ts_ = slice(R * 512, (R + 1) * 512)
                bA, bB, bGA, bGB = (0 + 4 * k, 1 + 4 * k, 2 + 4 * k, 3 + 4 * k)
                for c in range(4):
                    MM(banks[bA][:, :], wa3[:, c, :], oaT3[:, c, ts_], start=(c == 0), stop=(c == 3),
                       reads=['WA%d' % ws], writes=[B[bA]])
                for c in range(8):
                    MM(banks[bB][:, :], wb3[:, c, :], ogT3[:, c, ts_], start=(c == 0), stop=(c == 7),
                       reads=['WB%d' % ws], writes=[B[bB]])
                for c in range(8):
                    MM(banks[bGA][:, :], wga3[:, c, :], hTo3[:, c, ts_], start=(c == 0), stop=(c == 7),
                       reads=['WGA%d' % ws], writes=[B[bGA]])
                for c in range(8):
                    MM(banks[bGB][:, :], wgb3[:, c, :], hTo3[:, c, ts_], start=(c == 0), stop=(c == 7),
                       reads=['WGB%d' % ws], writes=[B[bGB]])
                ACT(sgA[k], banks[bGA][:, :], AF.Sigmoid, reads=[B[bGA]], writes=['sgA%d' % k],
                    bias=cf[:, CF_GB + jc:CF_GB + jc + 1])
                ACT(sgB[k], banks[bGB][:, :], AF.Sigmoid, reads=[B[bGB]], writes=['sgB%d' % k],
                    bias=cf[:, CF_GB + 8 + jc:CF_GB + 8 + jc + 1])
                TT('dve', tA[k], banks[bA][:, :], sgA[k], ALU.mult, reads=[B[bA], 'sgA%d' % k], writes=['tA%d' % k])
                TT('dve', tB[k], banks[bB][:, :], sgB[k], ALU.mult, reads=[B[bB], 'sgB%d' % k], writes=['tB%d' % k])
                TT('pool', mixT3[:, jc, ts_], tA[k], tB[k], ALU.add, reads=['tA%d' % k, 'tB%d' % k], writes=['mixT'])
        P.barrier()
        dump("mixT", hTh[:, :], [128, 8 * NTOK])
        P.barrier()

        RA.reset()
        RG.reset()
        RT.reset()
        x1 = RA.f32(NT_OWN * 1024)
        Wo = RT.bf(8 * 1024)
        Wo3 = v3(Wo, 8)
        xt = [RG.f32(1024) for _ in range(2)]
        xs = [RG.bf(1024) for _ in range(2)]
        junk = RG.bf(1024)
        DMA('pool', Wo3, wcols(w_o, 0, 1024), 'Wo', writes=['Wo'])
        for j in range(NT_OWN):
            s = j % 2
            DMA('sp', xt[s], xin[NPRE + j * 128:NPRE + (j + 1) * 128, :], 'xt%d' % s, writes=['xt%d' % s])
            for half in range(2):
                bk = 1 + half + 2 * s
                for c in range(8):
                    MM(banks[bk][:, :], mixT3[:, c, j * 128:(j + 1) * 128], Wo3[:, c, half * 512:(half + 1) * 512],
                       start=(c == 0), stop=(c == 7), reads=['Wo'], writes=[B[bk]])
                TT('dve', x1[:, j * 1024 + half * 512:j * 1024 + (half + 1) * 512], banks[bk][:, :],
                   xt[s][:, half * 512:(half + 1) * 512], ALU.add, reads=[B[bk], 'xt%d' % s], writes=['x1t%d' % s])
            norm_transpose(x1[:, j * 1024:(j + 1) * 1024], s, hTo3[:, :, j * 128:(j + 1) * 128], g2b,
                           xkey='x1t%d' % s)
        P.barrier()
        dump("x1", x1, [128, NT_OWN * 1024])
        P.barrier()

        RG.reset()
        RH.reset()
        RT.reset()
        h2T3 = hTo3
        aT = [RG.bf(4 * NTOK) for _ in range(2)]
        Wup = [RH.bf(8 * 512) for _ in range(2)]
        Wdn = [RH.bf(4 * 1024) for _ in range(2)]
        rst = [RT.f32(512) for _ in range(2)]
        ui = 0
        di = 0
        for f in range(8):
            ws = f % 2
            wu3, wd3, a3 = v3(Wup[ws], 8), v3(Wdn[ws], 4), v3(aT[ws], 4)
            DMA('pool', wu3, wcols(w_up, f * 512, 512), 'Wup%d' % ws, writes=['Wup%d' % ws])
            DMA('pool', wd3, w_dn[f * 512:(f + 1) * 512, :].rearrange("(c p) n -> p c n", p=128), 'Wdn%d' % ws,
                writes=['Wdn%d' % ws])
            for q in range(4):
                for u in range(4):
                    bk = ui % 4
                    k = ui % 2
                    ui += 1
                    for c in range(8):
                        MM(banks[bk][:, :], wu3[:, c, q * 128:(q + 1) * 128], h2T3[:, c, u * 512:(u + 1) * 512],
                           start=(c == 0), stop=(c == 7), reads=['Wup%d' % ws], writes=[B[bk]])
                    ACT(rst[k], banks[bk][:, :], AF.Relu, reads=[B[bk]], writes=['rst%d' % k])
                    TT('pool', a3[:, q, u * 512:(u + 1) * 512], rst[k], rst[k], ALU.mult, reads=['rst%d' % k],
                       writes=['aT%d' % ws])
            for j in range(NT_OWN):
                for half in range(2):
                    bk = 4 + di % 4
                    di += 1
                    for q in range(4):
                        MM(banks[bk][:, :], a3[:, q, j * 128:(j + 1) * 128], wd3[:, q, half * 512:(half + 1) * 512],
                           start=(q == 0), stop=(q == 3), reads=['aT%d' % ws, 'Wdn%d' % ws], writes=[B[bk]])
                    xsl = x1[:, j * 1024 + half * 512:j * 1024 + (half + 1) * 512]
                    TT('dve', xsl, xsl, banks[bk][:, :], ALU.add, reads=[B[bk], 'x1_%d_%d' % (j, half)],
                       writes=['x1_%d_%d' % (j, half)])
                if f == 7:
                    DMA('sp', out_d[j * 128:(j + 1) * 128, :], x1[:, j * 1024:(j + 1) * 1024], 'out%d' % (j % 4),
                        reads=['x1_%d_0' % j, 'x1_%d_1' % j])
        P.emit(st)
        info = {"n_sems": P.n_sems, "max_semval": P.max_semval,
                "n_ops": {e: len(P.ops[e]) for e in P.ENG}}
    return nc, dbg, info


def _consts(inputs, flag):
    i = np.arange(128)
    same = np.ones((128, 128), bool)
    uinc = (same & (i[:, None] <= i[None, :])).astype(np.float32)
    lst = (same & (i[:, None] > i[None, :])).astype(np.float32)
    cf = np.zeros((128, NCF), np.float32)
    cf[:, CF_UINC:CF_UINC + 128] = uinc
    cf[:, CF_LST:CF_LST + 128] = lst
    cf[:, CF_CIND:CF_CIND + 2] = 1.0
    cf[:, CF_G1:CF_G1 + 8] = inputs['norm1_g'].reshape(8, 128).T
    cf[:, CF_G2:CF_G2 + 8] = inputs['norm2_g'].reshape(8, 128).T
    cf[:, CF_GB:CF_GB + 16] = inputs['branch_gate_bias'].reshape(16, 128).T
    cf[:, CF_GQ] = np.tile(inputs['attn_q_norm_g'].reshape(64), 2)
    cf[:, CF_GK] = np.tile(inputs['attn_k_norm_g'].reshape(64), 2)
    cf[:, CF_GOUT:CF_GOUT + 256] = np.broadcast_to(inputs['gla_out_norm_g'].reshape(1, 256), (128, 256))
    cf[:, CF_EPS] = EPS
    cf[:, CF_ONE] = 1.0
    cb = np.zeros((128, NCB), np.float32)
    cb[:, CB_ID:CB_ID + 128] = np.eye(128, dtype=np.float32)
    prev = (i[:, None] >= i[None, :]).astype(np.float32)
    cur = (i[:, None] <= i[None, :]).astype(np.float32)
    cb[:, CB_M4:CB_M4 + 512] = np.concatenate([prev, cur, prev, cur], 1)
    cb[:, CB_M4H0:CB_M4H0 + 512] = np.concatenate([prev * flag, cur, prev, cur], 1)
    cb[:, CB_M4HH:CB_M4HH + 512] = np.concatenate([prev * flag, cur, prev * flag, cur], 1)
    cb[:, CB_GM:CB_GM + 128] = uinc
    cb[:64, CB_BONES:CB_BONES + 64] = 1.0 / 64
    cb[64:, CB_BONES + 64:CB_BONES + 128] = 1.0 / 64
    cb[:, CB_ONES:CB_ONES + 64] = 1.0
    return cf, cb


_CACHE = {}


def kernel(**inputs):
    inputs = {k: np.asarray(v, dtype=np.float32) for k, v in inputs.items()}
    x = inputs['x']
    if 'nc' not in _CACHE:
        _CACHE['nc'] = build_program()
    nc, dbg, info = _CACHE['nc']
    w_in = np.ascontiguousarray(inputs['w_in'].reshape(D, DIN))
    shared = {
        "w_in": w_in,
        "w_a": np.ascontiguousarray(inputs['w_attn_branch'].reshape(512, D)),
        "w_b": np.ascontiguousarray(inputs['w_gla_branch'].reshape(D, D)),
        "w_o": np.ascontiguousarray(inputs['w_out'].reshape(D, D)),
        "w_up": np.ascontiguousarray(inputs['w_ff_up'].reshape(D, 4 * D)),
        "w_dn": np.ascontiguousarray(inputs['w_ff_down'].reshape(4 * D, D)),
        "gup": np.ascontiguousarray(np.concatenate([inputs['gla_gate_up'].reshape(16, 512),
                                                    inputs['gla_gate_bias'].reshape(1, 512)], 0)),
    }
    in_maps = []
    for c in range(8):
        b, ch = c // 4, c % 4
        xi = np.zeros((NPRE + NTOK, D), np.float32)
        npre = ch * NTOK
        if npre:
            xi[NPRE - npre:NPRE] = x[b, 0:npre]
        xi[NPRE:] = x[b, ch * NTOK:(ch + 1) * NTOK]
        cf, cb = _consts(inputs, 0.0 if ch == 0 else 1.0)
        m = dict(shared)
        m.update({"xin": xi, "cf": cf, "cb": cb})
        in_maps.append(m)
    res = run_bass_kernel_spmd(nc, in_maps, core_ids=list(range(8)))
    out = np.zeros((2, SEQ, D), np.float32)
    for c in range(8):
        b, ch = c // 4, c % 4
        out[b, ch * NTOK:(ch + 1) * NTOK] = res.results[c]["out"]
    if DEBUG:
        _CACHE['last'] = (res, dbg)
    return out
```

```python
import os
import bisect
import numpy as np
import concourse.bass as bass
import concourse.mybir as mybir
from concourse.bass_utils import run_bass_kernel_spmd
from contextlib import ExitStack

F32 = mybir.dt.float32
BF16 = mybir.dt.bfloat16
AF = mybir.ActivationFunctionType
ALU = mybir.AluOpType

D = 1024
SEQ = 8192
NTOK = 2048
NPRE = 6144
NT_PRE = NPRE // 128
NT_OWN = NTOK // 128
DIN = 9744
EPS = 1e-6
O_AQ, O_AK, O_AV = 0, 1536, 3072
O_GQ, O_GK, O_GV, O_GR, O_GA, O_GATE = 4608, 5120, 5632, 6656, 7680, 7696
DIL = (1, 4, 16)

CF_UINC, CF_LST, CF_CIND, CF_G1, CF_G2, CF_GB, CF_GQ, CF_GK, CF_GOUT, CF_EPS, CF_ONE = \
    0, 128, 256, 258, 266, 274, 290, 291, 292, 548, 549
NCF = 552
CB_ID, CB_M4, CB_M4H0, CB_M4HH, CB_GM, CB_BONES, CB_ONES = 0, 128, 640, 1152, 1664, 1792, 1920
NCB = 1984

DEBUG = bool(os.environ.get("KDEBUG"))


class Prog:
    ENG = ('pe', 'act', 'dve', 'pool', 'sp')

    def __init__(self, nc):
        self.nc = nc
        self.ops = {e: [] for e in self.ENG}
        self.lastw = {}
        self.readers = {}
        self.seen = {e: {} for e in self.ENG}
        self.dma_cnt = {}

    def add(self, eng, fn, reads=(), writes=(), dma_key=None):
        bank_r = [k for k in reads if len(k) >= 2 and k[0] == 'b' and k[1].isdigit()]
        if bank_r:
            reads = [k for k in reads if k not in bank_r]
            writes = list(writes) + bank_r
        deps = set()
        for k in reads:
            t = self.lastw.get(k)
            if t is not None:
                deps.add(t)
        for k in writes:
            t = self.lastw.get(k)
            if t is not None:
                deps.add(t)
            for t in self.readers.get(k, ()):
                deps.add(t)
        if dma_key is None:
            tok = (eng, len(self.ops[eng]))
        else:
            self.dma_cnt[dma_key] = self.dma_cnt.get(dma_key, 0) + 1
            tok = (('dma', dma_key), self.dma_cnt[dma_key])
        waits = {}
        for (s, v) in deps:
            if s == eng and eng == 'pe':
                continue
            if self.seen[eng].get(s, -1) >= v:
                continue
            waits[s] = max(waits.get(s, -1), v)
        for s, v in waits.items():
            self.seen[eng][s] = v
        self.ops[eng].append([waits, fn, tok, dma_key])
        for k in writes:
            self.lastw[k] = tok
            self.readers[k] = []
        for k in reads:
            self.readers.setdefault(k, []).append(tok)
        return tok

    def barrier(self):
        last = {}
        for e in self.ENG:
            n = [i for i, o in enumerate(self.ops[e]) if o[3] is None and o[1] is not None]
            if n:
                last[e] = n[-1]
        dm = dict(self.dma_cnt)
        for e in self.ENG:
            waits = {}
            for s, v in last.items():
                if s != e and self.seen[e].get(s, -1) < v:
                    waits[s] = v
            for dk, c in dm.items():
                s = ('dma', dk)
                if self.seen[e].get(s, -1) < c:
                    waits[s] = c
            for s, v in waits.items():
                self.seen[e][s] = v
            self.ops[e].append([waits, None, None, None])
        self.lastw = {}
        self.readers = {}

    def emit(self, stack, final_eng='sp'):
        nc = self.nc
        self.barrier()
        needed = {e: set() for e in self.ENG}
        for e in self.ENG:
            for waits, fn, tok, dk in self.ops[e]:
                for s, v in waits.items():
                    if not isinstance(s, tuple):
                        needed[s].add(v)
        semval = {}
        for e in self.ENG:
            for c, i in enumerate(sorted(needed[e])):
                semval[(e, i)] = c + 1
        sems = {}
        for e in self.ENG:
            sems[e] = stack.enter_context(nc.semaphore("sem_" + e))
        for dk in self.dma_cnt:
            sems[('dma', dk)] = stack.enter_context(nc.semaphore("dsem_%d" % len(sems)))
        self.n_sems = len(sems)
        self.max_semval = max(list(semval.values()) + [0])
        block = stack.enter_context(nc.Block())
        engmap = {'pe': block.tensor, 'act': block.scalar, 'dve': block.vector,
                  'pool': block.gpsimd, 'sp': block.sync}

        def run(ename, eobj):
            for idx, (waits, fn, tok, dk) in enumerate(self.ops[ename]):
                for s, v in waits.items():
                    if isinstance(s, tuple):
                        eobj.wait_ge(sems[s], 16 * v)
                    else:
                        eobj.wait_ge(sems[s], semval[(s, v)])
                if fn is None:
                    continue
                ins = fn(eobj)
                if dk is not None:
                    ins.then_inc(sems[tok[0]], 16)
                elif (ename, idx) in semval:
                    ins.then_inc(sems[ename], 1)

        for ename in self.ENG:
            def mk(ename):
                def f(eobj):
                    run(ename, eobj)
                return f
            engmap[ename](mk(ename))


class Region:
    def __init__(self, t, n):
        self.t = t
        self.n = n
        self.off = 0

    def reset(self):
        self.off = 0

    def bf(self, n):
        n2 = (n + 15) // 16 * 16
        assert self.off + n2 <= self.n, (self.off, n2, self.n)
        ap = self.t[:, self.off:self.off + n]
        self.off += n2
        return ap

    def f32(self, n):
        return self.bf(2 * n).bitcast(F32)


def build_program():
    nc = bass.Bass("TRN2", target_bir_lowering=False)

    def dram(name, shape, kind="ExternalInput", dt=F32):
        return nc.dram_tensor(name, list(shape), dt, kind=kind).ap()

    xin = dram("xin", [NPRE + NTOK, D])
    w_in = dram("w_in", [D, DIN])
    w_a = dram("w_a", [512, D])
    w_b = dram("w_b", [D, D])
    w_o = dram("w_o", [D, D])
    w_up = dram("w_up", [D, 4 * D])
    w_dn = dram("w_dn", [4 * D, D])
    cf_d = dram("cf", [128, NCF])
    cb_d = dram("cb", [128, NCB])
    gup_d = dram("gup", [17, 512])
    out_d = dram("out", [NTOK, D], kind="ExternalOutput")
    dbg = {}

    with ExitStack() as st:
        def sb(name, shape, dt):
            return st.enter_context(nc.sbuf_tensor(name, list(shape), dt))

        hTo = sb("hTo", [128, 8 * NTOK], BF16)
        hTh = sb("hTh", [128, 8 * NTOK], BF16)
        ogT = sb("ogT", [128, 8 * NTOK], BF16)
        oaT = sb("oaT", [128, 4 * NTOK], BF16)
        arena = sb("arena", [128, 32768], BF16)
        S = sb("S", [128, 1024], F32)
        Sbf = sb("Sbf", [128, 1024], BF16)
        cf = sb("cfs", [128, NCF], F32)
        cb = sb("cbs", [128, NCB], BF16)
        gup = sb("gups", [17, 512], F32)
        paT = sb("paT", [17, 256], F32)
        sm = sb("smalls", [128, 64], F32)
        banks = [st.enter_context(nc.psum_tensor("bank%d" % i, [128, 512], F32)) for i in range(8)]

        hTo3 = hTo[:, :].rearrange("p (c t) -> p c t", c=8)
        hTh3 = hTh[:, :].rearrange("p (c t) -> p c t", c=8)
        ogT3 = ogT[:, :].rearrange("p (c t) -> p c t", c=8)
        oaT3 = oaT[:, :].rearrange("p (c t) -> p c t", c=4)
        RA = Region(arena, 32768)
        RG = Region(ogT, 8 * NTOK)
        RH = Region(hTh, 8 * NTOK)
        RO = Region(hTo, 8 * NTOK)
        RT = Region(oaT, 4 * NTOK)

        P = Prog(nc)
        B = ['b%d' % i for i in range(8)]

        def MM(out, lhsT, rhs, start=True, stop=True, reads=(), writes=(), skip=False):
            if skip:
                P.add('pe', lambda e: e.matmul(out, lhsT, rhs, start=start, stop=stop, skip_group_check=True),
                      reads, writes)
            else:
                P.add('pe', lambda e: e.matmul(out, lhsT, rhs, start=start, stop=stop), reads, writes)

        def TR(out, in_, reads=(), writes=()):
            ident = cb[:, CB_ID:CB_ID + 128]
            P.add('pe', lambda e: e.transpose(out, in_, ident), reads, writes)

        def ACT(out, in_, func, reads=(), writes=(), scale=None, bias=None, accum_out=None):
            kw = {}
            if scale is not None:
                kw['scale'] = scale
            if bias is not None:
                kw['bias'] = bias
            if accum_out is not None:
                kw['accum_out'] = accum_out
            P.add('act', lambda e: e.activation(out=out, in_=in_, func=func, **kw), reads, writes)

        def TT(eng, out, in0, in1, op, reads=(), writes=()):
            P.add(eng, lambda e: e.tensor_tensor(out=out, in0=in0, in1=in1, op=op), reads, writes)

        def TS(eng, out, in0, s1, op0, reads=(), writes=(), s2=None, op1=None):
            if op1 is None:
                P.add(eng, lambda e: e.tensor_scalar(out=out, in0=in0, scalar1=s1, scalar2=None, op0=op0),
                      reads, writes)
            else:
                P.add(eng, lambda e: e.tensor_scalar(out=out, in0=in0, scalar1=s1, scalar2=s2, op0=op0, op1=op1),
                      reads, writes)

        def STT(out, in0, scalar, in1, op0, op1, reads=(), writes=()):
            P.add('dve', lambda e: e.scalar_tensor_tensor(out=out, in0=in0, scalar=scalar, in1=in1,
                                                          op0=op0, op1=op1), reads, writes)

        def CP(eng, out, in_, reads=(), writes=()):
            if eng == 'act':
                ACT(out, in_, AF.Copy, reads, writes)
            else:
                P.add(eng, lambda e: e.tensor_copy(out=out, in_=in_), reads, writes)

        def MSET(eng, ap, val, writes=()):
            P.add(eng, lambda e: e.memset(ap, val), (), writes)

        def DMA(eng, out, in_, key, reads=(), writes=()):
            P.add(eng, lambda e: e.dma_start(out=out, in_=in_), reads, writes, dma_key=key)

        def wcols(w, c0, n):
            return w[:, c0:c0 + n].rearrange("(c p) n -> p c n", p=128)

        def v3(ap, c):
            return ap.rearrange("p (c n) -> p c n", c=c)

        def dump(name, ap, shape, key_reads=()):
            if not DEBUG:
                return
            d = dram("dbg_" + name, shape, kind="ExternalOutput", dt=ap.dtype)
            dbg[name] = d
            DMA('sp', d, ap, 'dbg_' + name, reads=key_reads)

        eps_ap = cf[:, CF_EPS:CF_EPS + 1]
        one_ap = cf[:, CF_ONE:CF_ONE + 1]

        def rstd_from_sum(out_ap, in_ap, inv_n, key_in, key_out, tmpkey):
            ACT(out_ap, in_ap, AF.Ln, reads=[key_in], writes=[key_out], scale=inv_n, bias=eps_ap)
            ACT(out_ap, out_ap, AF.Exp, reads=[key_out], writes=[key_out], scale=-0.5)

        DMA('sp', cf[:, :], cf_d, 'cf', writes=['cf'])
        DMA('pool', cb[:, :], cb_d, 'cb', writes=['cb'])
        DMA('sp', gup[:, :], gup_d, 'gup', writes=['gup'])
        MSET('dve', paT[:, :], 1.0, writes=['paT0', 'paT1'])
        MSET('dve', S[:, :], 0.0, writes=['S0', 'S1', 'S2', 'S3'])
        MSET('pool', Sbf[:, :], 0.0, writes=['Sbf0', 'Sbf1', 'Sbf2', 'Sbf3'])
        P.barrier()

        RA.reset()
        Wk_tok = RA.bf(8 * 512)
        Wv = RA.bf(8 * 1024)
        Wpa = RA.bf(8 * 16)
        Wk3, Wv3, Wpa3 = v3(Wk_tok, 8), v3(Wv, 8), v3(Wpa, 8)
        xt = [RA.f32(1024) for _ in range(2)]
        xs = [RA.bf(1024) for _ in range(2)]
        junk = RA.bf(1024)
        hTt = [RA.bf(1024) for _ in range(2)]
        la = [RA.f32(512) for _ in range(2)]
        e1 = RA.f32(512)
        ekd = RA.f32(512)
        kd = [RA.bf(512) for _ in range(2)]
        vbf = [RA.bf(1024) for _ in range(2)]

        DMA('pool', Wk3, wcols(w_in, O_GK, 512), 'Wk', writes=['Wk'])
        DMA('pool', Wv3, wcols(w_in, O_GV, 1024), 'Wv', writes=['Wv'])
        DMA('pool', Wpa3, wcols(w_in, O_GA, 16), 'Wpa', writes=['Wpa'])

        g1b = bass.AP(cf, CF_G1, [[NCF, 128], [1, 8], [0, 128]])
        g2b = bass.AP(cf, CF_G2, [[NCF, 128], [1, 8], [0, 128]])
        lst_ap = cf[:, CF_LST:CF_LST + 128]
        uinc_ap = cf[:, CF_UINC:CF_UINC + 128]
        cind_ap = cf[:, CF_CIND:CF_CIND + 2]

        def norm_transpose(src_rows, s, dst3, gb, xkey=None):
            ss = sm[:, s:s + 1]
            rs = sm[:, 2 + s:3 + s]
            if xkey is None:
                DMA('sp', xt[s], src_rows, 'xt%d' % s, writes=['xt%d' % s])
                xap, xk = xt[s], 'xt%d' % s
            else:
                xap, xk = src_rows, xkey
            ACT(junk, xap, AF.Square, reads=[xk], writes=['junk', 'ss%d' % s], accum_out=ss)
            rstd_from_sum(rs, ss, 1.0 / D, 'ss%d' % s, 'rs%d' % s, 'rst%d' % s)
            TS('dve', xs[s], xap, rs, ALU.mult, reads=[xk, 'rs%d' % s], writes=['xs%d' % s])
            pb = banks[0][:, :].bitcast(BF16)
            for c in range(8):
                TR(pb[:, c * 128:(c + 1) * 128], xs[s][:, c * 128:(c + 1) * 128], reads=['xs%d' % s], writes=[B[0]])
            TT('dve', dst3, v3(pb, 8), gb, ALU.mult, reads=[B[0]], writes=['hT%d' % s])

        def gla_stage2(hT3, s, own):
            hk = 'hT%d' % s
            for c in range(8):
                MM(banks[4][0:16, 0:128], Wpa3[:, c, :], hT3[:, c, :], start=(c == 0), stop=(c == 7),
                   reads=[hk, 'Wpa'], writes=[B[4]])
            CP('act', paT[0:16, s * 128:(s + 1) * 128], banks[4][0:16, 0:128], reads=[B[4]], writes=['paT%d' % s])
            for c in range(8):
                MM(banks[1][:, :], hT3[:, c, :], Wk3[:, c, :], start=(c == 0), stop=(c == 7),
                   reads=[hk, 'Wk'], writes=[B[1]])
            if own:
                for c in range(8):
                    MM(banks[6][:, :], hT3[:, c, :], Wq3[:, c, :], start=(c == 0), stop=(c == 7),
                       reads=[hk, 'Wq_g'], writes=[B[6]])
            MM(banks[5][:, :], paT[0:17, s * 128:(s + 1) * 128], gup[0:17, :], reads=['paT%d' % s, 'gup'],
               writes=[B[5]])
            ACT(e1, banks[5][:, :], AF.Exp, reads=[B[5]], writes=['e1'], scale=-1.0)
            ACT(la[s], e1, AF.Ln, reads=['e1'], writes=['la%d' % s], bias=one_ap)
            for half in range(2):
                for c in range(8):
                    MM(banks[2 + half][:, :], hT3[:, c, :], Wv3[:, c, half * 512:(half + 1) * 512],
                       start=(c == 0), stop=(c == 7), reads=[hk, 'Wv'], writes=[B[2 + half]])
            CP('act', vbf[s][:, 0:512], banks[2][:, :], reads=[B[2]], writes=['vbfa%d' % s])
            CP('act', vbf[s][:, 512:1024], banks[3][:, :], reads=[B[3]], writes=['vbfb%d' % s])
            MM(banks[5][:, :], lst_ap, la[s], reads=['la%d' % s], writes=[B[5]])
            for h in range(4):
                MM(banks[4][:, 128 + 2 * h:130 + 2 * h], la[s][:, h * 128:(h + 1) * 128], cind_ap,
                   reads=['la%d' % s], writes=[B[4]])
            ACT(ekd, banks[5][:, :], AF.Exp, reads=[B[5]], writes=['ekd'], scale=-1.0 / 16)
            ebl = sm[:, 8 + 8 * s:16 + 8 * s]
            ACT(ebl, banks[4][:, 128:136], AF.Exp, reads=[B[4]], writes=['ebl%d' % s], scale=-1.0 / 16)
            TT('dve', kd[s], banks[1][:, :], ekd, ALU.mult, reads=[B[1], 'ekd'], writes=['kd%d' % s])
            return ebl

        def gla_stage3(s, own, ubanks=(6, 7)):
            ebl = sm[:, 8 + 8 * s:16 + 8 * s]
            for h in range(4):
                ub = ubanks[0] if h < 2 else ubanks[1]
                MM(banks[ub][:, (h % 2) * 256:(h % 2) * 256 + 256], kd[s][:, h * 128:(h + 1) * 128],
                   vbf[s][:, h * 256:(h + 1) * 256],
                   reads=['kd%d' % s, 'vbfa%d' % s if h < 2 else 'vbfb%d' % s], writes=[B[ub]])
            for h in range(4):
                ub = ubanks[0] if h < 2 else ubanks[1]
                Sh = S[:, h * 256:(h + 1) * 256]
                STT(Sh, Sh, ebl[:, 2 * h:2 * h + 1], banks[ub][:, (h % 2) * 256:(h % 2) * 256 + 256], ALU.mult, ALU.add,
                    reads=['S%d' % h, 'ebl%d' % s, B[ub]], writes=['S%d' % h])
                if own:
                    CP('act', Sbf[:, h * 256:(h + 1) * 256], Sh, reads=['S%d' % h], writes=['Sbf%d' % h])

        def tile_dst(ti):
            if ti >= NT_PRE:
                j = ti - NT_PRE
                return hTo3[:, :, j * 128:(j + 1) * 128]
            if ti >= NT_PRE - NT_OWN:
                j = ti - (NT_PRE - NT_OWN)
                return hTh3[:, :, j * 128:(j + 1) * 128]
            return v3(hTt[ti % 2], 8)

        NT_ALL = NT_PRE + NT_OWN
        norm_transpose(xin[0:128, :], 0, tile_dst(0), g1b)
        for ti in range(NT_PRE):
            s = ti % 2
            if ti + 1 < NT_ALL:
                norm_transpose(xin[(ti + 1) * 128:(ti + 2) * 128, :], (ti + 1) % 2, tile_dst(ti + 1), g1b)
            gla_stage2(tile_dst(ti), s, False)
            if ti >= 1:
                gla_stage3((ti - 1) % 2, False)
        gla_stage3((NT_PRE - 1) % 2, False)
        for ti in range(NT_PRE + 1, NT_ALL):
            norm_transpose(xin[ti * 128:(ti + 1) * 128, :], ti % 2, tile_dst(ti), g1b)
        P.barrier()
        dump("hTo", hTo[:, :], [128, 8 * NTOK])
        dump("hTh", hTh[:, :], [128, 8 * NTOK])
        dump("Spre", S[:, :], [128, 1024])
        for h in range(4):
            CP('act', Sbf[:, h * 256:(h + 1) * 256], S[:, h * 256:(h + 1) * 256])
        P.barrier()

        RA.reset()
        RG.reset()
        Wq_a = [RA.bf(1024) for _ in range(3)]
        Wk_a = [RA.bf(1024) for _ in range(3)]
        QT = [RA.bf(NTOK) for _ in range(3)]
        HALO = [128, 512, 2048]
        KT = [RA.bf(HALO[g] + NTOK) for g in range(3)]
        NBLK = [17, 20, 32]
        Vb = [RA.bf(NBLK[g] * 128) for g in range(3)]
        Wv_a = [RG.bf(1024) for _ in range(3)]
        PT3h = [RG.bf(16 * 256) for _ in range(2)]
        ptb = [RG.bf(512) for _ in range(4)]
        sqb = [RG.bf(512) for _ in range(2)]
        lnb = [RG.f32(512) for _ in range(2)]
        lnd = [RA.f32(512) for _ in range(2)]
        gq_ap = cf[:, CF_GQ:CF_GQ + 1]
        gk_ap = cf[:, CF_GK:CF_GK + 1]
        bones = cb[:, CB_BONES:CB_BONES + 128]
        ones64 = cb[:, CB_ONES:CB_ONES + 64]
        m4 = {0: cb[:, CB_M4:CB_M4 + 512], 1: cb[:, CB_M4H0:CB_M4H0 + 512], 2: cb[:, CB_M4HH:CB_M4HH + 512]}
        cnt = {'qk': 0, 'pt': 0, 'rnd': 0, 'sc': 0, 'vb': 0}

        def qk_tile(wap, hsrc3, t0, n, dst, gain, wkey):
            i = cnt['qk']
            cnt['qk'] += 1
            pbk = (0, 5)[i % 2]
            pk = B[pbk]
            stb = 3 + (i % 2)
            w3 = v3(wap, 8)
            for c in range(8):
                MM(banks[pbk][:, 0:n], w3[:, c, :], hsrc3[:, c, t0:t0 + n], start=(c == 0), stop=(c == 7),
                   reads=[wkey], writes=[pk])
            sq = sqb[i % 2]
            ACT(sq[:, 0:n], banks[pbk][:, 0:n], AF.Square, reads=[pk], writes=['sq%d' % (i % 2)])
            MM(banks[stb][:, 0:n], bones, sq[:, 0:n], reads=['sq%d' % (i % 2)], writes=[B[stb]])
            ln = lnb[i % 2]
            ACT(ln[:, 0:n], banks[stb][:, 0:n], AF.Ln, reads=[B[stb]], writes=['ln%d' % (i % 2)], bias=eps_ap)
            ACT(ln[:, 0:n], ln[:, 0:n], AF.Exp, reads=['ln%d' % (i % 2)], writes=['ln%d' % (i % 2)], scale=-0.5)
            STT(dst, banks[pbk][:, 0:n], gain, ln[:, 0:n], ALU.mult, ALU.mult,
                reads=[pk, 'ln%d' % (i % 2)], writes=['qkdst'])

        for p in range(4):
            for g in range(3):
                hc = (g * 8 + 2 * p) * 64
                DMA('pool', v3(Wq_a[g], 8), wcols(w_in, O_AQ + hc, 128), 'Wq_a%d' % g, writes=['Wq_a%d' % g])
                DMA('pool', v3(Wk_a[g], 8), wcols(w_in, O_AK + hc, 128), 'Wk_a%d' % g, writes=['Wk_a%d' % g])
                DMA('pool', v3(Wv_a[g], 8), wcols(w_in, O_AV + hc, 128), 'Wv_a%d' % g, writes=['Wv_a%d' % g])
            for g in range(3):
                d = DIL[g]
                for u in range(4):
                    qk_tile(Wq_a[g], hTo3, u * 512, 512, QT[g][:, u * 512:(u + 1) * 512], gq_ap, 'Wq_a%d' % g)
                hl = HALO[g]
                nsp = max(1, hl // 512)
                w = hl // nsp
                for u in range(nsp):
                    qk_tile(Wk_a[g], hTh3, NTOK - hl + u * w, w, KT[g][:, u * w:(u + 1) * w], gk_ap, 'Wk_a%d' % g)
                for u in range(4):
                    qk_tile(Wk_a[g], hTo3, u * 512, 512, KT[g][:, hl + u * 512:hl + (u + 1) * 512], gk_ap,
                            'Wk_a%d' % g)
                seg = 128 * d
                wv3 = v3(Wv_a[g], 8)
                for b0 in range(0, NBLK[g], 4):
                    nb = min(4, NBLK[g] - b0)
                    vi = cnt['vb']
                    cnt['vb'] += 1
                    vbank = 3 + (vi % 2)
                    for bi in range(nb):
                        blk = b0 + bi
                        n_, r_ = blk // d - 1, blk % d
                        t0 = n_ * seg + r_
                        if t0 < 0:
                            src3, tt0 = hTh3, NTOK + t0
                        else:
                            src3, tt0 = hTo3, t0
                        for c in range(8):
                            MM(banks[vbank][:, bi * 128:(bi + 1) * 128],
                               src3[:, c, tt0:tt0 + 127 * d + 1:d], wv3[:, c, :],
                               start=(c == 0), stop=(c == 7), reads=['Wv_a%d' % g], writes=[B[vbank]])
                    CP('act' if vi % 2 == 0 else 'dve', Vb[g][:, b0 * 128:(b0 + nb) * 128],
                       banks[vbank][:, 0:nb * 128], reads=[B[vbank]], writes=['Vb'])
            nbk, dbk = (1, 2)
            numb, denb = banks[nbk], banks[dbk]
            SB_ROT = (6, 7, 0, 5)
            items = []

            def mk_score(hd, g, blks, mkind, rec, fixed=None):
                def f():
                    d = DIL[g]
                    seg = 128 * d
                    hl = HALO[g]
                    po = hd * 64
                    si = cnt['sc']
                    cnt['sc'] += 1
                    sbank = SB_ROT[si % 4]
                    if fixed is None:
                        i = cnt['pt']
                        cnt['pt'] += 1
                        dst_pt, dst_key = ptb[i % 4], 'pt%d' % (i % 4)
                    else:
                        dst_pt, dst_key = fixed
                    rec['pt'], rec['key'] = dst_pt, dst_key
                    for bi, (n_, r_) in enumerate(blks):
                        t0 = n_ * seg + r_
                        qap = QT[g][po:po + 64, t0:t0 + 127 * d + 1:d]
                        kprev = KT[g][po:po + 64, hl + t0 - seg:hl + t0 - seg + 127 * d + 1:d]
                        kcur = KT[g][po:po + 64, hl + t0:hl + t0 + 127 * d + 1:d]
                        MM(banks[sbank][:, bi * 256:bi * 256 + 128], kprev, qap, reads=['qkdst'], writes=[B[sbank]])
                        MM(banks[sbank][:, bi * 256 + 128:bi * 256 + 256], kcur, qap, reads=['qkdst'],
                           writes=[B[sbank]])
                    ACT(dst_pt[:, 0:512], banks[sbank][:, 0:512], AF.Exp, reads=[B[sbank]], writes=[dst_key],
                        scale=0.125)
                    TT('dve', dst_pt[:, 0:512], dst_pt[:, 0:512], m4[mkind][:, 0:512], ALU.mult,
                       reads=[dst_key], writes=[dst_key])
                return f

            def mk_pv(hd, g, R, rec, half, first):
                def f():
                    po = hd * 64
                    pt, key = rec['pt'], rec['key']
                    pt3 = v3(pt, 2)

                    def num(kb, mov, cols):
                        stf = first['n%d' % hd]
                        first['n%d' % hd] = False
                        MM(numb[po:po + 64, cols], Vb[g][:, kb * 128 + hd * 64:kb * 128 + hd * 64 + 64], mov,
                           start=stf, stop=False, reads=['Vb', key], writes=[B[nbk] + str(hd)], skip=True)

                    def den(out_ap, mov):
                        stf = first['d%d' % hd]
                        first['d%d' % hd] = False
                        MM(out_ap, ones64, mov, start=stf, stop=False, reads=[key], writes=[B[dbk] + str(hd)],
                           skip=True)
                    if g == 0:
                        n0 = 4 * R + 2 * half
                        for bi in range(2):
                            n_ = n0 + bi
                            cols = slice((n_ - 4 * R) * 128, (n_ - 4 * R) * 128 + 128)
                            num(n_, pt[:, bi * 256:bi * 256 + 128], cols)
                            num(n_ + 1, pt[:, bi * 256 + 128:bi * 256 + 256], cols)
                        for bi in range(2):
                            n_ = n0 + bi
                            cols = slice((n_ - 4 * R) * 128, (n_ - 4 * R) * 128 + 128)
                            den(denb[po:po + 64, cols], pt[:, bi * 256:bi * 256 + 128])
                            den(denb[po:po + 64, cols], pt[:, bi * 256 + 128:bi * 256 + 256])
                    else:
                        for bi in range(2):
                            r_ = 2 * half + bi
                            cols = slice(r_, 512, 4)
                            blk = (R + 1) * 4 + r_
                            num(blk - 4, pt[:, bi * 256:bi * 256 + 128], cols)
                            num(blk, pt[:, bi * 256 + 128:bi * 256 + 256], cols)
                        for bi in range(2):
                            r_ = 2 * half + bi
                            den(denb[po:po + 64, r_:512:4], pt[:, bi * 256:bi * 256 + 128])
                            den(denb[po:po + 64, r_:512:4], pt[:, bi * 256 + 128:bi * 256 + 256])
                return f

            def mk_pv3(hd, R, first):
                def f():
                    po = hd * 64
                    keys = ['PT3_%d_%d' % (hd, rp) for rp in range(8)]
                    for r_ in range(16):
                        cols = slice(r_, 512, 16)
                        base = r_ * 256
                        for part, kb in ((0, r_), (1, 16 + r_)):
                            stf = first['n%d' % hd]
                            first['n%d' % hd] = False
                            MM(numb[po:po + 64, cols], Vb[2][:, kb * 128 + hd * 64:kb * 128 + hd * 64 + 64],
                               PT3h[hd][:, base + part * 128 + 32 * R:base + part * 128 + 32 * R + 32],
                               start=stf, stop=False, reads=['Vb', 'PT3_%d_%d' % (hd, r_ // 2)],
                               writes=[B[nbk] + str(hd)], skip=True)
                    p16 = v3(PT3h[hd], 16)
                    for r_ in range(16):
                        cols = slice(r_, 512, 16)
                        base = r_ * 256
                        for part in range(2):
                            stf = first['d%d' % hd]
                            first['d%d' % hd] = False
                            MM(denb[po:po + 64, cols], ones64,
                               PT3h[hd][:, base + part * 128 + 32 * R:base + part * 128 + 32 * R + 32],
                               start=stf, stop=False, reads=['PT3_%d_%d' % (hd, r_ // 2)],
                               writes=[B[dbk] + str(hd)], skip=True)
                return f

            def mk_norm(p, R):
                def f():
                    rn = cnt['rnd']
                    cnt['rnd'] += 1
                    nk = [B[nbk] + '0', B[nbk] + '1']
                    dk_ = [B[dbk] + '0', B[dbk] + '1']
                    l = lnd[rn % 2]
                    ACT(l, denb[:, :], AF.Ln, reads=dk_, writes=['lnd%d' % (rn % 2)])
                    ACT(l, l, AF.Exp, reads=['lnd%d' % (rn % 2)], writes=['lnd%d' % (rn % 2)], scale=-1.0)
                    TT('dve', oaT3[:, p, R * 512:(R + 1) * 512], numb[:, :], l, ALU.mult,
                       reads=nk + ['lnd%d' % (rn % 2)], writes=['oaT'])
                return f

            for R in range(4):
                first = {'n0': True, 'n1': True, 'd0': True, 'd1': True}
                for hd in range(2):
                    if R == 0:
                        for rp in range(8):
                            items.append((mk_score(hd, 2, [(0, 2 * rp), (0, 2 * rp + 1)], 2, {},
                                                   fixed=(PT3h[hd][:, rp * 512:(rp + 1) * 512],
                                                          'PT3_%d_%d' % (hd, rp))), None))
                    for half in range(2):
                        n0 = 4 * R + 2 * half
                        rec = {}
                        items.append((mk_score(hd, 0, [(n0, 0), (n0 + 1, 0)], 1 if n0 == 0 else 0, rec),
                                      mk_pv(hd, 0, R, rec, half, first)))
                    for half in range(2):
                        rec = {}
                        items.append((mk_score(hd, 1, [(R, 2 * half), (R, 2 * half + 1)], 2 if R == 0 else 0, rec),
                                      mk_pv(hd, 1, R, rec, half, first)))
                    items.append((None, mk_pv3(hd, R, first)))
                items.append((None, mk_norm(p, R)))
            LOOK = 2
            nxt = 0
            for i, (sf, pf) in enumerate(items):
                while nxt <= min(i + LOOK, len(items) - 1):
                    if items[nxt][0] is not None:
                        items[nxt][0]()
                    nxt += 1
                if pf is not None:
                    pf()
        P.barrier()
        dump("oaT", oaT[:, :], [128, 4 * NTOK])
        P.barrier()

        RA.reset()
        RH.reset()
        Wk_tok = RA.bf(8 * 512)
        Wv = RA.bf(8 * 1024)
        Wpa = RA.bf(8 * 16)
        Wk3, Wv3, Wpa3 = v3(Wk_tok, 8), v3(Wv, 8), v3(Wpa, 8)
        la = [RA.f32(512) for _ in range(2)]
        e1 = RA.f32(512)
        ekd = RA.f32(512)
        kd = [RA.bf(512) for _ in range(2)]
        vbf = [RA.bf(1024) for _ in range(2)]
        junk = RA.bf(1024)
        Wq_g = RH.bf(8 * 512)
        Wr_g = RH.bf(8 * 1024)
        Wq3, Wr3 = v3(Wq_g, 8), v3(Wr_g, 8)
        sr = RA.f32(1024)
        eq = RA.f32(512)
        ek = RA.f32(512)
        qe_tok = RA.bf(512)
        ke_tok = RA.bf(512)
        qeT = RA.bf(512)
        keT = RA.bf(512)
        AT = RA.bf(512)
        t1 = RA.f32(1024)
        og = RA.bf(1024)
        gm_b = bass.AP(cb, CB_GM, [[NCB, 128], [0, 4], [1, 128]])
        gout_ap = cf[:, CF_GOUT:CF_GOUT + 256]
        DMA('pool', Wk3, wcols(w_in, O_GK, 512), 'Wk', writes=['Wk'])
        DMA('pool', Wv3, wcols(w_in, O_GV, 1024), 'Wv', writes=['Wv'])
        DMA('pool', Wpa3, wcols(w_in, O_GA, 16), 'Wpa', writes=['Wpa'])
        DMA('pool', Wq3, wcols(w_in, O_GQ, 512), 'Wq_g', writes=['Wq_g'])
        DMA('pool', Wr3, wcols(w_in, O_GR, 1024), 'Wr_g', writes=['Wr_g'])

        ob = {0: 7, 1: 7, 2: 0, 3: 0}
        pb = banks[0][:, :].bitcast(BF16)
        ssh = sm[:, 32:36]
        rsh = sm[:, 36:40]

        def main_part2(j):
            s = j % 2
            hT3 = hTo3[:, :, j * 128:(j + 1) * 128]
            MM(banks[5][:, :], uinc_ap, la[s], reads=['la%d' % s], writes=[B[5]])
            ACT(eq, banks[5][:, :], AF.Exp, reads=[B[5]], writes=['eq'], scale=-1.0 / 16)
            ACT(ek, banks[5][:, :], AF.Exp, reads=[B[5]], writes=['ek'], scale=1.0 / 16)
            STT(qe_tok, banks[6][:, :], float(128 ** -0.5), eq, ALU.mult, ALU.mult, reads=[B[6], 'eq'],
                writes=['qe_tok'])
            TT('dve', ke_tok, banks[1][:, :], ek, ALU.mult, reads=[B[1], 'ek'], writes=['ke_tok'])
            for half in range(2):
                for c in range(8):
                    MM(banks[2 + half][:, :], hT3[:, c, :], Wr3[:, c, half * 512:(half + 1) * 512],
                       start=(c == 0), stop=(c == 7), reads=['Wr_g'], writes=[B[2 + half]])
                ACT(sr[:, half * 512:(half + 1) * 512], banks[2 + half][:, :], AF.Silu,
                    reads=[B[2 + half]], writes=['sr%d' % half])
            for h in range(4):
                TR(pb[:, h * 128:(h + 1) * 128], qe_tok[:, h * 128:(h + 1) * 128], reads=['qe_tok'], writes=[B[0]])
            for h in range(4):
                TR(pb[:, 512 + h * 128:512 + (h + 1) * 128], ke_tok[:, h * 128:(h + 1) * 128], reads=['ke_tok'],
                   writes=[B[0]])
            CP('act', qeT, pb[:, 0:512], reads=[B[0]], writes=['qeT'])
            CP('dve', keT, pb[:, 512:1024], reads=[B[0]], writes=['keT'])
            for h in range(4):
                MM(banks[7][:, h * 128:(h + 1) * 128], keT[:, h * 128:(h + 1) * 128], qeT[:, h * 128:(h + 1) * 128],
                   reads=['keT', 'qeT'], writes=[B[7]])
            TT('dve', v3(AT, 4), v3(banks[7][:, :], 4), gm_b, ALU.mult, reads=[B[7]], writes=['AT'])
            for h in range(4):
                MM(banks[ob[h]][:, (h % 2) * 256:(h % 2) * 256 + 256], AT[:, h * 128:(h + 1) * 128],
                   vbf[s][:, h * 256:(h + 1) * 256], start=(h % 2 == 0), stop=False,
                   reads=['AT', 'vbfa%d' % s if h < 2 else 'vbfb%d' % s], writes=[B[ob[h]]], skip=True)
            for h in range(4):
                MM(banks[ob[h]][:, (h % 2) * 256:(h % 2) * 256 + 256], qeT[:, h * 128:(h + 1) * 128],
                   Sbf[:, h * 256:(h + 1) * 256], start=False, stop=True,
                   reads=['qeT', 'Sbf%d' % h], writes=[B[ob[h]]], skip=True)
            gla_stage3(s, True, ubanks=(6, 1))

        def main_tail_a(j):
            for h in range(4):
                ACT(junk[:, h * 256:(h + 1) * 256], banks[ob[h]][:, (h % 2) * 256:(h % 2) * 256 + 256], AF.Square,
                    reads=[B[ob[h]]], writes=['junk%d' % h, 'ssh%d' % h], accum_out=ssh[:, h:h + 1])
            ACT(rsh, ssh, AF.Ln, reads=['ssh0', 'ssh1', 'ssh2', 'ssh3'], writes=['rsh'], scale=1.0 / 256, bias=eps_ap)
            ACT(rsh, rsh, AF.Exp, reads=['rsh'], writes=['rsh'], scale=-0.5)
            for h in range(4):
                STT(t1[:, h * 256:(h + 1) * 256], banks[ob[h]][:, (h % 2) * 256:(h % 2) * 256 + 256],
                    rsh[:, h:h + 1], gout_ap, ALU.mult, ALU.mult, reads=[B[ob[h]], 'rsh'],
                    writes=['t1_%d' % h])
            TT('pool', og, t1, sr, ALU.mult, reads=['t1_0', 't1_1', 't1_2', 't1_3', 'sr0', 'sr1'], writes=['og'])

        def main_tail_b(j):
            for c in range(8):
                TR(pb[:, c * 128:(c + 1) * 128], og[:, c * 128:(c + 1) * 128], reads=['og'], writes=[B[0]])
            CP('act', ogT3[:, :, j * 128:(j + 1) * 128], v3(pb, 8), reads=[B[0]], writes=['ogT'])

        for j in range(NT_OWN):
            if j >= 1:
                main_tail_a(j - 1)
            gla_stage2(hTo3[:, :, j * 128:(j + 1) * 128], j % 2, True)
            if j >= 1:
                main_tail_b(j - 1)
            main_part2(j)
        main_tail_a(NT_OWN - 1)
        main_tail_b(NT_OWN - 1)
        P.barrier()
        dump("ogT", ogT[:, :], [128, 8 * NTOK])
        P.barrier()

        RA.reset()
        mixT3 = hTh3
        WA = [RA.bf(4 * 128) for _ in range(2)]
        WB = [RA.bf(8 * 128) for _ in range(2)]
        WGA = [RA.bf(8 * 128) for _ in range(2)]
        WGB = [RA.bf(8 * 128) for _ in range(2)]
        sgA = [RA.f32(512) for _ in range(2)]
        sgB = [RA.f32(512) for _ in range(2)]
        tA = [RA.f32(512) for _ in range(2)]
        tB = [RA.f32(512) for _ in range(2)]
        it = 0
        for jc in range(8):
            ws = jc % 2
            wa3, wb3, wga3, wgb3 = v3(WA[ws], 4), v3(WB[ws], 8), v3(WGA[ws], 8), v3(WGB[ws], 8)
            DMA('pool', wa3, wcols(w_a, jc * 128, 128), 'WA%d' % ws, writes=['WA%d' % ws])
            DMA('pool', wb3, wcols(w_b, jc * 128, 128), 'WB%d' % ws, writes=['WB%d' % ws])
            DMA('pool', wga3, wcols(w_in, O_GATE + jc * 128, 128), 'WGA%d' % ws, writes=['WGA%d' % ws])
            DMA('pool', wgb3, wcols(w_in, O_GATE + D + jc * 128, 128), 'WGB%d' % ws, writes=['WGB%d' % ws])
            for R in range(4):
                k = it % 2
                it += 1
                ts_ = slice(R * 512, (R + 1) * 512)
                bA, bB, bGA, bGB = (0 + 4 * k, 1 + 4 * k, 2 + 4 * k, 3 + 4 * k)
                for c in range(4):
                    MM(banks[bA][:, :], wa3[:, c, :], oaT3[:, c, ts_], start=(c == 0), stop=(c == 3),
                       reads=['WA%d' % ws], writes=[B[bA]])
                for c in range(8):
                    MM(banks[bB][:, :], wb3[:, c, :], ogT3[:, c, ts_], start=(c == 0), stop=(c == 7),
                       reads=['WB%d' % ws], writes=[B[bB]])
                for c in range(8):
                    MM(banks[bGA][:, :], wga3[:, c, :], hTo3[:, c, ts_], start=(c == 0), stop=(c == 7),
                       reads=['WGA%d' % ws], writes=[B[bGA]])
                for c in range(8):
                    MM(banks[bGB][:, :], wgb3[:, c, :], hTo3[:, c, ts_], start=(c == 0), stop=(c == 7),
                       reads=['WGB%d' % ws], writes=[B[bGB]])
                ACT(sgA[k], banks[bGA][:, :], AF.Sigmoid, reads=[B[bGA]], writes=['sgA%d' % k],
                    bias=cf[:, CF_GB + jc:CF_GB + jc + 1])
                ACT(sgB[k], banks[bGB][:, :], AF.Sigmoid, reads=[B[bGB]], writes=['sgB%d' % k],
                    bias=cf[:, CF_GB + 8 + jc:CF_GB + 8 + jc + 1])
                TT('dve', tA[k], banks[bA][:, :], sgA[k], ALU.mult, reads=[B[bA], 'sgA%d' % k], writes=['tA%d' % k])
                TT('dve', tB[k], banks[bB][:, :], sgB[k], ALU.mult, reads=[B[bB], 'sgB%d' % k], writes=['tB%d' % k])
                TT('pool', mixT3[:, jc, ts_], tA[k], tB[k], ALU.add, reads=['tA%d' % k, 'tB%d' % k], writes=['mixT'])
        P.barrier()
        dump("mixT", hTh[:, :], [128, 8 * NTOK])
        P.barrier()

        RA.reset()
        RG.reset()
        RT.reset()
        x1 = RA.f32(NT_OWN * 1024)
        Wo = RT.bf(8 * 1024)
        Wo3 = v3(Wo, 8)
        xt = [RG.f32(1024) for _ in range(2)]
        xs = [RG.bf(1024) for _ in range(2)]
        junk = RG.bf(1024)
        DMA('pool', Wo3, wcols(w_o, 0, 1024), 'Wo', writes=['Wo'])
        def d2_mm(j):
            s = j % 2
            DMA('sp', xt[s], xin[NPRE + j * 128:NPRE + (j + 1) * 128, :], 'xt%d' % s, writes=['xt%d' % s])
            for half in range(2):
                bk = 1 + half + 2 * s
                for c in range(8):
                    MM(banks[bk][:, :], mixT3[:, c, j * 128:(j + 1) * 128], Wo3[:, c, half * 512:(half + 1) * 512],
                       start=(c == 0), stop=(c == 7), reads=['Wo'], writes=[B[bk]])
                TT('dve', x1[:, j * 1024 + half * 512:j * 1024 + (half + 1) * 512], banks[bk][:, :],
                   xt[s][:, half * 512:(half + 1) * 512], ALU.add, reads=[B[bk], 'xt%d' % s], writes=['x1t%d' % s])

        d2_mm(0)
        for j in range(NT_OWN):
            if j + 1 < NT_OWN:
                d2_mm(j + 1)
            norm_transpose(x1[:, j * 1024:(j + 1) * 1024], j % 2, hTo3[:, :, j * 128:(j + 1) * 128], g2b,
                           xkey='x1t%d' % (j % 2))
        P.barrier()
        dump("x1", x1, [128, NT_OWN * 1024])
        P.barrier()

        RG.reset()
        RH.reset()
        RT.reset()
        h2T3 = hTo3
        aT = [RG.bf(4 * NTOK) for _ in range(2)]
        Wup = [RH.bf(8 * 512) for _ in range(2)]
        Wdn = [RH.bf(4 * 1024) for _ in range(2)]
        rst = [RT.f32(512) for _ in range(2)]
        ui = 0
        di = 0
        for f in range(8):
            ws = f % 2
            wu3, wd3, a3 = v3(Wup[ws], 8), v3(Wdn[ws], 4), v3(aT[ws], 4)
            DMA('pool', wu3, wcols(w_up, f * 512, 512), 'Wup%d' % ws, writes=['Wup%d' % ws])
            DMA('pool', wd3, w_dn[f * 512:(f + 1) * 512, :].rearrange("(c p) n -> p c n", p=128), 'Wdn%d' % ws,
                writes=['Wdn%d' % ws])
            for q in range(4):
                for u in range(4):
                    bk = ui % 4
                    k = ui % 2
                    ui += 1
                    for c in range(8):
                        MM(banks[bk][:, :], wu3[:, c, q * 128:(q + 1) * 128], h2T3[:, c, u * 512:(u + 1) * 512],
                           start=(c == 0), stop=(c == 7), reads=['Wup%d' % ws], writes=[B[bk]])
                    ACT(rst[k], banks[bk][:, :], AF.Relu, reads=[B[bk]], writes=['rst%d' % k])
                    TT('pool', a3[:, q, u * 512:(u + 1) * 512], rst[k], rst[k], ALU.mult, reads=['rst%d' % k],
                       writes=['aT%d' % ws])
            for j in range(NT_OWN):
                for half in range(2):
                    bk = 4 + di % 4
                    di += 1
                    for q in range(4):
                        MM(banks[bk][:, :], a3[:, q, j * 128:(j + 1) * 128], wd3[:, q, half * 512:(half + 1) * 512],
                           start=(q == 0), stop=(q == 3), reads=['aT%d' % ws, 'Wdn%d' % ws], writes=[B[bk]])
                    xsl = x1[:, j * 1024 + half * 512:j * 1024 + (half + 1) * 512]
                    TT('dve', xsl, xsl, banks[bk][:, :], ALU.add, reads=[B[bk], 'x1_%d_%d' % (j, half)],
                       writes=['x1_%d_%d' % (j, half)])
                if f == 7:
                    DMA('sp', out_d[j * 128:(j + 1) * 128, :], x1[:, j * 1024:(j + 1) * 1024], 'out%d' % (j % 4),
                        reads=['x1_%d_0' % j, 'x1_%d_1' % j])
        P.emit(st)
        info = {"n_sems": P.n_sems, "max_semval": P.max_semval,
                "n_ops": {e: len(P.ops[e]) for e in P.ENG}}
    return nc, dbg, info


def _consts(inputs, flag):
    i = np.arange(128)
    same = np.ones((128, 128), bool)
    uinc = (same & (i[:, None] <= i[None, :])).astype(np.float32)
    lst = (same & (i[:, None] > i[None, :])).astype(np.float32)
    cf = np.zeros((128, NCF), np.float32)
    cf[:, CF_UINC:CF_UINC + 128] = uinc
    cf[:, CF_LST:CF_LST + 128] = lst
    cf[:, CF_CIND:CF_CIND + 2] = 1.0
    cf[:, CF_G1:CF_G1 + 8] = inputs['norm1_g'].reshape(8, 128).T
    cf[:, CF_G2:CF_G2 + 8] = inputs['norm2_g'].reshape(8, 128).T
    cf[:, CF_GB:CF_GB + 16] = inputs['branch_gate_bias'].reshape(16, 128).T
    cf[:, CF_GQ] = np.tile(inputs['attn_q_norm_g'].reshape(64), 2)
    cf[:, CF_GK] = np.tile(inputs['attn_k_norm_g'].reshape(64), 2)
    cf[:, CF_GOUT:CF_GOUT + 256] = np.broadcast_to(inputs['gla_out_norm_g'].reshape(1, 256), (128, 256))
    cf[:, CF_EPS] = EPS
    cf[:, CF_ONE] = 1.0
    cb = np.zeros((128, NCB), np.float32)
    cb[:, CB_ID:CB_ID + 128] = np.eye(128, dtype=np.float32)
    prev = (i[:, None] >= i[None, :]).astype(np.float32)
    cur = (i[:, None] <= i[None, :]).astype(np.float32)
    cb[:, CB_M4:CB_M4 + 512] = np.concatenate([prev, cur, prev, cur], 1)
    cb[:, CB_M4H0:CB_M4H0 + 512] = np.concatenate([prev * flag, cur, prev, cur], 1)
    cb[:, CB_M4HH:CB_M4HH + 512] = np.concatenate([prev * flag, cur, prev * flag, cur], 1)
    cb[:, CB_GM:CB_GM + 128] = uinc
    cb[:64, CB_BONES:CB_BONES + 64] = 1.0 / 64
    cb[64:, CB_BONES + 64:CB_BONES + 128] = 1.0 / 64
    cb[:, CB_ONES:CB_ONES + 64] = 1.0
    return cf, cb


_CACHE = {}


def kernel(**inputs):
    inputs = {k: np.asarray(v, dtype=np.float32) for k, v in inputs.items()}
    x = inputs['x']
    if 'nc' not in _CACHE:
        _CACHE['nc'] = build_program()
    nc, dbg, info = _CACHE['nc']
    w_in = np.ascontiguousarray(inputs['w_in'].reshape(D, DIN))
    shared = {
        "w_in": w_in,
        "w_a": np.ascontiguousarray(inputs['w_attn_branch'].reshape(512, D)),
        "w_b": np.ascontiguousarray(inputs['w_gla_branch'].reshape(D, D)),
        "w_o": np.ascontiguousarray(inputs['w_out'].reshape(D, D)),
        "w_up": np.ascontiguousarray(inputs['w_ff_up'].reshape(D, 4 * D)),
        "w_dn": np.ascontiguousarray(inputs['w_ff_down'].reshape(4 * D, D)),
        "gup": np.ascontiguousarray(np.concatenate([inputs['gla_gate_up'].reshape(16, 512),
                                                    inputs['gla_gate_bias'].reshape(1, 512)], 0)),
    }
    in_maps = []
    for c in range(8):
        b, ch = c // 4, c % 4
        xi = np.zeros((NPRE + NTOK, D), np.float32)
        npre = ch * NTOK
        if npre:
            xi[NPRE - npre:NPRE] = x[b, 0:npre]
        xi[NPRE:] = x[b, ch * NTOK:(ch + 1) * NTOK]
        cf, cb = _consts(inputs, 0.0 if ch == 0 else 1.0)
        m = dict(shared)
        m.update({"xin": xi, "cf": cf, "cb": cb})
        in_maps.append(m)
    res = run_bass_kernel_spmd(nc, in_maps, core_ids=list(range(8)))
    out = np.zeros((2, SEQ, D), np.float32)
    for c in range(8):
        b, ch = c // 4, c % 4
        out[b, ch * NTOK:(ch + 1) * NTOK] = res.results[c]["out"]
    if DEBUG:
        _CACHE['last'] = (res, dbg)
    return out
```

```python
import os
import bisect
import numpy as np
import concourse.bass as bass
import concourse.mybir as mybir
from concourse.bass_utils import run_bass_kernel_spmd
from contextlib import ExitStack

F32 = mybir.dt.float32
BF16 = mybir.dt.bfloat16
AF = mybir.ActivationFunctionType
ALU = mybir.AluOpType

D = 1024
SEQ = 8192
NTOK = 2048
NPRE = 6144
NT_PRE = NPRE // 128
NT_OWN = NTOK // 128
DIN = 9744
EPS = 1e-6
O_AQ, O_AK, O_AV = 0, 1536, 3072
O_GQ, O_GK, O_GV, O_GR, O_GA, O_GATE = 4608, 5120, 5632, 6656, 7680, 7696
DIL = (1, 4, 16)

CF_UINC, CF_LST, CF_CIND, CF_G1, CF_G2, CF_GB, CF_GQ, CF_GK, CF_GOUT, CF_EPS, CF_ONE = \
    0, 128, 256, 258, 266, 274, 290, 291, 292, 548, 549
NCF = 552
CB_ID, CB_M4, CB_M4H0, CB_M4HH, CB_GM, CB_BONES, CB_ONES = 0, 128, 640, 1152, 1664, 1792, 1920
NCB = 1984

DEBUG = bool(os.environ.get("KDEBUG"))


class Prog:
    ENG = ('pe', 'act', 'dve', 'pool', 'sp')

    def __init__(self, nc):
        self.nc = nc
        self.ops = {e: [] for e in self.ENG}
        self.lastw = {}
        self.readers = {}
        self.seen = {e: {} for e in self.ENG}
        self.dma_cnt = {}

    def add(self, eng, fn, reads=(), writes=(), dma_key=None):
        bank_r = [k for k in reads if len(k) >= 2 and k[0] == 'b' and k[1].isdigit()]
        if bank_r:
            reads = [k for k in reads if k not in bank_r]
            writes = list(writes) + bank_r
        deps = set()
        for k in reads:
            t = self.lastw.get(k)
            if t is not None:
                deps.add(t)
        for k in writes:
            t = self.lastw.get(k)
            if t is not None:
                deps.add(t)
            for t in self.readers.get(k, ()):
                deps.add(t)
        if dma_key is None:
            tok = (eng, len(self.ops[eng]))
        else:
            self.dma_cnt[dma_key] = self.dma_cnt.get(dma_key, 0) + 1
            tok = (('dma', dma_key), self.dma_cnt[dma_key])
        waits = {}
        for (s, v) in deps:
            if s == eng and eng == 'pe':
                continue
            if self.seen[eng].get(s, -1) >= v:
                continue
            waits[s] = max(waits.get(s, -1), v)
        for s, v in waits.items():
            self.seen[eng][s] = v
        self.ops[eng].append([waits, fn, tok, dma_key])
        for k in writes:
            self.lastw[k] = tok
            self.readers[k] = []
        for k in reads:
            self.readers.setdefault(k, []).append(tok)
        return tok

    def barrier(self):
        last = {}
        for e in self.ENG:
            n = [i for i, o in enumerate(self.ops[e]) if o[3] is None and o[1] is not None]
            if n:
                last[e] = n[-1]
        dm = dict(self.dma_cnt)
        for e in self.ENG:
            waits = {}
            for s, v in last.items():
                if s != e and self.seen[e].get(s, -1) < v:
                    waits[s] = v
            for dk, c in dm.items():
                s = ('dma', dk)
                if self.seen[e].get(s, -1) < c:
                    waits[s] = c
            for s, v in waits.items():
                self.seen[e][s] = v
            self.ops[e].append([waits, None, None, None])
        self.lastw = {}
        self.readers = {}

    def emit(self, stack, final_eng='sp'):
        nc = self.nc
        self.barrier()
        needed = {e: set() for e in self.ENG}
        for e in self.ENG:
            for waits, fn, tok, dk in self.ops[e]:
                for s, v in waits.items():
                    if not isinstance(s, tuple):
                        needed[s].add(v)
        semval = {}
        for e in self.ENG:
            for c, i in enumerate(sorted(needed[e])):
                semval[(e, i)] = c + 1
        sems = {}
        for e in self.ENG:
            sems[e] = stack.enter_context(nc.semaphore("sem_" + e))
        for dk in self.dma_cnt:
            sems[('dma', dk)] = stack.enter_context(nc.semaphore("dsem_%d" % len(sems)))
        self.n_sems = len(sems)
        self.max_semval = max(list(semval.values()) + [0])
        block = stack.enter_context(nc.Block())
        engmap = {'pe': block.tensor, 'act': block.scalar, 'dve': block.vector,
                  'pool': block.gpsimd, 'sp': block.sync}

        def run(ename, eobj):
            for idx, (waits, fn, tok, dk) in enumerate(self.ops[ename]):
                for s, v in waits.items():
                    if isinstance(s, tuple):
                        eobj.wait_ge(sems[s], 16 * v)
                    else:
                        eobj.wait_ge(sems[s], semval[(s, v)])
                if fn is None:
                    continue
                ins = fn(eobj)
                if dk is not None:
                    ins.then_inc(sems[tok[0]], 16)
                elif (ename, idx) in semval:
                    ins.then_inc(sems[ename], 1)

        for ename in self.ENG:
            def mk(ename):
                def f(eobj):
                    run(ename, eobj)
                return f
            engmap[ename](mk(ename))


class Region:
    def __init__(self, t, n):
        self.t = t
        self.n = n
        self.off = 0

    def reset(self):
        self.off = 0

    def bf(self, n):
        n2 = (n + 15) // 16 * 16
        assert self.off + n2 <= self.n, (self.off, n2, self.n)
        ap = self.t[:, self.off:self.off + n]
        self.off += n2
        return ap

    def f32(self, n):
        return self.bf(2 * n).bitcast(F32)


def build_program():
    nc = bass.Bass("TRN2", target_bir_lowering=False)

    def dram(name, shape, kind="ExternalInput", dt=F32):
        return nc.dram_tensor(name, list(shape), dt, kind=kind).ap()

    xin = dram("xin", [NPRE + NTOK, D])
    w_in = dram("w_in", [D, DIN])
    w_a = dram("w_a", [512, D])
    w_b = dram("w_b", [D, D])
    w_o = dram("w_o", [D, D])
    w_up = dram("w_up", [D, 4 * D])
    w_dn = dram("w_dn", [4 * D, D])
    cf_d = dram("cf", [128, NCF])
    cb_d = dram("cb", [128, NCB])
    gup_d = dram("gup", [17, 512])
    out_d = dram("out", [NTOK, D], kind="ExternalOutput")
    dbg = {}

    with ExitStack() as st:
        def sb(name, shape, dt):
            return st.enter_context(nc.sbuf_tensor(name, list(shape), dt))

        hTo = sb("hTo", [128, 8 * NTOK], BF16)
        hTh = sb("hTh", [128, 8 * NTOK], BF16)
        ogT = sb("ogT", [128, 8 * NTOK], BF16)
        oaT = sb("oaT", [128, 4 * NTOK], BF16)
        arena = sb("arena", [128, 32768], BF16)
        S = sb("S", [128, 1024], F32)
        Sbf = sb("Sbf", [128, 1024], BF16)
        cf = sb("cfs", [128, NCF], F32)
        cb = sb("cbs", [128, NCB], BF16)
        gup = sb("gups", [17, 512], F32)
        paT = sb("paT", [17, 256], F32)
        sm = sb("smalls", [128, 64], F32)
        banks = [st.enter_context(nc.psum_tensor("bank%d" % i, [128, 512], F32)) for i in range(8)]

        hTo3 = hTo[:, :].rearrange("p (c t) -> p c t", c=8)
        hTh3 = hTh[:, :].rearrange("p (c t) -> p c t", c=8)
        ogT3 = ogT[:, :].rearrange("p (c t) -> p c t", c=8)
        oaT3 = oaT[:, :].rearrange("p (c t) -> p c t", c=4)
        RA = Region(arena, 32768)
        RG = Region(ogT, 8 * NTOK)
        RH = Region(hTh, 8 * NTOK)
        RO = Region(hTo, 8 * NTOK)
        RT = Region(oaT, 4 * NTOK)

        P = Prog(nc)
        B = ['b%d' % i for i in range(8)]

        def MM(out, lhsT, rhs, start=True, stop=True, reads=(), writes=(), skip=False):
            if skip:
                P.add('pe', lambda e: e.matmul(out, lhsT, rhs, start=start, stop=stop, skip_group_check=True),
                      reads, writes)
            else:
                P.add('pe', lambda e: e.matmul(out, lhsT, rhs, start=start, stop=stop), reads, writes)

        def TR(out, in_, reads=(), writes=()):
            ident = cb[:, CB_ID:CB_ID + 128]
            P.add('pe', lambda e: e.transpose(out, in_, ident), reads, writes)

        def ACT(out, in_, func, reads=(), writes=(), scale=None, bias=None, accum_out=None):
            kw = {}
            if scale is not None:
                kw['scale'] = scale
            if bias is not None:
                kw['bias'] = bias
            if accum_out is not None:
                kw['accum_out'] = accum_out
            P.add('act', lambda e: e.activation(out=out, in_=in_, func=func, **kw), reads, writes)

        def TT(eng, out, in0, in1, op, reads=(), writes=()):
            P.add(eng, lambda e: e.tensor_tensor(out=out, in0=in0, in1=in1, op=op), reads, writes)

        def TS(eng, out, in0, s1, op0, reads=(), writes=(), s2=None, op1=None):
            if op1 is None:
                P.add(eng, lambda e: e.tensor_scalar(out=out, in0=in0, scalar1=s1, scalar2=None, op0=op0),
                      reads, writes)
            else:
                P.add(eng, lambda e: e.tensor_scalar(out=out, in0=in0, scalar1=s1, scalar2=s2, op0=op0, op1=op1),
                      reads, writes)

        def STT(out, in0, scalar, in1, op0, op1, reads=(), writes=()):
            P.add('dve', lambda e: e.scalar_tensor_tensor(out=out, in0=in0, scalar=scalar, in1=in1,
                                                          op0=op0, op1=op1), reads, writes)

        def CP(eng, out, in_, reads=(), writes=()):
            if eng == 'act':
                ACT(out, in_, AF.Copy, reads, writes)
            else:
                P.add(eng, lambda e: e.tensor_copy(out=out, in_=in_), reads, writes)

        def MSET(eng, ap, val, writes=()):
            P.add(eng, lambda e: e.memset(ap, val), (), writes)

        def DMA(eng, out, in_, key, reads=(), writes=()):
            P.add(eng, lambda e: e.dma_start(out=out, in_=in_), reads, writes, dma_key=key)

        def wcols(w, c0, n):
            return w[:, c0:c0 + n].rearrange("(c p) n -> p c n", p=128)

        def v3(ap, c):
            return ap.rearrange("p (c n) -> p c n", c=c)

        def dump(name, ap, shape, key_reads=()):
            if not DEBUG:
                return
            d = dram("dbg_" + name, shape, kind="ExternalOutput", dt=ap.dtype)
            dbg[name] = d
            DMA('sp', d, ap, 'dbg_' + name, reads=key_reads)

        eps_ap = cf[:, CF_EPS:CF_EPS + 1]
        one_ap = cf[:, CF_ONE:CF_ONE + 1]

        def rstd_from_sum(out_ap, in_ap, inv_n, key_in, key_out, tmpkey):
            ACT(out_ap, in_ap, AF.Ln, reads=[key_in], writes=[key_out], scale=inv_n, bias=eps_ap)
            ACT(out_ap, out_ap, AF.Exp, reads=[key_out], writes=[key_out], scale=-0.5)

        DMA('sp', cf[:, :], cf_d, 'cf', writes=['cf'])
        DMA('pool', cb[:, :], cb_d, 'cb', writes=['cb'])
        DMA('sp', gup[:, :], gup_d, 'gup', writes=['gup'])
        MSET('dve', paT[:, :], 1.0, writes=['paT0', 'paT1'])
        MSET('dve', S[:, :], 0.0, writes=['S0', 'S1', 'S2', 'S3'])
        MSET('pool', Sbf[:, :], 0.0, writes=['Sbf0', 'Sbf1', 'Sbf2', 'Sbf3'])
        P.barrier()

        RA.reset()
        Wk_tok = RA.bf(8 * 512)
        Wv = RA.bf(8 * 1024)
        Wpa = RA.bf(8 * 16)
        Wk3, Wv3, Wpa3 = v3(Wk_tok, 8), v3(Wv, 8), v3(Wpa, 8)
        xt = [RA.f32(1024) for _ in range(2)]
        xs = [RA.bf(1024) for _ in range(2)]
        junk = RA.bf(1024)
        hTt = [RA.bf(1024) for _ in range(2)]
        la = [RA.f32(512) for _ in range(2)]
        e1 = RA.f32(512)
        ekd = RA.f32(512)
        kd = [RA.bf(512) for _ in range(2)]
        vbf = [RA.bf(1024) for _ in range(2)]

        DMA('pool', Wk3, wcols(w_in, O_GK, 512), 'Wk', writes=['Wk'])
        DMA('pool', Wv3, wcols(w_in, O_GV, 1024), 'Wv', writes=['Wv'])
        DMA('pool', Wpa3, wcols(w_in, O_GA, 16), 'Wpa', writes=['Wpa'])

        g1b = bass.AP(cf, CF_G1, [[NCF, 128], [1, 8], [0, 128]])
        g2b = bass.AP(cf, CF_G2, [[NCF, 128], [1, 8], [0, 128]])
        lst_ap = cf[:, CF_LST:CF_LST + 128]
        uinc_ap = cf[:, CF_UINC:CF_UINC + 128]
        cind_ap = cf[:, CF_CIND:CF_CIND + 2]

        def norm_a(src_rows, s, xkey=None):
            ss = sm[:, s:s + 1]
            rs = sm[:, 2 + s:3 + s]
            if xkey is None:
                DMA('sp', xt[s], src_rows, 'xt%d' % s, writes=['xt%d' % s])
                xap, xk = xt[s], 'xt%d' % s
            else:
                xap, xk = src_rows, xkey
            ACT(junk, xap, AF.Square, reads=[xk], writes=['junk', 'ss%d' % s], accum_out=ss)
            rstd_from_sum(rs, ss, 1.0 / D, 'ss%d' % s, 'rs%d' % s, 'rst%d' % s)
            TS('dve', xs[s], xap, rs, ALU.mult, reads=[xk, 'rs%d' % s], writes=['xs%d' % s])

        def norm_b(s, dst3, gb):
            pb = banks[0][:, :].bitcast(BF16)
            for c in range(8):
                TR(pb[:, c * 128:(c + 1) * 128], xs[s][:, c * 128:(c + 1) * 128], reads=['xs%d' % s], writes=[B[0]])
            TT('dve', dst3, v3(pb, 8), gb, ALU.mult, reads=[B[0]], writes=['hT%d' % s])

        def norm_transpose(src_rows, s, dst3, gb, xkey=None):
            norm_a(src_rows, s, xkey)
            norm_b(s, dst3, gb)

        def gla_stage2a(hT3, s, own):
            hk = 'hT%d' % s
            for c in range(8):
                MM(banks[4][0:16, 0:128], Wpa3[:, c, :], hT3[:, c, :], start=(c == 0), stop=(c == 7),
                   reads=[hk, 'Wpa'], writes=[B[4]])
            CP('act', paT[0:16, s * 128:(s + 1) * 128], banks[4][0:16, 0:128], reads=[B[4]], writes=['paT%d' % s])
            for c in range(8):
                MM(banks[1][:, :], hT3[:, c, :], Wk3[:, c, :], start=(c == 0), stop=(c == 7),
                   reads=[hk, 'Wk'], writes=[B[1]])
            if own:
                for c in range(8):
                    MM(banks[6][:, :], hT3[:, c, :], Wq3[:, c, :], start=(c == 0), stop=(c == 7),
                       reads=[hk, 'Wq_g'], writes=[B[6]])
            MM(banks[5][:, :], paT[0:17, s * 128:(s + 1) * 128], gup[0:17, :], reads=['paT%d' % s, 'gup'],
               writes=[B[5]])
            ACT(e1, banks[5][:, :], AF.Exp, reads=[B[5]], writes=['e1'], scale=-1.0)
            ACT(la[s], e1, AF.Ln, reads=['e1'], writes=['la%d' % s], bias=one_ap)
            for half in range(2):
                for c in range(8):
                    MM(banks[2 + half][:, :], hT3[:, c, :], Wv3[:, c, half * 512:(half + 1) * 512],
                       start=(c == 0), stop=(c == 7), reads=[hk, 'Wv'], writes=[B[2 + half]])
            CP('act', vbf[s][:, 0:512], banks[2][:, :], reads=[B[2]], writes=['vbfa%d' % s])
            CP('act', vbf[s][:, 512:1024], banks[3][:, :], reads=[B[3]], writes=['vbfb%d' % s])
            return None

        def gla_stage2b(s, own):
            MM(banks[5][:, :], lst_ap, la[s], reads=['la%d' % s], writes=[B[5]])
            for h in range(4):
                MM(banks[4][:, 128 + 2 * h:130 + 2 * h], la[s][:, h * 128:(h + 1) * 128], cind_ap,
                   reads=['la%d' % s], writes=[B[4]])
            ACT(ekd, banks[5][:, :], AF.Exp, reads=[B[5]], writes=['ekd'], scale=-1.0 / 16)
            ebl = sm[:, 8 + 8 * s:16 + 8 * s]
            ACT(ebl, banks[4][:, 128:136], AF.Exp, reads=[B[4]], writes=['ebl%d' % s], scale=-1.0 / 16)
            TT('dve', kd[s], banks[1][:, :], ekd, ALU.mult, reads=[B[1], 'ekd'], writes=['kd%d' % s])
            return ebl

        def gla_stage2(hT3, s, own):
            gla_stage2a(hT3, s, own)
            return gla_stage2b(s, own)

        def gla_stage3(s, own, ubanks=(6, 7)):
            ebl = sm[:, 8 + 8 * s:16 + 8 * s]
            for h in range(4):
                ub = ubanks[0] if h < 2 else ubanks[1]
                MM(banks[ub][:, (h % 2) * 256:(h % 2) * 256 + 256], kd[s][:, h * 128:(h + 1) * 128],
                   vbf[s][:, h * 256:(h + 1) * 256],
                   reads=['kd%d' % s, 'vbfa%d' % s if h < 2 else 'vbfb%d' % s], writes=[B[ub]])
            for h in range(4):
                ub = ubanks[0] if h < 2 else ubanks[1]
                Sh = S[:, h * 256:(h + 1) * 256]
                STT(Sh, Sh, ebl[:, 2 * h:2 * h + 1], banks[ub][:, (h % 2) * 256:(h % 2) * 256 + 256], ALU.mult, ALU.add,
                    reads=['S%d' % h, 'ebl%d' % s, B[ub]], writes=['S%d' % h])
                if own:
                    CP('act', Sbf[:, h * 256:(h + 1) * 256], Sh, reads=['S%d' % h], writes=['Sbf%d' % h])

        def tile_dst(ti):
            if ti >= NT_PRE:
                j = ti - NT_PRE
                return hTo3[:, :, j * 128:(j + 1) * 128]
            if ti >= NT_PRE - NT_OWN:
                j = ti - (NT_PRE - NT_OWN)
                return hTh3[:, :, j * 128:(j + 1) * 128]
            return v3(hTt[ti % 2], 8)

        NT_ALL = NT_PRE + NT_OWN
        norm_transpose(xin[0:128, :], 0, tile_dst(0), g1b)
        for ti in range(NT_PRE):
            s = ti % 2
            if ti + 1 < NT_ALL:
                norm_a(xin[(ti + 1) * 128:(ti + 2) * 128, :], (ti + 1) % 2)
            gla_stage2a(tile_dst(ti), s, False)
            if ti + 1 < NT_ALL:
                norm_b((ti + 1) % 2, tile_dst(ti + 1), g1b)
            gla_stage2b(s, False)
            if ti >= 1:
                gla_stage3((ti - 1) % 2, False)
        gla_stage3((NT_PRE - 1) % 2, False)
        norm_a(xin[(NT_PRE + 1) * 128:(NT_PRE + 2) * 128, :], (NT_PRE + 1) % 2)
        for ti in range(NT_PRE + 1, NT_ALL):
            if ti + 1 < NT_ALL:
                norm_a(xin[(ti + 1) * 128:(ti + 2) * 128, :], (ti + 1) % 2)
            norm_b(ti % 2, tile_dst(ti), g1b)
        P.barrier()
        dump("hTo", hTo[:, :], [128, 8 * NTOK])
        dump("hTh", hTh[:, :], [128, 8 * NTOK])
        dump("Spre", S[:, :], [128, 1024])
        for h in range(4):
            CP('act', Sbf[:, h * 256:(h + 1) * 256], S[:, h * 256:(h + 1) * 256])
        P.barrier()

        RA.reset()
        RG.reset()
        Wq_a = [RA.bf(1024) for _ in range(3)]
        Wk_a = [RA.bf(1024) for _ in range(3)]
        QT = [RA.bf(NTOK) for _ in range(3)]
        HALO = [128, 512, 2048]
        KT = [RA.bf(HALO[g] + NTOK) for g in range(3)]
        NBLK = [17, 20, 32]
        Vb = [RA.bf(NBLK[g] * 128) for g in range(3)]
        Wv_a = [RG.bf(1024) for _ in range(3)]
        PT3h = [RG.bf(16 * 256) for _ in range(2)]
        ptb = [RG.bf(512) for _ in range(4)]
        sqb = [RG.bf(512) for _ in range(2)]
        lnb = [RG.f32(512) for _ in range(2)]
        lnd = [RA.f32(512) for _ in range(2)]
        gq_ap = cf[:, CF_GQ:CF_GQ + 1]
        gk_ap = cf[:, CF_GK:CF_GK + 1]
        bones = cb[:, CB_BONES:CB_BONES + 128]
        ones64 = cb[:, CB_ONES:CB_ONES + 64]
        m4 = {0: cb[:, CB_M4:CB_M4 + 512], 1: cb[:, CB_M4H0:CB_M4H0 + 512], 2: cb[:, CB_M4HH:CB_M4HH + 512]}
        cnt = {'qk': 0, 'pt': 0, 'rnd': 0, 'sc': 0, 'vb': 0}

        def qk_tile(wap, hsrc3, t0, n, dst, gain, wkey):
            i = cnt['qk']
            cnt['qk'] += 1
            pbk = (0, 5)[i % 2]
            pk = B[pbk]
            stb = 3 + (i % 2)
            w3 = v3(wap, 8)
            for c in range(8):
                MM(banks[pbk][:, 0:n], w3[:, c, :], hsrc3[:, c, t0:t0 + n], start=(c == 0), stop=(c == 7),
                   reads=[wkey], writes=[pk])
            sq = sqb[i % 2]
            ACT(sq[:, 0:n], banks[pbk][:, 0:n], AF.Square, reads=[pk], writes=['sq%d' % (i % 2)])
            MM(banks[stb][:, 0:n], bones, sq[:, 0:n], reads=['sq%d' % (i % 2)], writes=[B[stb]])
            ln = lnb[i % 2]
            ACT(ln[:, 0:n], banks[stb][:, 0:n], AF.Ln, reads=[B[stb]], writes=['ln%d' % (i % 2)], bias=eps_ap)
            ACT(ln[:, 0:n], ln[:, 0:n], AF.Exp, reads=['ln%d' % (i % 2)], writes=['ln%d' % (i % 2)], scale=-0.5)
            STT(dst, banks[pbk][:, 0:n], gain, ln[:, 0:n], ALU.mult, ALU.mult,
                reads=[pk, 'ln%d' % (i % 2)], writes=['qkdst'])

        for p in range(4):
            for g in range(3):
                hc = (g * 8 + 2 * p) * 64
                DMA('pool', v3(Wq_a[g], 8), wcols(w_in, O_AQ + hc, 128), 'Wq_a%d' % g, writes=['Wq_a%d' % g])
                DMA('pool', v3(Wk_a[g], 8), wcols(w_in, O_AK + hc, 128), 'Wk_a%d' % g, writes=['Wk_a%d' % g])
                DMA('pool', v3(Wv_a[g], 8), wcols(w_in, O_AV + hc, 128), 'Wv_a%d' % g, writes=['Wv_a%d' % g])
            for g in range(3):
                d = DIL[g]
                for u in range(4):
                    qk_tile(Wq_a[g], hTo3, u * 512, 512, QT[g][:, u * 512:(u + 1) * 512], gq_ap, 'Wq_a%d' % g)
                hl = HALO[g]
                nsp = max(1, hl // 512)
                w = hl // nsp
                for u in range(nsp):
                    qk_tile(Wk_a[g], hTh3, NTOK - hl + u * w, w, KT[g][:, u * w:(u + 1) * w], gk_ap, 'Wk_a%d' % g)
                for u in range(4):
                    qk_tile(Wk_a[g], hTo3, u * 512, 512, KT[g][:, hl + u * 512:hl + (u + 1) * 512], gk_ap,
                            'Wk_a%d' % g)
                seg = 128 * d
                wv3 = v3(Wv_a[g], 8)
                for b0 in range(0, NBLK[g], 4):
                    nb = min(4, NBLK[g] - b0)
                    vi = cnt['vb']
                    cnt['vb'] += 1
                    vbank = 3 + (vi % 2)
                    for bi in range(nb):
                        blk = b0 + bi
                        n_, r_ = blk // d - 1, blk % d
                        t0 = n_ * seg + r_
                        if t0 < 0:
                            src3, tt0 = hTh3, NTOK + t0
                        else:
                            src3, tt0 = hTo3, t0
                        for c in range(8):
                            MM(banks[vbank][:, bi * 128:(bi + 1) * 128],
                               src3[:, c, tt0:tt0 + 127 * d + 1:d], wv3[:, c, :],
                               start=(c == 0), stop=(c == 7), reads=['Wv_a%d' % g], writes=[B[vbank]])
                    CP('act' if vi % 2 == 0 else 'dve', Vb[g][:, b0 * 128:(b0 + nb) * 128],
                       banks[vbank][:, 0:nb * 128], reads=[B[vbank]], writes=['Vb'])
            nbk, dbk = (1, 2)
            numb, denb = banks[nbk], banks[dbk]
            SB_ROT = (6, 7, 0, 5)
            items = []

            def mk_score(hd, g, blks, mkind, rec, fixed=None):
                def f():
                    d = DIL[g]
                    seg = 128 * d
                    hl = HALO[g]
                    po = hd * 64
                    si = cnt['sc']
                    cnt['sc'] += 1
                    sbank = SB_ROT[si % 4]
                    if fixed is None:
                        i = cnt['pt']
                        cnt['pt'] += 1
                        dst_pt, dst_key = ptb[i % 4], 'pt%d' % (i % 4)
                    else:
                        dst_pt, dst_key = fixed
                    rec['pt'], rec['key'] = dst_pt, dst_key
                    for bi, (n_, r_) in enumerate(blks):
                        t0 = n_ * seg + r_
                        qap = QT[g][po:po + 64, t0:t0 + 127 * d + 1:d]
                        kprev = KT[g][po:po + 64, hl + t0 - seg:hl + t0 - seg + 127 * d + 1:d]
                        kcur = KT[g][po:po + 64, hl + t0:hl + t0 + 127 * d + 1:d]
                        MM(banks[sbank][:, bi * 256:bi * 256 + 128], kprev, qap, reads=['qkdst'], writes=[B[sbank]])
                        MM(banks[sbank][:, bi * 256 + 128:bi * 256 + 256], kcur, qap, reads=['qkdst'],
                           writes=[B[sbank]])
                    ACT(dst_pt[:, 0:512], banks[sbank][:, 0:512], AF.Exp, reads=[B[sbank]], writes=[dst_key],
                        scale=0.125)
                    TT('dve', dst_pt[:, 0:512], dst_pt[:, 0:512], m4[mkind][:, 0:512], ALU.mult,
                       reads=[dst_key], writes=[dst_key])
                return f

            def mk_pv(hd, g, R, rec, half, first):
                def f():
                    po = hd * 64
                    pt, key = rec['pt'], rec['key']
                    pt3 = v3(pt, 2)

                    def num(kb, mov, cols):
                        stf = first['n%d' % hd]
                        first['n%d' % hd] = False
                        MM(numb[po:po + 64, cols], Vb[g][:, kb * 128 + hd * 64:kb * 128 + hd * 64 + 64], mov,
                           start=stf, stop=False, reads=['Vb', key], writes=[B[nbk] + str(hd)], skip=True)

                    def den(out_ap, mov):
                        stf = first['d%d' % hd]
                        first['d%d' % hd] = False
                        MM(out_ap, ones64, mov, start=stf, stop=False, reads=[key], writes=[B[dbk] + str(hd)],
                           skip=True)
                    if g == 0:
                        n0 = 4 * R + 2 * half
                        for bi in range(2):
                            n_ = n0 + bi
                            cols = slice((n_ - 4 * R) * 128, (n_ - 4 * R) * 128 + 128)
                            num(n_, pt[:, bi * 256:bi * 256 + 128], cols)
                            num(n_ + 1, pt[:, bi * 256 + 128:bi * 256 + 256], cols)
                        for bi in range(2):
                            n_ = n0 + bi
                            cols = slice((n_ - 4 * R) * 128, (n_ - 4 * R) * 128 + 128)
                            den(denb[po:po + 64, cols], pt[:, bi * 256:bi * 256 + 128])
                            den(denb[po:po + 64, cols], pt[:, bi * 256 + 128:bi * 256 + 256])
                    else:
                        for bi in range(2):
                            r_ = 2 * half + bi
                            cols = slice(r_, 512, 4)
                            blk = (R + 1) * 4 + r_
                            num(blk - 4, pt[:, bi * 256:bi * 256 + 128], cols)
                            num(blk, pt[:, bi * 256 + 128:bi * 256 + 256], cols)
                        for bi in range(2):
                            r_ = 2 * half + bi
                            den(denb[po:po + 64, r_:512:4], pt[:, bi * 256:bi * 256 + 128])
                            den(denb[po:po + 64, r_:512:4], pt[:, bi * 256 + 128:bi * 256 + 256])
                return f

            def mk_pv3(hd, R, first):
                def f():
                    po = hd * 64
                    keys = ['PT3_%d_%d' % (hd, rp) for rp in range(8)]
                    for r_ in range(16):
                        cols = slice(r_, 512, 16)
                        base = r_ * 256
                        for part, kb in ((0, r_), (1, 16 + r_)):
                            stf = first['n%d' % hd]
                            first['n%d' % hd] = False
                            MM(numb[po:po + 64, cols], Vb[2][:, kb * 128 + hd * 64:kb * 128 + hd * 64 + 64],
                               PT3h[hd][:, base + part * 128 + 32 * R:base + part * 128 + 32 * R + 32],
                               start=stf, stop=False, reads=['Vb', 'PT3_%d_%d' % (hd, r_ // 2)],
                               writes=[B[nbk] + str(hd)], skip=True)
                    p16 = v3(PT3h[hd], 16)
                    for r_ in range(16):
                        cols = slice(r_, 512, 16)
                        base = r_ * 256
                        for part in range(2):
                            stf = first['d%d' % hd]
                            first['d%d' % hd] = False
                            MM(denb[po:po + 64, cols], ones64,
                               PT3h[hd][:, base + part * 128 + 32 * R:base + part * 128 + 32 * R + 32],
                               start=stf, stop=False, reads=['PT3_%d_%d' % (hd, r_ // 2)],
                               writes=[B[dbk] + str(hd)], skip=True)
                return f

            def mk_norm(p, R):
                def f():
                    rn = cnt['rnd']
                    cnt['rnd'] += 1
                    nk = [B[nbk] + '0', B[nbk] + '1']
                    dk_ = [B[dbk] + '0', B[dbk] + '1']
                    l = lnd[rn % 2]
                    ACT(l, denb[:, :], AF.Ln, reads=dk_, writes=['lnd%d' % (rn % 2)])
                    ACT(l, l, AF.Exp, reads=['lnd%d' % (rn % 2)], writes=['lnd%d' % (rn % 2)], scale=-1.0)
                    TT('dve', oaT3[:, p, R * 512:(R + 1) * 512], numb[:, :], l, ALU.mult,
                       reads=nk + ['lnd%d' % (rn % 2)], writes=['oaT'])
                return f

            for R in range(4):
                first = {'n0': True, 'n1': True, 'd0': True, 'd1': True}
                for hd in range(2):
                    if R == 0:
                        for rp in range(8):
                            items.append((mk_score(hd, 2, [(0, 2 * rp), (0, 2 * rp + 1)], 2, {},
                                                   fixed=(PT3h[hd][:, rp * 512:(rp + 1) * 512],
                                                          'PT3_%d_%d' % (hd, rp))), None))
                    for half in range(2):
                        n0 = 4 * R + 2 * half
                        rec = {}
                        items.append((mk_score(hd, 0, [(n0, 0), (n0 + 1, 0)], 1 if n0 == 0 else 0, rec),
                                      mk_pv(hd, 0, R, rec, half, first)))
                    for half in range(2):
                        rec = {}
                        items.append((mk_score(hd, 1, [(R, 2 * half), (R, 2 * half + 1)], 2 if R == 0 else 0, rec),
                                      mk_pv(hd, 1, R, rec, half, first)))
                    items.append((None, mk_pv3(hd, R, first)))
                items.append((None, mk_norm(p, R)))
            LOOK = 2
            nxt = 0
            for i, (sf, pf) in enumerate(items):
                while nxt <= min(i + LOOK, len(items) - 1):
                    if items[nxt][0] is not None:
                        items[nxt][0]()
                    nxt += 1
                if pf is not None:
                    pf()
        P.barrier()
        dump("oaT", oaT[:, :], [128, 4 * NTOK])
        P.barrier()

        RA.reset()
        RH.reset()
        Wk_tok = RA.bf(8 * 512)
        Wv = RA.bf(8 * 1024)
        Wpa = RA.bf(8 * 16)
        Wk3, Wv3, Wpa3 = v3(Wk_tok, 8), v3(Wv, 8), v3(Wpa, 8)
        la = [RA.f32(512) for _ in range(2)]
        e1 = RA.f32(512)
        ekd = RA.f32(512)
        kd = [RA.bf(512) for _ in range(2)]
        vbf = [RA.bf(1024) for _ in range(2)]
        junk = RA.bf(1024)
        Wq_g = RH.bf(8 * 512)
        Wr_g = RH.bf(8 * 1024)
        Wq3, Wr3 = v3(Wq_g, 8), v3(Wr_g, 8)
        sr = RA.f32(1024)
        eq = RA.f32(512)
        ek = RA.f32(512)
        qe_tok = RA.bf(512)
        ke_tok = RA.bf(512)
        qeT = RA.bf(512)
        keT = RA.bf(512)
        AT = RA.bf(512)
        t1 = RA.f32(1024)
        og = RA.bf(1024)
        gm_b = bass.AP(cb, CB_GM, [[NCB, 128], [0, 4], [1, 128]])
        gout_ap = cf[:, CF_GOUT:CF_GOUT + 256]
        DMA('pool', Wk3, wcols(w_in, O_GK, 512), 'Wk', writes=['Wk'])
        DMA('pool', Wv3, wcols(w_in, O_GV, 1024), 'Wv', writes=['Wv'])
        DMA('pool', Wpa3, wcols(w_in, O_GA, 16), 'Wpa', writes=['Wpa'])
        DMA('pool', Wq3, wcols(w_in, O_GQ, 512), 'Wq_g', writes=['Wq_g'])
        DMA('pool', Wr3, wcols(w_in, O_GR, 1024), 'Wr_g', writes=['Wr_g'])

        ob = {0: 7, 1: 7, 2: 0, 3: 0}
        pb = banks[0][:, :].bitcast(BF16)
        ssh = sm[:, 32:36]
        rsh = sm[:, 36:40]

        def main_part2(j):
            s = j % 2
            hT3 = hTo3[:, :, j * 128:(j + 1) * 128]
            MM(banks[5][:, :], uinc_ap, la[s], reads=['la%d' % s], writes=[B[5]])
            ACT(eq, banks[5][:, :], AF.Exp, reads=[B[5]], writes=['eq'], scale=-1.0 / 16)
            ACT(ek, banks[5][:, :], AF.Exp, reads=[B[5]], writes=['ek'], scale=1.0 / 16)
            STT(qe_tok, banks[6][:, :], float(128 ** -0.5), eq, ALU.mult, ALU.mult, reads=[B[6], 'eq'],
                writes=['qe_tok'])
            TT('dve', ke_tok, banks[1][:, :], ek, ALU.mult, reads=[B[1], 'ek'], writes=['ke_tok'])
            for half in range(2):
                for c in range(8):
                    MM(banks[2 + half][:, :], hT3[:, c, :], Wr3[:, c, half * 512:(half + 1) * 512],
                       start=(c == 0), stop=(c == 7), reads=['Wr_g'], writes=[B[2 + half]])
                ACT(sr[:, half * 512:(half + 1) * 512], banks[2 + half][:, :], AF.Silu,
                    reads=[B[2 + half]], writes=['sr%d' % half])
            for h in range(4):
                TR(pb[:, h * 128:(h + 1) * 128], qe_tok[:, h * 128:(h + 1) * 128], reads=['qe_tok'], writes=[B[0]])
            for h in range(4):
                TR(pb[:, 512 + h * 128:512 + (h + 1) * 128], ke_tok[:, h * 128:(h + 1) * 128], reads=['ke_tok'],
                   writes=[B[0]])
            CP('act', qeT, pb[:, 0:512], reads=[B[0]], writes=['qeT'])
            CP('dve', keT, pb[:, 512:1024], reads=[B[0]], writes=['keT'])
            for h in range(4):
                MM(banks[7][:, h * 128:(h + 1) * 128], keT[:, h * 128:(h + 1) * 128], qeT[:, h * 128:(h + 1) * 128],
                   reads=['keT', 'qeT'], writes=[B[7]])
            TT('dve', v3(AT, 4), v3(banks[7][:, :], 4), gm_b, ALU.mult, reads=[B[7]], writes=['AT'])
            for h in range(4):
                MM(banks[ob[h]][:, (h % 2) * 256:(h % 2) * 256 + 256], AT[:, h * 128:(h + 1) * 128],
                   vbf[s][:, h * 256:(h + 1) * 256], start=(h % 2 == 0), stop=False,
                   reads=['AT', 'vbfa%d' % s if h < 2 else 'vbfb%d' % s], writes=[B[ob[h]]], skip=True)
            for h in range(4):
                MM(banks[ob[h]][:, (h % 2) * 256:(h % 2) * 256 + 256], qeT[:, h * 128:(h + 1) * 128],
                   Sbf[:, h * 256:(h + 1) * 256], start=False, stop=True,
                   reads=['qeT', 'Sbf%d' % h], writes=[B[ob[h]]], skip=True)
            gla_stage3(s, True, ubanks=(6, 1))

        def main_tail_a(j):
            for h in range(4):
                ACT(junk[:, h * 256:(h + 1) * 256], banks[ob[h]][:, (h % 2) * 256:(h % 2) * 256 + 256], AF.Square,
                    reads=[B[ob[h]]], writes=['junk%d' % h, 'ssh%d' % h], accum_out=ssh[:, h:h + 1])
            ACT(rsh, ssh, AF.Ln, reads=['ssh0', 'ssh1', 'ssh2', 'ssh3'], writes=['rsh'], scale=1.0 / 256, bias=eps_ap)
            ACT(rsh, rsh, AF.Exp, reads=['rsh'], writes=['rsh'], scale=-0.5)
            for h in range(4):
                STT(t1[:, h * 256:(h + 1) * 256], banks[ob[h]][:, (h % 2) * 256:(h % 2) * 256 + 256],
                    rsh[:, h:h + 1], gout_ap, ALU.mult, ALU.mult, reads=[B[ob[h]], 'rsh'],
                    writes=['t1_%d' % h])
            TT('pool', og, t1, sr, ALU.mult, reads=['t1_0', 't1_1', 't1_2', 't1_3', 'sr0', 'sr1'], writes=['og'])

        def main_tail_b(j):
            for c in range(8):
                TR(pb[:, c * 128:(c + 1) * 128], og[:, c * 128:(c + 1) * 128], reads=['og'], writes=[B[0]])
            CP('act', ogT3[:, :, j * 128:(j + 1) * 128], v3(pb, 8), reads=[B[0]], writes=['ogT'])

        for j in range(NT_OWN):
            if j >= 1:
                main_tail_a(j - 1)
            gla_stage2(hTo3[:, :, j * 128:(j + 1) * 128], j % 2, True)
            if j >= 1:
                main_tail_b(j - 1)
            main_part2(j)
        main_tail_a(NT_OWN - 1)
        main_tail_b(NT_OWN - 1)
        P.barrier()
        dump("ogT", ogT[:, :], [128, 8 * NTOK])
        P.barrier()

        RA.reset()
        mixT3 = hTh3
        WA = [RA.bf(4 * 128) for _ in range(2)]
        WB = [RA.bf(8 * 128) for _ in range(2)]
        WGA = [RA.bf(8 * 128) for _ in range(2)]
        WGB = [RA.bf(8 * 128) for _ in range(2)]
        sgA = [RA.f32(512) for _ in range(2)]
        sgB = [RA.f32(512) for _ in range(2)]
        tA = [RA.f32(512) for _ in range(2)]
        tB = [RA.f32(512) for _ in range(2)]
        it = 0
        def d1_load(jc):
            ws = jc % 2
            DMA('pool', v3(WA[ws], 4), wcols(w_a, jc * 128, 128), 'WA%d' % ws, writes=['WA%d' % ws])
            DMA('pool', v3(WB[ws], 8), wcols(w_b, jc * 128, 128), 'WB%d' % ws, writes=['WB%d' % ws])
            DMA('pool', v3(WGA[ws], 8), wcols(w_in, O_GATE + jc * 128, 128), 'WGA%d' % ws, writes=['WGA%d' % ws])
            DMA('pool', v3(WGB[ws], 8), wcols(w_in, O_GATE + D + jc * 128, 128), 'WGB%d' % ws, writes=['WGB%d' % ws])

        d1_load(0)
        for jc in range(8):
            ws = jc % 2
            wa3, wb3, wga3, wgb3 = v3(WA[ws], 4), v3(WB[ws], 8), v3(WGA[ws], 8), v3(WGB[ws], 8)
            if jc + 1 < 8:
                d1_load(jc + 1)
            for R in range(4):
                k = it % 2
                it += 1
                ts_ = slice(R * 512, (R + 1) * 512)
                bA, bB, bGA, bGB = (0 + 4 * k, 1 + 4 * k, 2 + 4 * k, 3 + 4 * k)
                for c in range(4):
                    MM(banks[bA][:, :], wa3[:, c, :], oaT3[:, c, ts_], start=(c == 0), stop=(c == 3),
                       reads=['WA%d' % ws], writes=[B[bA]])
                for c in range(8):
                    MM(banks[bB][:, :], wb3[:, c, :], ogT3[:, c, ts_], start=(c == 0), stop=(c == 7),
                       reads=['WB%d' % ws], writes=[B[bB]])
                for c in range(8):
                    MM(banks[bGA][:, :], wga3[:, c, :], hTo3[:, c, ts_], start=(c == 0), stop=(c == 7),
                       reads=['WGA%d' % ws], writes=[B[bGA]])
                for c in range(8):
                    MM(banks[bGB][:, :], wgb3[:, c, :], hTo3[:, c, ts_], start=(c == 0), stop=(c == 7),
                       reads=['WGB%d' % ws], writes=[B[bGB]])
                ACT(sgA[k], banks[bGA][:, :], AF.Sigmoid, reads=[B[bGA]], writes=['sgA%d' % k],
                    bias=cf[:, CF_GB + jc:CF_GB + jc + 1])
                ACT(sgB[k], banks[bGB][:, :], AF.Sigmoid, reads=[B[bGB]], writes=['sgB%d' % k],
                    bias=cf[:, CF_GB + 8 + jc:CF_GB + 8 + jc + 1])
                TT('dve', tA[k], banks[bA][:, :], sgA[k], ALU.mult, reads=[B[bA], 'sgA%d' % k], writes=['tA%d' % k])
                TT('dve', tB[k], banks[bB][:, :], sgB[k], ALU.mult, reads=[B[bB], 'sgB%d' % k], writes=['tB%d' % k])
                TT('pool', mixT3[:, jc, ts_], tA[k], tB[k], ALU.add, reads=['tA%d' % k, 'tB%d' % k], writes=['mixT'])
        P.barrier()
        dump("mixT", hTh[:, :], [128, 8 * NTOK])
        P.barrier()

        RA.reset()
        RG.reset()
        RT.reset()
        x1 = RA.f32(NT_OWN * 1024)
        Wo = RT.bf(8 * 1024)
        Wo3 = v3(Wo, 8)
        xt = [RG.f32(1024) for _ in range(2)]
        xs = [RG.bf(1024) for _ in range(2)]
        junk = RG.bf(1024)
        DMA('pool', Wo3, wcols(w_o, 0, 1024), 'Wo', writes=['Wo'])
        def d2_mm(j):
            s = j % 2
            DMA('sp', xt[s], xin[NPRE + j * 128:NPRE + (j + 1) * 128, :], 'xt%d' % s, writes=['xt%d' % s])
            for half in range(2):
                bk = 1 + half + 2 * s
                for c in range(8):
                    MM(banks[bk][:, :], mixT3[:, c, j * 128:(j + 1) * 128], Wo3[:, c, half * 512:(half + 1) * 512],
                       start=(c == 0), stop=(c == 7), reads=['Wo'], writes=[B[bk]])
                TT('dve', x1[:, j * 1024 + half * 512:j * 1024 + (half + 1) * 512], banks[bk][:, :],
                   xt[s][:, half * 512:(half + 1) * 512], ALU.add, reads=[B[bk], 'xt%d' % s], writes=['x1t%d' % s])

        d2_mm(0)
        for j in range(NT_OWN):
            if j + 1 < NT_OWN:
                d2_mm(j + 1)
            norm_transpose(x1[:, j * 1024:(j + 1) * 1024], j % 2, hTo3[:, :, j * 128:(j + 1) * 128], g2b,
                           xkey='x1t%d' % (j % 2))
        P.barrier()
        dump("x1", x1, [128, NT_OWN * 1024])
        P.barrier()

        RG.reset()
        RH.reset()
        RT.reset()
        h2T3 = hTo3
        aT = [RG.bf(4 * NTOK) for _ in range(2)]
        Wup = [RH.bf(8 * 512) for _ in range(2)]
        Wdn = [RH.bf(4 * 1024) for _ in range(2)]
        rst = [RT.f32(512) for _ in range(2)]
        ui = 0
        di = 0
        def ffn_load(f):
            ws = f % 2
            DMA('pool', v3(Wup[ws], 8), wcols(w_up, f * 512, 512), 'Wup%d' % ws, writes=['Wup%d' % ws])
            DMA('pool', v3(Wdn[ws], 4), w_dn[f * 512:(f + 1) * 512, :].rearrange("(c p) n -> p c n", p=128),
                'Wdn%d' % ws, writes=['Wdn%d' % ws])

        ffn_load(0)
        for f in range(8):
            ws = f % 2
            wu3, wd3, a3 = v3(Wup[ws], 8), v3(Wdn[ws], 4), v3(aT[ws], 4)
            if f + 1 < 8:
                ffn_load(f + 1)
            for q in range(4):
                for u in range(4):
                    bk = ui % 4
                    k = ui % 2
                    ui += 1
                    for c in range(8):
                        MM(banks[bk][:, :], wu3[:, c, q * 128:(q + 1) * 128], h2T3[:, c, u * 512:(u + 1) * 512],
                           start=(c == 0), stop=(c == 7), reads=['Wup%d' % ws], writes=[B[bk]])
                    ACT(rst[k], banks[bk][:, :], AF.Relu, reads=[B[bk]], writes=['rst%d' % k])
                    TT('pool', a3[:, q, u * 512:(u + 1) * 512], rst[k], rst[k], ALU.mult, reads=['rst%d' % k],
                       writes=['aT%d' % ws])
            for j in range(NT_OWN):
                for half in range(2):
                    bk = 4 + di % 4
                    di += 1
                    for q in range(4):
                        MM(banks[bk][:, :], a3[:, q, j * 128:(j + 1) * 128], wd3[:, q, half * 512:(half + 1) * 512],
                           start=(q == 0), stop=(q == 3), reads=['aT%d' % ws, 'Wdn%d' % ws], writes=[B[bk]])
                    xsl = x1[:, j * 1024 + half * 512:j * 1024 + (half + 1) * 512]
                    TT('dve', xsl, xsl, banks[bk][:, :], ALU.add, reads=[B[bk], 'x1_%d_%d' % (j, half)],
                       writes=['x1_%d_%d' % (j, half)])
                if f == 7:
                    DMA('sp', out_d[j * 128:(j + 1) * 128, :], x1[:, j * 1024:(j + 1) * 1024], 'out%d' % (j % 4),
                        reads=['x1_%d_0' % j, 'x1_%d_1' % j])
        P.emit(st)
        info = {"n_sems": P.n_sems, "max_semval": P.max_semval,
                "n_ops": {e: len(P.ops[e]) for e in P.ENG}}
    return nc, dbg, info


def _consts(inputs, flag):
    i = np.arange(128)
    same = np.ones((128, 128), bool)
    uinc = (same & (i[:, None] <= i[None, :])).astype(np.float32)
    lst = (same & (i[:, None] > i[None, :])).astype(np.float32)
    cf = np.zeros((128, NCF), np.float32)
    cf[:, CF_UINC:CF_UINC + 128] = uinc
    cf[:, CF_LST:CF_LST + 128] = lst
    cf[:, CF_CIND:CF_CIND + 2] = 1.0
    cf[:, CF_G1:CF_G1 + 8] = inputs['norm1_g'].reshape(8, 128).T
    cf[:, CF_G2:CF_G2 + 8] = inputs['norm2_g'].reshape(8, 128).T
    cf[:, CF_GB:CF_GB + 16] = inputs['branch_gate_bias'].reshape(16, 128).T
    cf[:, CF_GQ] = np.tile(inputs['attn_q_norm_g'].reshape(64), 2)
    cf[:, CF_GK] = np.tile(inputs['attn_k_norm_g'].reshape(64), 2)
    cf[:, CF_GOUT:CF_GOUT + 256] = np.broadcast_to(inputs['gla_out_norm_g'].reshape(1, 256), (128, 256))
    cf[:, CF_EPS] = EPS
    cf[:, CF_ONE] = 1.0
    cb = np.zeros((128, NCB), np.float32)
    cb[:, CB_ID:CB_ID + 128] = np.eye(128, dtype=np.float32)
    prev = (i[:, None] >= i[None, :]).astype(np.float32)
    cur = (i[:, None] <= i[None, :]).astype(np.float32)
    cb[:, CB_M4:CB_M4 + 512] = np.concatenate([prev, cur, prev, cur], 1)
    cb[:, CB_M4H0:CB_M4H0 + 512] = np.concatenate([prev * flag, cur, prev, cur], 1)
    cb[:, CB_M4HH:CB_M4HH + 512] = np.concatenate([prev * flag, cur, prev * flag, cur], 1)
    cb[:, CB_GM:CB_GM + 128] = uinc
    cb[:64, CB_BONES:CB_BONES + 64] = 1.0 / 64
    cb[64:, CB_BONES + 64:CB_BONES + 128] = 1.0 / 64
    cb[:, CB_ONES:CB_ONES + 64] = 1.0
    return cf, cb


_CACHE = {}


def kernel(**inputs):
    inputs = {k: np.asarray(v, dtype=np.float32) for k, v in inputs.items()}
    x = inputs['x']
    if 'nc' not in _CACHE:
        _CACHE['nc'] = build_program()
    nc, dbg, info = _CACHE['nc']
    w_in = np.ascontiguousarray(inputs['w_in'].reshape(D, DIN))
    shared = {
        "w_in": w_in,
        "w_a": np.ascontiguousarray(inputs['w_attn_branch'].reshape(512, D)),
        "w_b": np.ascontiguousarray(inputs['w_gla_branch'].reshape(D, D)),
        "w_o": np.ascontiguousarray(inputs['w_out'].reshape(D, D)),
        "w_up": np.ascontiguousarray(inputs['w_ff_up'].reshape(D, 4 * D)),
        "w_dn": np.ascontiguousarray(inputs['w_ff_down'].reshape(4 * D, D)),
        "gup": np.ascontiguousarray(np.concatenate([inputs['gla_gate_up'].reshape(16, 512),
                                                    inputs['gla_gate_bias'].reshape(1, 512)], 0)),
    }
    in_maps = []
    for c in range(8):
        b, ch = c // 4, c % 4
        xi = np.zeros((NPRE + NTOK, D), np.float32)
        npre = ch * NTOK
        if npre:
            xi[NPRE - npre:NPRE] = x[b, 0:npre]
        xi[NPRE:] = x[b, ch * NTOK:(ch + 1) * NTOK]
        cf, cb = _consts(inputs, 0.0 if ch == 0 else 1.0)
        m = dict(shared)
        m.update({"xin": xi, "cf": cf, "cb": cb})
        in_maps.append(m)
    res = run_bass_kernel_spmd(nc, in_maps, core_ids=list(range(8)))
    out = np.zeros((2, SEQ, D), np.float32)
    for c in range(8):
        b, ch = c // 4, c % 4
        out[b, ch * NTOK:(ch + 1) * NTOK] = res.results[c]["out"]
    if DEBUG:
        _CACHE['last'] = (res, dbg)
    return out
```
